# Optimizing a Trainium2 kernel written in Bass

```python
import math
import jax, jax.numpy as jnp
from jax import lax
import numpy as np

D_MODEL = 1024
BATCH = 8
SEQ = 2048
DEPTH = 1

SSM_HEADS = 8
SSM_HEAD_DIM = 64
D_SSM = SSM_HEADS * SSM_HEAD_DIM
SSM_GROUPS = 2
D_STATE = 128
CONV_WIDTH = 4
CHUNK = 128
D_CONV = D_SSM + 2 * SSM_GROUPS * D_STATE
ATTN_HEADS = 8
KV_HEADS = 2
HEAD_DIM = 64
D_ATTN = ATTN_HEADS * HEAD_DIM
D_KV = KV_HEADS * HEAD_DIM
WINDOW = 128
D_MIX = D_SSM + D_ATTN
D_IN_PROJ = D_SSM + D_CONV + SSM_HEADS + D_ATTN + 2 * D_KV
D_FF = -(-8 * D_MODEL // (3 * 256)) * 256
EPS = 1e-5

kernel_name = "hymba_ssd_swa_sink_hybrid"


def rms_norm(x, w):
    xf = x.astype(jnp.float32)
    y = xf * lax.rsqrt(jnp.mean(xf * xf, axis=-1, keepdims=True) + EPS)
    return (y * w.astype(jnp.float32)).astype(x.dtype)


def causal_depthwise_conv(u, w, b):
    k_width, ch = w.shape
    y = lax.conv_general_dilated(
        u, w[:, None, :].astype(u.dtype), window_strides=(1,),
        padding=[(k_width - 1, 0)], dimension_numbers=("NWC", "WIO", "NWC"),
        feature_group_count=ch)
    return y + b.astype(u.dtype)


def ssd_chunked(x, dt, a, b_mat, c_mat):
    bsz, seq, n_h, p = x.shape
    n_g, n_s = b_mat.shape[2], b_mat.shape[3]
    r = n_h // n_g
    nc = seq // CHUNK
    xdt = (x.astype(jnp.float32) * dt[..., None]).reshape(bsz, nc, CHUNK, n_g, r, p)
    bc = b_mat.astype(jnp.float32).reshape(bsz, nc, CHUNK, n_g, n_s)
    cc = c_mat.astype(jnp.float32).reshape(bsz, nc, CHUNK, n_g, n_s)
    d_a = (dt * a).reshape(bsz, nc, CHUNK, n_g, r).transpose(0, 3, 4, 1, 2)
    a_cum = jnp.cumsum(d_a, axis=-1)
    causal = jnp.tril(jnp.ones((CHUNK, CHUNK), dtype=bool))
    seg = a_cum[..., :, None] - a_cum[..., None, :]
    decay_in = jnp.exp(jnp.where(causal, seg, -jnp.inf))
    cb = jnp.einsum("bctgn,bcsgn->bgcts", cc, bc)
    y_diag = jnp.einsum("bgcts,bgrcts,bcsgrp->bctgrp", cb, decay_in, xdt)
    decay_to_end = jnp.exp(a_cum[..., -1:] - a_cum)
    chunk_states = jnp.einsum("bctgn,bgrct,bctgrp->bcgrpn", bc, decay_to_end, xdt)
    chunk_decay = jnp.exp(a_cum[..., -1])

    def step(h, inp):
        s_c, d_c = inp
        return h * d_c[..., None, None] + s_c, h

    h0 = jnp.zeros((bsz, n_g, r, p, n_s), jnp.float32)
    _, prev_states = lax.scan(step, h0, (jnp.moveaxis(chunk_states, 1, 0),
                                         jnp.moveaxis(chunk_decay, 3, 0)))
    y_off = jnp.einsum("bctgn,cbgrpn,bgrct->bctgrp", cc, prev_states, jnp.exp(a_cum))
    return (y_diag + y_off).reshape(bsz, seq, n_h, p)


def sliding_window_attention_with_sinks(q, k, v, sinks):
    bsz, seq, n_q, d = q.shape
    n_kv = k.shape[2]
    r = n_q // n_kv
    nb = seq // WINDOW
    qb = q.reshape(bsz, nb, WINDOW, n_kv, r, d)

    def with_prev(t):
        tb = t.reshape(bsz, nb, WINDOW, n_kv, d)
        prev = jnp.pad(tb[:, :-1], ((0, 0), (1, 0), (0, 0), (0, 0), (0, 0)))
        return jnp.concatenate([prev, tb], axis=2)

    kb, vb = with_prev(k), with_prev(v)
    scores = jnp.einsum("bnqhrd,bnkhd->bnhrqk", qb, kb).astype(jnp.float32) * (d ** -0.5)
    q_pos = jnp.arange(WINDOW)[:, None] + WINDOW
    k_pos = jnp.arange(2 * WINDOW)[None, :]
    band = (k_pos <= q_pos) & (k_pos > q_pos - WINDOW)
    k_global = (jnp.arange(nb) * WINDOW - WINDOW)[:, None] + k_pos
    mask = band[None] & (k_global >= 0)[:, None, :]
    scores = jnp.where(mask[None, :, None, None], scores, -jnp.inf)
    sink = sinks.astype(jnp.float32).reshape(n_kv, r)[None, None, :, :, None, None]
    m = jnp.maximum(jnp.max(scores, axis=-1, keepdims=True), sink)
    p = jnp.exp(scores - m)
    probs = p / (jnp.sum(p, axis=-1, keepdims=True) + jnp.exp(sink - m))
    out = jnp.einsum("bnhrqk,bnkhd->bnqhrd", probs.astype(v.dtype), vb)
    return out.reshape(bsz, seq, n_q * d)


def hybrid_layer(x, norm_mix_w, w_in, conv_w, conv_b, dt_bias, a_log, d_skip,
                 ssm_norm_w, attn_sinks, w_out, norm_ffn_w, w_gate, w_up, w_down):
    bsz, seq, _ = x.shape
    h = rms_norm(x, norm_mix_w)
    proj = h @ w_in
    s0 = D_SSM
    s1 = s0 + D_CONV
    s2 = s1 + SSM_HEADS
    s3 = s2 + D_ATTN
    s4 = s3 + D_KV
    z, xbc, dt_raw, q, k, v = jnp.split(proj, [s0, s1, s2, s3, s4], axis=-1)

    xbc = jax.nn.silu(causal_depthwise_conv(xbc, conv_w, conv_b))
    xs, bm, cm = jnp.split(xbc, [D_SSM, D_SSM + SSM_GROUPS * D_STATE], axis=-1)
    xs = xs.reshape(bsz, seq, SSM_HEADS, SSM_HEAD_DIM)
    bm = bm.reshape(bsz, seq, SSM_GROUPS, D_STATE)
    cm = cm.reshape(bsz, seq, SSM_GROUPS, D_STATE)
    dt = jax.nn.softplus(dt_raw.astype(jnp.float32) + dt_bias.astype(jnp.float32))
    a = -jnp.exp(a_log.astype(jnp.float32))
    y = ssd_chunked(xs, dt, a, bm, cm) + d_skip.astype(jnp.float32)[:, None] * xs.astype(jnp.float32)
    y = y.reshape(bsz, seq, D_SSM) * jax.nn.silu(z.astype(jnp.float32))
    y_ssm = rms_norm(y.reshape(bsz, seq, SSM_GROUPS, D_SSM // SSM_GROUPS),
                     ssm_norm_w.reshape(SSM_GROUPS, D_SSM // SSM_GROUPS)).reshape(bsz, seq, D_SSM)

    y_attn = sliding_window_attention_with_sinks(
        q.reshape(bsz, seq, ATTN_HEADS, HEAD_DIM),
        k.reshape(bsz, seq, KV_HEADS, HEAD_DIM),
        v.reshape(bsz, seq, KV_HEADS, HEAD_DIM), attn_sinks)

    mixed = jnp.concatenate([y_ssm.astype(x.dtype), y_attn.astype(x.dtype)], axis=-1) @ w_out
    x = x + mixed

    h = rms_norm(x, norm_ffn_w)
    x = x + (jax.nn.silu(h @ w_gate) * (h @ w_up)) @ w_down
    return x


def setup_inputs(seed: int = 0) -> dict:
    key = jax.random.key(seed)
    ks = jax.random.split(key, 20)
    f32 = jnp.float32

    def normal(k, shape, scale):
        return jax.random.normal(k, shape, f32) * scale

    x = jax.random.normal(ks[0], (BATCH, SEQ, D_MODEL), f32)
    norm_mix_w = 1.0 + normal(ks[1], (DEPTH, D_MODEL), 0.02)
    w_in = normal(ks[2], (DEPTH, D_MODEL, D_IN_PROJ), D_MODEL ** -0.5)
    conv_w = normal(ks[3], (DEPTH, CONV_WIDTH, D_CONV), CONV_WIDTH ** -0.5)
    conv_b = normal(ks[4], (DEPTH, D_CONV), 0.02)
    u = jax.random.uniform(ks[5], (DEPTH, SSM_HEADS), f32)
    dt0 = jnp.exp(u * (math.log(0.1) - math.log(0.001)) + math.log(0.001))
    dt_bias = dt0 + jnp.log(-jnp.expm1(-dt0))
    a_log = jnp.log(jax.random.uniform(ks[6], (DEPTH, SSM_HEADS), f32, 1.0, 16.0))
    d_skip = 1.0 + normal(ks[7], (DEPTH, SSM_HEADS), 0.02)
    ssm_norm_w = 1.0 + normal(ks[8], (DEPTH, D_SSM), 0.02)
    attn_sinks = normal(ks[9], (DEPTH, ATTN_HEADS), 0.5)
    w_out = normal(ks[10], (DEPTH, D_MIX, D_MODEL), D_MIX ** -0.5)
    norm_ffn_w = 1.0 + normal(ks[11], (DEPTH, D_MODEL), 0.02)
    w_gate = normal(ks[12], (DEPTH, D_MODEL, D_FF), D_MODEL ** -0.5)
    w_up = normal(ks[13], (DEPTH, D_MODEL, D_FF), D_MODEL ** -0.5)
    w_down = normal(ks[14], (DEPTH, D_FF, D_MODEL), D_FF ** -0.5)
    norm_final_w = 1.0 + normal(ks[15], (D_MODEL,), 0.02)
    return {"x": x, "norm_mix_w": norm_mix_w, "w_in": w_in, "conv_w": conv_w,
            "conv_b": conv_b, "dt_bias": dt_bias, "a_log": a_log, "d_skip": d_skip,
            "ssm_norm_w": ssm_norm_w, "attn_sinks": attn_sinks, "w_out": w_out,
            "norm_ffn_w": norm_ffn_w, "w_gate": w_gate, "w_up": w_up,
            "w_down": w_down, "norm_final_w": norm_final_w}


def reference(x, norm_mix_w, w_in, conv_w, conv_b, dt_bias, a_log, d_skip,
              ssm_norm_w, attn_sinks, w_out, norm_ffn_w, w_gate, w_up, w_down,
              norm_final_w):
    for i in range(DEPTH):
        x = hybrid_layer(x, norm_mix_w[i], w_in[i], conv_w[i], conv_b[i], dt_bias[i],
                         a_log[i], d_skip[i], ssm_norm_w[i], attn_sinks[i], w_out[i],
                         norm_ffn_w[i], w_gate[i], w_up[i], w_down[i])
    return rms_norm(x, norm_final_w)
```

```python
from contextlib import ExitStack

import numpy as np
import concourse.bass as bass
import concourse.mybir as mybir
from concourse.bass_utils import run_bass_kernel_spmd

F32 = mybir.dt.float32
BF16 = mybir.dt.bfloat16
AF = mybir.ActivationFunctionType
ALU = mybir.AluOpType
AX = mybir.AxisListType

SAME_ENGINE_SYNC = True

L = 2048
D = 1024
NT = 16
KC = 8
TB = 256
NBLK = L // TB
D_IN = 2312
D_FF = 2816
NJ = D_FF // 128
PASSES = [(0, 6), (6, 6), (12, 5), (17, 5)]
JMAX = 6
EPS = 1e-5
NEG = -30000.0


class Trk:
    __slots__ = ("name", "w", "r", "dsem", "dcnt", "excl")

    def __init__(self, name="", fence=None, excl=False):
        self.name = name
        self.excl = excl
        self.w = None
        self.r = list(fence) if fence else []
        self.dsem = None
        self.dcnt = 0


class Ctx:
    def __init__(self, nc, stack):
        self.nc = nc
        self.stack = stack
        self.eng = {"pe": nc.tensor, "act": nc.scalar, "dve": nc.vector,
                    "pool": nc.gpsimd, "sp": nc.sync}
        self.sem = {}
        self.cnt = {}
        self.seen = {}
        for k in self.eng:
            self.sem[k] = stack.enter_context(nc.semaphore("s_" + k))
            self.cnt[k] = 0
            self.seen[k] = {}
        self.n_dsem = 0
        self.n_wait = 0
        self.n_ins = 0
        self.tfree = {k: 0.0 for k in self.eng}
        self.ttok = {}
        self.step_fin = 0.0
        self.log = None
        self.tag = ""
        self.tokinfo = {}
        self.crit = None

    def _tdeps(self, e, deps):
        t = 0.0
        self.crit = None
        for d in deps:
            if d is None:
                continue
            lat = 0.05 if d[0] is self.sem.get(e) else 0.25
            td = self.ttok.get((id(d[0]), d[1]), 0.0) + lat
            if td > t:
                t = td
                self.crit = self.tokinfo.get((id(d[0]), d[1]))
        return t

    def sb(self, name, shape, dtype, stack=None):
        return (stack or self.stack).enter_context(
            self.nc.sbuf_tensor("sb_" + name, list(shape), dtype))

    def _wait(self, e, deps):
        h = self.eng[e]
        best = {}
        for d in deps:
            if d is None:
                continue
            s, v = d
            if s is self.sem.get(e):
                if e == "pe" or not SAME_ENGINE_SYNC or v > self.cnt[e]:
                    continue
            k = id(s)
            if k not in best or best[k][1] < v:
                best[k] = (s, v)
        for k, (s, v) in best.items():
            if self.seen[e].get(k, 0) >= v:
                continue
            h.wait_ge(s, v)
            self.n_wait += 1
            self.seen[e][k] = v

    def op(self, e, fn, reads=(), writes=(), signal=True, cost=0.1):
        deps = []
        for t in reads:
            deps.append(t.w)
            if t.excl:
                deps.extend(r for r in t.r if r[0] is not self.sem[e])
        for t in writes:
            deps.append(t.w)
            deps.extend(t.r)
        self._wait(e, deps)
        ins = fn()
        self.n_ins += 1
        self.tfree_prev = self.tfree[e]
        start = max(self.tfree[e], self._tdeps(e, deps))
        fin = start + cost
        self.tfree[e] = fin
        if signal:
            self.cnt[e] += 1
            ins.then_inc(self.sem[e], 1)
            tok = (self.sem[e], self.cnt[e])
        else:
            tok = (self.sem[e], self.cnt[e] + 1)
        self.ttok[(id(tok[0]), tok[1])] = fin + (0.1 if e == "pe" else 0.0)
        self.step_fin = max(self.step_fin, fin)
        if self.log is not None:
            try:
                nm = str(ins.ins.name)
                self.tokinfo[(id(tok[0]), tok[1])] = (nm, self.tag, e)
                self.log.append((nm, e, self.tag, start, fin, self.crit, self.tfree_prev))
            except Exception:
                pass
        for t in reads:
            t.r.append(tok)
        for t in writes:
            t.w = tok
            t.r = []
        return ins

    def dma(self, q, out, in_, reads=(), writes=(), owner=None, join=False, nbytes=65536):
        own = owner or (writes[0] if writes else reads[0])
        if own.dsem is None:
            own.dsem = self.stack.enter_context(self.nc.semaphore("d%d" % self.n_dsem))
            self.n_dsem += 1
        deps = []
        for t in reads:
            deps.append(t.w)
        for t in writes:
            if not (join and t.w is not None and t.w[0] is own.dsem):
                deps.append(t.w)
            deps.extend(t.r)
        self._wait(q, deps)
        own.dcnt += 16
        self.eng[q].dma_start(out=out, in_=in_).then_inc(own.dsem, 16)
        self.n_ins += 1
        tok = (own.dsem, own.dcnt)
        start = max(self.tfree[q], self._tdeps(q, deps))
        self.tfree[q] = start + (1.0 if q == "pool" else 0.1)
        self.ttok[(id(tok[0]), tok[1])] = start + 2.5 + nbytes / 2.0e5
        for t in reads:
            t.r.append(tok)
        for t in writes:
            t.w = tok
            t.r = []

    def wait_all(self, e, trks):
        deps = []
        for t in trks:
            deps.append(t.w)
            deps.extend(t.r)
        self._wait(e, deps)

    def fence_tokens(self):
        toks = []
        for e in ("pe", "act", "dve", "pool"):
            if self.cnt[e] > 0:
                toks.append((self.sem[e], self.cnt[e]))
        return toks


def build_program(nblk=NBLK, do_ffn=True, debug=()):
    nc = bass.Bass("TRN2", target_bir_lowering=False)

    def din(name, shape):
        return nc.dram_tensor(name, list(shape), F32, kind="ExternalInput").ap()

    x_d = din("x", [L, D])
    w_in_d = din("w_in", [128, KC, D_IN])
    w_k2_d = din("w_k2", [128, KC, 256])
    w_out_d = din("w_out", [128, KC, D])
    w_gu_d = din("w_gu", [NJ, 128, 2, KC, 128])
    w_dn_d = din("w_dn", [128, NJ, D])
    nw_mix_d = din("nw_mix", [128, D])
    nw_ffn_d = din("nw_ffn", [128, D])
    nw_fin_d = din("nw_fin", [128, D])
    convw_d = din("convw", [128, 8, 4])
    sm_d = din("smallp", [128, 48])
    ssmw_d = din("ssmw", [128, 512])
    identf_d = din("identf", [128, 128])
    tri_d = din("tri", [128, 128])
    negmask_d = din("negmask", [128, 512])
    maskb_d = din("maskb", [128, 512])
    out_d = nc.dram_tensor("out", [L, D], F32, kind="ExternalOutput").ap()
    dbg_outs = {}

    with ExitStack() as st:
        c = Ctx(nc, st)
        op = c.op
        if debug:
            c.log = []

        ps = st.enter_context(nc.psum_tensor("ps", [128, 4096], F32))
        psb = ps.bitcast(BF16) if hasattr(ps, "bitcast") else None
        tb = [Trk("bank%d" % k, excl=True) for k in range(8)]
        bank_ctr = [0]

        def bank(k):
            return ps[:, k * 512:(k + 1) * 512]

        def bankb(k):
            return psb[:, k * 1024:(k + 1) * 1024]

        def nb():
            k = bank_ctr[0] % 8
            bank_ctr[0] += 1
            return k

        def nb2():
            if bank_ctr[0] % 2:
                bank_ctr[0] += 1
            k = bank_ctr[0] % 8
            bank_ctr[0] += 2
            return k

        def nfree(ap):
            n = 1
            for d in ap.shape[1:]:
                n *= int(d)
            return n

        def mm(out, lhsT, rhs, start, stop, reads, writes, signal):
            n = max(nfree(rhs), 64)
            cst = n / 2400.0 * (4.0 if rhs.dtype == F32 else 1.0) + 0.01
            op("pe", lambda: nc.tensor.matmul(out, lhsT=lhsT, rhs=rhs, start=start, stop=stop),
               reads=reads, writes=writes, signal=signal, cost=cst)

        def tr(out, in_, ident, reads, writes, signal):
            op("pe", lambda: nc.tensor.transpose(out, in_, ident),
               reads=reads, writes=writes, signal=signal, cost=0.07)

        def act(out, in_, func, reads, writes, bias=None, scale=None, accum=None):
            kw = {}
            if bias is not None:
                kw["bias"] = bias
            if scale is not None:
                kw["scale"] = scale
            if accum is not None:
                kw["accum_out"] = accum
            op("act", lambda: nc.scalar.activation(out=out, in_=in_, func=func, **kw),
               reads=reads, writes=writes, cost=0.12 + nfree(out) / 1200.0 + (0.1 if accum is not None else 0.0))

        def ecost(e, out, mult=1.0):
            n = nfree(out)
            if e == "dve":
                return 0.07 + mult * n / 960.0
            return 1.0 + n / 400.0

        def tt(e, out, in0, in1, aop, reads, writes):
            h = nc.vector if e == "dve" else nc.gpsimd
            op(e, lambda: h.tensor_tensor(out=out, in0=in0, in1=in1, op=aop),
               reads=reads, writes=writes, cost=ecost(e, out))

        def ts(e, out, in0, s1, s2, op0, op1, reads, writes):
            h = nc.vector if e == "dve" else nc.gpsimd
            if op1 is None:
                op(e, lambda: h.tensor_scalar(out=out, in0=in0, scalar1=s1, scalar2=None, op0=op0),
                   reads=reads, writes=writes, cost=ecost(e, out))
            else:
                op(e, lambda: h.tensor_scalar(out=out, in0=in0, scalar1=s1, scalar2=s2, op0=op0, op1=op1),
                   reads=reads, writes=writes, cost=ecost(e, out))

        def stt(out, in0, scalar, in1, op0, op1, reads, writes):
            op("dve", lambda: nc.vector.scalar_tensor_tensor(out=out, in0=in0, scalar=scalar, in1=in1,
                                                             op0=op0, op1=op1),
               reads=reads, writes=writes, cost=ecost("dve", out))

        def cp(e, out, in_, reads, writes):
            if e == "act":
                op("act", lambda: nc.scalar.copy(out=out, in_=in_), reads=reads, writes=writes,
                   cost=0.12 + nfree(out) / 1200.0)
            else:
                h = nc.vector if e == "dve" else nc.gpsimd
                op(e, lambda: h.tensor_copy(out=out, in_=in_), reads=reads, writes=writes, cost=ecost(e, out))

        def dbg(name, ap, trk, shape):
            if name not in debug:
                return
            d = nc.dram_tensor("dbg_" + name, list(shape), F32, kind="ExternalOutput").ap()
            stg = c.sb("dbgs_" + name, list(shape), F32)
            t = Trk()
            cp("dve", stg[:], ap, [trk], [t])
            c.dma("sp", d, stg[:], reads=[t])
            dbg_outs[name] = t

        X = c.sb("X", [128, NT, D], F32)
        tX = [Trk("X%d" % i) for i in range(NT)]
        identb = c.sb("identb", [128, 128], BF16); t_identb = Trk()
        tri = c.sb("tri", [128, 128], F32); t_tri = Trk()
        negmask = c.sb("negmask", [128, 512], BF16); t_negmask = Trk()
        maskb = c.sb("maskb", [128, 512], BF16); t_maskb = Trk()
        convw = c.sb("convw", [128, 8, 4], F32); t_convw = Trk()
        smallp = c.sb("smallp", [128, 48], F32); t_small = Trk()
        a_b = c.sb("a_b", [128, 8], F32); t_ab = Trk()
        ssmw = c.sb("ssmw", [128, 512], F32); t_ssmw = Trk()
        mhalf = c.sb("mhalf", [128, 2], F32); t_mhalf = Trk()
        wb = c.sb("wb", [128, D], F32); t_wb = Trk()
        NGU = 6
        gu = [c.sb("gu%d" % i, [128, 2, KC, 128], BF16) for i in range(1)]
        t_gu = [Trk("gu%d" % i) for i in range(NGU)]
        convb = smallp[:, 0:8]
        dtb = smallp[:, 8:16]
        alog = smallp[:, 16:24]
        dskip = smallp[:, 24:32]
        sinks = smallp[:, 32:40]

        def stat_tiles(name, n, w=64, stack=None):
            tiles = [c.sb("%s%d" % (name, i), [128, w], F32, stack) for i in range(n)]
            trks = [Trk("%s%d" % (name, i)) for i in range(n)]
            ctr = [0]

            def nxt():
                k = ctr[0] % n
                ctr[0] += 1
                return tiles[k], trks[k]
            return nxt

        nsm_N = stat_tiles("stN", 2, 8)
        nsm_F = stat_tiles("stF", 2, 8)

        def gu_load(j):
            slot = j % NGU
            c.dma("pool", gu[slot][:], w_gu_d[j], writes=[t_gu[slot]])

        done = set()

        def H(eng, *trks):
            return ("h", eng, list(trks))

        def vt_of(s_):
            h_ = s_["hint"]
            if h_ is None:
                return s_["vt"]
            t_ = c.tfree[h_[0]]
            for q_ in h_[1]:
                if q_.w is not None:
                    t_ = max(t_, c.ttok.get((id(q_.w[0]), q_.w[1]), 0.0))
            return t_

        def run_streams(gens):
            sts = [{"g": g_, "vt": 0.0, "blk": None, "nm": getattr(g_, "__name__", "?"), "k": 0, "hint": None} for g_ in gens]
            while sts:
                progressed = False
                for s_ in sorted(sts, key=vt_of):
                    if s_["blk"] is not None:
                        if not s_["blk"]():
                            continue
                        s_["blk"] = None
                    c.step_fin = 0.0
                    s_["k"] += 1
                    c.tag = "%s#%d" % (s_["nm"], s_["k"])
                    try:
                        r_ = next(s_["g"])
                    except StopIteration:
                        sts.remove(s_)
                        progressed = True
                        break
                    if isinstance(r_, tuple) and r_[0] == "wait":
                        if not r_[1]():
                            s_["blk"] = r_[1]
                    elif isinstance(r_, tuple) and r_[0] == "h":
                        s_["hint"] = (r_[1], r_[2])
                        s_["vt"] = max(s_["vt"], c.step_fin)
                    else:
                        s_["hint"] = None
                        s_["vt"] = max(s_["vt"], c.step_fin)
                    progressed = True
                    break
                if not progressed:
                    raise RuntimeError("emission deadlock")

        c.dma("sp", X[:, 0, :], x_d[0:128, :], writes=[tX[0]])
        c.dma("sp", X[:, 1, :], x_d[128:256, :], writes=[tX[1]])
        c.dma("sp", smallp[:], sm_d, writes=[t_small])
        c.dma("sp", convw[:], convw_d, writes=[t_convw])
        c.dma("sp", tri[:], tri_d, writes=[t_tri])
        c.dma("sp", ssmw[:], ssmw_d, writes=[t_ssmw])
        c.dma("pool", identb[:], identf_d, writes=[t_identb])
        c.dma("pool", negmask[:], negmask_d, writes=[t_negmask])
        op("pool", lambda: nc.gpsimd.memset(mhalf[:], -0.5), writes=[t_mhalf])

        with ExitStack() as ms:
            Win = c.sb("Win", [128, KC, D_IN], BF16, ms)
            t_Wxbc = Trk("Wxbc"); t_Wq = Trk("Wq"); t_Wz = Trk("Wz"); t_Wdt = t_Wq; t_Wv = t_Wq
            Wk2 = c.sb("Wk2", [128, KC, 256], BF16, ms); t_Wk2 = Trk()
            Wout = c.sb("Wout", [128, KC, D], BF16, ms); t_Wout = Trk("Wout")
            hT1 = c.sb("hT", [128, KC, TB], BF16, ms)
            t_hT1 = Trk("hT")
            hT = [hT1, hT1]
            t_hT = [t_hT1, t_hT1]
            xn = c.sb("xn", [128, D], BF16, ms); t_xn = Trk()
            pre2 = [c.sb("pre%d" % i, [128, TB + 3], F32, ms) for i in range(4)]
            t_pre2 = [Trk() for _ in range(4)]
            halo = c.sb("halo", [128, 8, 3], F32, ms)
            t_halo = [Trk() for _ in range(8)]
            sgt = [c.sb("sgt%d" % i, [128, TB], F32, ms) for i in range(2)]
            t_sgt = [Trk() for _ in range(2)]
            post = [c.sb("post%d" % i, [128, 8, TB], BF16, ms) for i in range(2)]
            t_post = [[Trk() for _ in range(8)] for _ in range(2)]
            qT = [c.sb("qT%d" % i, [128, 4, TB], BF16, ms) for i in range(2)]
            t_qT = [[Trk() for _ in range(4)] for _ in range(2)]
            kT = [c.sb("kT%d" % i, [128, 2, 128 + TB], BF16, ms) for i in range(2)]
            t_kT = [[Trk() for _ in range(2)] for _ in range(2)]
            vtok = [c.sb("vtok%d" % i, [128, 3, 128], BF16, ms) for i in range(2)]
            t_vtok = [Trk() for _ in range(2)]
            sz = [c.sb("sz%d" % i, [128, 2, 512], F32, ms) for i in range(2)]
            t_sz = [[Trk() for _ in range(2)] for _ in range(2)]
            dtraw = [c.sb("dtraw%d" % i, [128, 2, 8], F32, ms) for i in range(2)]
            t_dtraw = [[Trk() for _ in range(2)] for _ in range(2)]
            class NS:
                pass

            def Sset(q):
                T = NS()
                T.cbs = c.sb("cbs%d" % q, [128, 256], F32, ms); T.t_cbs = Trk()
                T.MT = c.sb("MT%d" % q, [128, 1024], BF16, ms); T.t_MT = Trk()
                T.xdt = c.sb("xdt%d" % q, [128, 512], BF16, ms); T.t_xdt = Trk()
                T.xdte = c.sb("xdte%d" % q, [128, 512], BF16, ms); T.t_xdte = Trk()
                T.xsD = c.sb("xsD%d" % q, [128, 512], BF16, ms); T.t_xsD = Trk()
                T.Btok = c.sb("Btok%d" % q, [128, 256], BF16, ms); T.t_Btok = Trk()
                T.ybuf = c.sb("ybuf%d" % q, [128, 512], F32, ms); T.t_ybuf = Trk()
                T.ysn = c.sb("ysn%d" % q, [128, 512], BF16, ms); T.t_ysn = Trk()
                T.nsm1 = stat_tiles("stS1_%d" % q, 1, 64, ms)
                T.nsm2 = stat_tiles("stS2_%d" % q, 1, 32, ms)
                return T

            def Aset(q):
                T = NS()
                T.Pexp = c.sb("Pexp%d" % q, [128, 1024], BF16, ms); T.t_Pexp = Trk()
                T.PT = c.sb("PT%d" % q, [128, 1024], BF16, ms); T.t_PT = Trk()
                T.yattn = c.sb("yattn%d" % q, [128, 512], BF16, ms); T.t_yattn = Trk()
                T.nsm = stat_tiles("stA_%d" % q, 1, 32, ms)
                return T

            ST = [Sset(0), Sset(1)]
            AT = [Aset(0), Aset(1)]
            S_f = c.sb("S_f", [128, 512], F32, ms); t_Sf = Trk()
            S_b = c.sb("S_b", [128, 512], BF16, ms); t_Sb = Trk()
            ycatT = c.sb("ycatT", [128, 8, TB], BF16, ms)
            t_ycatS = [Trk() for _ in range(2)]
            t_ycatA = [Trk() for _ in range(2)]

            c.dma("sp", wb[:], nw_mix_d, writes=[t_wb])
            for kc in range(KC):
                c.dma("pool", Win[:, kc, 512:1536], w_in_d[:, kc, 512:1536], writes=[t_Wxbc], join=True)
            for kc in range(KC):
                c.dma("pool", Win[:, kc, 1536:2312], w_in_d[:, kc, 1536:2312], writes=[t_Wq], join=True)
            c.dma("pool", Wk2[:], w_k2_d, writes=[t_Wk2])
            c.dma("pool", maskb[:], maskb_d, writes=[t_maskb])
            for kc in range(KC):
                c.dma("pool", Win[:, kc, 0:512], w_in_d[:, kc, 0:512], writes=[t_Wz], join=True)
            c.wait_all("sp", [t_Wxbc])
            for i in range(2, NT):
                c.dma("sp", X[:, i, :], x_d[i * 128:(i + 1) * 128, :], writes=[tX[i]])
            for kc in range(KC):
                c.dma("pool", Wout[:, kc, :], w_out_d[:, kc, :], writes=[t_Wout], join=True)
            if do_ffn:
                gu_load(0)

            act(a_b[:], alog, AF.Exp, [t_small], [t_ab])
            ts("dve", a_b[:], a_b[:], -1.0, None, ALU.mult, None, [t_ab], [t_ab])
            op("dve", lambda: nc.vector.memset(S_f[:], 0.0), writes=[t_Sf])
            op("dve", lambda: nc.vector.memset(S_b[:], 0.0), writes=[t_Sb])
            op("pool", lambda: nc.gpsimd.memset(halo[:], 0.0), writes=t_halo)
            for i in range(2):
                op("pool", lambda: nc.gpsimd.memset(kT[i][:], 0.0), writes=t_kT[i])
                op("pool", lambda: nc.gpsimd.memset(vtok[i][:], 0.0), writes=[t_vtok[i]])

            def rms_stats(src_ap, t_src, n, junk_ap, t_jk, nsm):
                s_, t_s = nsm()
                act(junk_ap, src_ap, AF.Square, [t_src], [t_jk, t_s], accum=s_[:, 0:1])
                ts("dve", s_[:, 1:2], s_[:, 0:1], 1.0 / n, EPS, ALU.mult, ALU.add, [t_s], [t_s])
                tt("pool", s_[:, 3:4], s_[:, 1:2], mhalf[:, 0:1], ALU.pow, [t_s, t_mhalf], [t_s])
                return s_, t_s

            def gen_N(b):
                s = b % 2
                for i in range(2):
                    gt = 2 * b + i
                    s_, t_s = rms_stats(X[:, gt, :], tX[gt], D, xn[:], t_xn, nsm_N)
                    yield H("dve", t_s)
                    stt(xn[:], X[:, gt, :], s_[:, 3:4], wb[:], ALU.mult, ALU.mult,
                        [tX[gt], t_s, t_wb], [t_xn])
                    k = nb()
                    for kc in range(KC):
                        tr(bankb(k)[:, kc * 128:(kc + 1) * 128], xn[:, kc * 128:(kc + 1) * 128], identb[:],
                           [t_xn, t_identb], [tb[k]], kc == KC - 1)
                    cp("act", hT[s][:, :, i * 128:(i + 1) * 128],
                       bankb(k)[:, 0:1024].rearrange("p (k t) -> p k t", k=KC), [tb[k]], [t_hT[s]])
                    yield H("act", tX[min(2 * b + i + 1, NT - 1)])

            def gen_P(b):
                s = b % 2
                o = 1 - s
                def stage_A(pr):
                    ocs = (2 * pr, 2 * pr + 1)
                    sl = 2 * (pr % 2)
                    ks = []
                    for q_, oc in enumerate(ocs):
                        k = nb()
                        ks.append(k)
                        c0 = 512 + oc * 128
                        for kc in range(KC):
                            mm(bank(k)[:, 0:TB], Win[:, kc, c0:c0 + 128], hT[s][:, kc, :], kc == 0, kc == KC - 1,
                               [t_Wxbc, t_hT[s]], [tb[k]], kc == KC - 1)
                    for q_, oc in enumerate(ocs):
                        cp("dve", pre2[sl + q_][:, 0:3], halo[:, oc, :], [t_halo[oc]], [t_pre2[sl + q_]])
                    for q_, oc in enumerate(ocs):
                        cp("act", pre2[sl + q_][:, 3:TB + 3], bank(ks[q_])[:, 0:TB], [tb[ks[q_]]], [t_pre2[sl + q_]])
                    for q_, oc in enumerate(ocs):
                        cp("dve", halo[:, oc, :], pre2[sl + q_][:, TB:TB + 3], [t_pre2[sl + q_]], [t_halo[oc]])

                def stage_B(pr):
                    ocs = (2 * pr, 2 * pr + 1)
                    sl = 2 * (pr % 2)
                    ka_ = nb()
                    accp = [bank(ka_)[:, 0:TB], bank(ka_)[:, TB:2 * TB]]
                    t_a = tb[ka_]
                    for q_, oc in enumerate(ocs):
                        ts("dve", accp[q_], pre2[sl + q_][:, 0:TB], convw[:, oc, 0:1], convb[:, oc:oc + 1], ALU.mult, ALU.add,
                           [t_pre2[sl + q_], t_convw, t_small], [t_a])
                    for kk in range(1, 4):
                        for q_, oc in enumerate(ocs):
                            stt(accp[q_], pre2[sl + q_][:, kk:kk + TB], convw[:, oc, kk:kk + 1], accp[q_], ALU.mult, ALU.add,
                                [t_pre2[sl + q_], t_convw, t_a], [t_a])
                    for q_, oc in enumerate(ocs):
                        act(sgt[q_][:], accp[q_], AF.Exp, [t_a], [t_sgt[q_]], scale=-1.0)
                    for q_, oc in enumerate(ocs):
                        act(sgt[q_][:], sgt[q_][:], AF.Ln, [t_sgt[q_]], [t_sgt[q_]], bias=1.0)
                    for q_, oc in enumerate(ocs):
                        act(sgt[q_][:], sgt[q_][:], AF.Exp, [t_sgt[q_]], [t_sgt[q_]], scale=-1.0)
                    for q_, oc in enumerate(ocs):
                        tt("dve", post[s][:, oc, :], accp[q_], sgt[q_][:], ALU.mult, [t_a, t_sgt[q_]], [t_post[s][oc]])

                stage_A(0)
                yield H("pe", t_hT[s])
                for pr in range(4):
                    if pr + 1 < 4:
                        stage_A(pr + 1)
                        yield H("dve", t_pre2[2 * (pr % 2)], t_pre2[2 * (pr % 2) + 1])
                    stage_B(pr)
                    yield H("pe", t_hT[s])
                for oc in range(4):
                    k = nb()
                    c0 = 1544 + oc * 128
                    for kc in range(KC):
                        mm(bank(k)[:, 0:TB], Win[:, kc, c0:c0 + 128], hT[s][:, kc, :], kc == 0, kc == KC - 1,
                           [t_Wq, t_hT[s]], [tb[k]], kc == KC - 1)
                    cp("act", qT[s][:, oc, :], bank(k)[:, 0:TB], [tb[k]], [t_qT[s][oc]])
                    if oc % 2:
                        yield H("pe", t_hT[s])
                for g in range(2):
                    k = nb()
                    for kc in range(KC):
                        mm(bank(k)[:, 0:TB], Wk2[:, kc, g * 128:(g + 1) * 128], hT[s][:, kc, :], kc == 0, kc == KC - 1,
                           [t_Wk2, t_hT[s]], [tb[k]], kc == KC - 1)
                    if b > 0:
                        cp("dve", kT[s][:, g, 0:128], kT[o][:, g, TB:TB + 128], [t_kT[o][g]], [t_kT[s][g]])
                    cp("act", kT[s][:, g, 128:128 + TB], bank(k)[:, 0:TB], [tb[k]], [t_kT[s][g]])
                if b > 0:
                    cp("dve", vtok[s][:, 0, :], vtok[o][:, 2, :], [t_vtok[o]], [t_vtok[s]])
                yield H("pe", t_hT[s])
                for i in range(2):
                    k = nb()
                    for kc in range(KC):
                        mm(bank(k)[:, 0:512], hT[s][:, kc, i * 128:(i + 1) * 128], Win[:, kc, 0:512], kc == 0, kc == KC - 1,
                           [t_Wz, t_hT[s]], [tb[k]], kc == KC - 1)
                    act(sz[s][:, i, :], bank(k)[:, 0:512], AF.Exp, [tb[k]], [t_sz[s][i]], scale=-1.0)
                    act(sz[s][:, i, :], sz[s][:, i, :], AF.Ln, [t_sz[s][i]], [t_sz[s][i]], bias=1.0)
                    act(sz[s][:, i, :], sz[s][:, i, :], AF.Exp, [t_sz[s][i]], [t_sz[s][i]], scale=-1.0)
                    tt("dve", sz[s][:, i, :], sz[s][:, i, :], bank(k)[:, 0:512], ALU.mult, [t_sz[s][i], tb[k]], [t_sz[s][i]])
                    yield H("pe", t_hT[s])
                    k = nb()
                    for kc in range(KC):
                        mm(bank(k)[:, 0:8], hT[s][:, kc, i * 128:(i + 1) * 128], Win[:, kc, 1536:1544], kc == 0, kc == KC - 1,
                           [t_Wdt, t_hT[s]], [tb[k]], False)
                    for kc in range(KC):
                        mm(bank(k)[:, 128:256], hT[s][:, kc, i * 128:(i + 1) * 128], Win[:, kc, 2184:2312], kc == 0, kc == KC - 1,
                           [t_Wv, t_hT[s]], [tb[k]], kc == KC - 1)
                    tt("dve", dtraw[s][:, i, :], bank(k)[:, 0:8], dtb, ALU.add, [tb[k], t_small], [t_dtraw[s][i]])
                    cp("dve", vtok[s][:, 1 + i, :], bank(k)[:, 128:256], [tb[k]], [t_vtok[s]])
                    yield H("pe", t_hT[s])

            def gen_S(b, i, T):
                par = b % 2
                post_ = post[par]
                tp_ = t_post[par]
                gc = 2 * b + i
                tsl = slice(i * 128, (i + 1) * 128)
                s1, t_s1 = T.nsm1()
                s2, t_s2 = T.nsm2()
                MT = T.MT; t_MT = T.t_MT
                ts("dve", s1[:, 0:8], dtraw[par][:, i, :], 60.0, None, ALU.min, None, [t_dtraw[par][i]], [t_s1])
                act(s1[:, 8:16], s1[:, 0:8], AF.Exp, [t_s1], [t_s1])
                act(s1[:, 16:24], s1[:, 8:16], AF.Ln, [t_s1], [t_s1], bias=1.0)
                tt("dve", s1[:, 24:32], s1[:, 16:24], a_b[:], ALU.mult, [t_s1, t_ab], [t_s1])
                dt_ = s1[:, 16:24]
                dA = s1[:, 24:32]
                kcb = nb()
                for g in range(2):
                    mm(bank(kcb)[:, g * 128:(g + 1) * 128], post_[:, 4 + g, tsl], post_[:, 6 + g, tsl], True, True,
                       [tp_[4 + g], tp_[6 + g]], [tb[kcb]], g == 1)
                cp("act", T.cbs[:], bank(kcb)[:, 0:256], [tb[kcb]], [T.t_cbs])
                yield H("pe", t_s1)
                ka = nb2()
                for hf in range(2):
                    for hh in range(4):
                        h = hf * 4 + hh
                        mm(bank(ka + hf)[:, hh * 128:(hh + 1) * 128], s1[:, 24 + h:25 + h].to_broadcast([128, 128]), tri[:],
                           hh == 0, False, [t_tri, t_s1], [tb[ka + hf]], False)
                    mm(bank(ka + hf), identb[:], negmask[:], False, True,
                       [t_identb, t_negmask], [tb[ka + hf]], True)
                Ab = ps[:, ka * 512:ka * 512 + 1024].rearrange("p (h t) -> p h t", t=128)
                Alast = Ab[:, :, 127:128].rearrange("p h o -> p (h o)")
                t_A = [tb[ka], tb[ka + 1]]
                kcu = nb()
                mm(bank(kcu)[:, 0:8], tri[:], dA, True, True, [t_tri, t_s1], [tb[kcu]], True)
                cp("dve", s1[:, 32:40], bank(kcu)[:, 0:8], [tb[kcu]], [t_s1])
                ts("dve", s1[:, 40:48], bank(kcu)[:, 0:8], -1.0, None, ALU.mult, None, [tb[kcu]], [t_s1])
                act(s1[:, 48:56], bank(kcu)[:, 0:8], AF.Exp, [tb[kcu]], [t_s1])
                tt("dve", s1[:, 56:64], Alast, s1[:, 32:40], ALU.subtract, t_A + [t_s1], [t_s1])
                act(s2[:, 0:8], s1[:, 56:64], AF.Exp, [t_s1], [t_s2])
                act(s2[:, 8:16], Alast, AF.Exp, t_A, [t_s2])
                tt("dve", s2[:, 16:24], dt_, s2[:, 0:8], ALU.mult, [t_s1, t_s2], [t_s2])
                for h in range(8):
                    tbh = tb[ka + h // 4]
                    act(Ab[:, h, :], Ab[:, h, :], AF.Exp, [tbh, t_s1], [tbh], bias=s1[:, 40 + h:41 + h])
                for g in range(2):
                    tt("dve", MT[:, g * 512:(g + 1) * 512].rearrange("p (r t) -> p r t", r=4),
                       Ab[:, 4 * g:4 * g + 4, :],
                       T.cbs[:, g * 128:(g + 1) * 128].unsqueeze(1).to_broadcast([128, 4, 128]),
                       ALU.mult, [tb[ka + g], T.t_cbs], [t_MT])
                yield H("pe", t_s1, t_s2)
                kx = nb()
                for cc in range(4):
                    tr(bankb(kx)[:, cc * 128:(cc + 1) * 128], post_[:, cc, tsl], identb[:],
                       [tp_[cc], t_identb], [tb[kx]], False)
                for g in range(2):
                    tr(bankb(kx)[:, 512 + g * 128:512 + (g + 1) * 128], post_[:, 4 + g, tsl], identb[:],
                       [tp_[4 + g], t_identb], [tb[kx]], g == 1)
                xs_ps = bankb(kx)[:, 0:512].rearrange("p (h d) -> p h d", h=8)
                tt("dve", T.xdt[:].rearrange("p (h d) -> p h d", h=8), xs_ps,
                   dt_.unsqueeze(2).to_broadcast([128, 8, 64]), ALU.mult, [tb[kx], t_s1], [T.t_xdt])
                tt("dve", T.xdte[:].rearrange("p (h d) -> p h d", h=8), xs_ps,
                   s2[:, 16:24].unsqueeze(2).to_broadcast([128, 8, 64]), ALU.mult, [tb[kx], t_s2], [T.t_xdte])
                tt("dve", T.xsD[:].rearrange("p (h d) -> p h d", h=8), xs_ps,
                   dskip.unsqueeze(2).to_broadcast([128, 8, 64]), ALU.mult, [tb[kx], t_small], [T.t_xsD])
                cp("act", T.Btok[:], bankb(kx)[:, 512:768], [tb[kx]], [T.t_Btok])
                if gc > 0:
                    yield ("wait", lambda: ("Sst", gc - 1) in done)
                yield H("pe", t_MT, T.t_xdt, T.t_xdte, T.t_xsD, T.t_Btok, t_Sb)
                kcs = nb()
                for g in range(2):
                    mm(bank(kcs)[:, g * 256:(g + 1) * 256], T.Btok[:, g * 128:(g + 1) * 128],
                       T.xdte[:, g * 256:(g + 1) * 256], True, True, [T.t_Btok, T.t_xdte], [tb[kcs]], g == 1)
                ko = nb()
                for g in range(2):
                    mm(bank(ko)[:, g * 256:(g + 1) * 256], post_[:, 6 + g, tsl], S_b[:, g * 256:(g + 1) * 256],
                       True, True, [tp_[6 + g], t_Sb], [tb[ko]], g == 1)
                kd = nb()
                mm(bank(kd)[:, 0:512], identb[:], T.xsD[:], True, False, [t_identb, T.t_xsD], [tb[kd]], False)
                for h in range(8):
                    mm(bank(kd)[:, h * 64:(h + 1) * 64], MT[:, h * 128:(h + 1) * 128], T.xdt[:, h * 64:(h + 1) * 64],
                       False, h == 7, [t_MT, T.t_xdt], [tb[kd]], h == 7)
                ybuf = T.ybuf; t_ybuf = T.t_ybuf
                tt("dve", ybuf[:].rearrange("p (h d) -> p h d", h=8),
                   bank(ko)[:, 0:512].rearrange("p (h d) -> p h d", h=8),
                   s1[:, 48:56].unsqueeze(2).to_broadcast([128, 8, 64]), ALU.mult, [tb[ko], t_s1], [t_ybuf])
                tt("dve", ybuf[:], ybuf[:], bank(kd)[:, 0:512], ALU.add, [t_ybuf, tb[kd]], [t_ybuf])
                tt("dve", S_f[:].rearrange("p (h d) -> p h d", h=8), S_f[:].rearrange("p (h d) -> p h d", h=8),
                   s2[:, 8:16].unsqueeze(2).to_broadcast([128, 8, 64]), ALU.mult, [t_Sf, t_s2], [t_Sf])
                tt("dve", S_f[:], S_f[:], bank(kcs)[:, 0:512], ALU.add, [t_Sf, tb[kcs]], [t_Sf])
                cp("act", S_b[:], S_f[:], [t_Sf], [t_Sb])
                done.add(("Sst", gc))
                yield H("pool", t_ybuf)
                ysn = T.ysn; t_ysn = T.t_ysn
                tt("pool", ybuf[:], ybuf[:], sz[par][:, i, :], ALU.mult, [t_ybuf, t_sz[par][i]], [t_ybuf])
                for g in range(2):
                    act(ysn[:, g * 256:(g + 1) * 256], ybuf[:, g * 256:(g + 1) * 256], AF.Square,
                        [t_ybuf], [t_ysn, t_s2], accum=s2[:, 24 + g:25 + g])
                ts("dve", s2[:, 26:28], s2[:, 24:26], 1.0 / 256, EPS, ALU.mult, ALU.add, [t_s2], [t_s2])
                tt("pool", s2[:, 30:32], s2[:, 26:28], mhalf[:, 0:2], ALU.pow, [t_s2, t_mhalf], [t_s2])
                if b > 0:
                    yield ("wait", lambda: ("O", b - 1) in done)
                yield H("dve", t_s2)
                for g in range(2):
                    stt(ysn[:, g * 256:(g + 1) * 256], ybuf[:, g * 256:(g + 1) * 256], s2[:, 30 + g:31 + g],
                        ssmw[:, g * 256:(g + 1) * 256], ALU.mult, ALU.mult, [t_ybuf, t_s2, t_ssmw], [t_ysn])
                ky = nb()
                for cc in range(4):
                    tr(bankb(ky)[:, cc * 128:(cc + 1) * 128], ysn[:, cc * 128:(cc + 1) * 128], identb[:],
                       [t_ysn, t_identb], [tb[ky]], cc == 3)
                cp("act", ycatT[:, 0:4, tsl], bankb(ky)[:, 0:512].rearrange("p (c t) -> p c t", c=4),
                   [tb[ky]], [t_ycatS[i]])
                yield H("dve", t_dtraw[par][i])

            def gen_A(b, j, T):
                par = b % 2
                qT_ = qT[par]; kT_ = kT[par]; v_ = vtok[par]
                Pexp = T.Pexp; t_Pexp = T.t_Pexp; PT = T.PT; t_PT = T.t_PT; yattn = T.yattn; t_yattn = T.t_yattn
                ga = 2 * b + j
                qsl = slice(j * 128, (j + 1) * 128)
                ksl = slice(j * 128, j * 128 + 256)
                msk = maskb[:, 256:512] if ga == 0 else maskb[:, 0:256]
                for g in range(2):
                    s3, t_s3 = T.nsm()
                    ka = nb2()
                    sc = ps[:, ka * 512:ka * 512 + 1024]
                    t_sc = [tb[ka], tb[ka + 1]]
                    for r in range(4):
                        hf = r % 2
                        psl = slice(hf * 64, (hf + 1) * 64)
                        kk = ka + r // 2
                        mm(sc[:, r * 256:(r + 1) * 256], qT_[psl, 2 * g + r // 2, qsl], kT_[psl, g, ksl], True, False,
                           [t_qT[par][2 * g + r // 2], t_kT[par][g]], [tb[kk]], False)
                        mm(sc[:, r * 256:(r + 1) * 256], identb[:], msk, False, True,
                           [t_identb, t_maskb], [tb[kk]], True)
                    op("dve", lambda: nc.vector.tensor_reduce(
                        out=s3[:, 0:4], in_=sc.rearrange("p (r k) -> p r k", r=4), axis=AX.X, op=ALU.max),
                        reads=t_sc, writes=[t_s3], cost=1.2)
                    stt(s3[:, 4:8], s3[:, 0:4], 0.125, sinks[:, g * 4:(g + 1) * 4], ALU.mult, ALU.max,
                        [t_s3, t_small], [t_s3])
                    ts("dve", s3[:, 8:12], s3[:, 4:8], -1.0, None, ALU.mult, None, [t_s3], [t_s3])
                    for r in range(4):
                        act(Pexp[:, r * 256:(r + 1) * 256], sc[:, r * 256:(r + 1) * 256], AF.Exp,
                            [tb[ka + r // 2], t_s3], [t_Pexp, t_s3], bias=s3[:, 8 + r:9 + r], scale=0.125,
                            accum=s3[:, 12 + r:13 + r])
                    yield H("dve", t_s3, t_Pexp)
                    tt("dve", s3[:, 16:20], sinks[:, g * 4:(g + 1) * 4], s3[:, 4:8], ALU.subtract,
                       [t_small, t_s3], [t_s3])
                    act(s3[:, 20:24], s3[:, 16:20], AF.Exp, [t_s3], [t_s3])
                    tt("dve", s3[:, 24:28], s3[:, 12:16], s3[:, 20:24], ALU.add, [t_s3], [t_s3])
                    op("dve", lambda: nc.vector.reciprocal(out=s3[:, 28:32], in_=s3[:, 24:28]),
                       reads=[t_s3], writes=[t_s3])
                    kt = nb()
                    for r in range(4):
                        for kk in range(2):
                            idx = r * 2 + kk
                            tr(bankb(kt)[:, idx * 128:(idx + 1) * 128],
                               Pexp[:, r * 256 + kk * 128:r * 256 + (kk + 1) * 128], identb[:],
                               [t_Pexp, t_identb], [tb[kt]], idx == 7)
                    cp("act", PT[:], bankb(kt)[:, 0:1024], [tb[kt]], [t_PT])
                    yield H("pe", t_PT, t_s3)
                    kv = nb()
                    for r in range(4):
                        o_ = bank(kv)[:, r * 64:(r + 1) * 64]
                        mm(o_, PT[:, (r * 2) * 128:(r * 2 + 1) * 128], v_[:, j, g * 64:(g + 1) * 64], True, False,
                           [t_PT, t_vtok[par]], [tb[kv]], False)
                        mm(o_, PT[:, (r * 2 + 1) * 128:(r * 2 + 2) * 128], v_[:, j + 1, g * 64:(g + 1) * 64], False, True,
                           [t_PT, t_vtok[par]], [tb[kv]], r == 3)
                    tt("dve", yattn[:, g * 256:(g + 1) * 256].rearrange("p (r d) -> p r d", r=4),
                       bank(kv)[:, 0:256].rearrange("p (r d) -> p r d", r=4),
                       s3[:, 28:32].unsqueeze(2).to_broadcast([128, 4, 64]), ALU.mult,
                       [tb[kv], t_s3], [t_yattn])
                    yield H("pe", t_yattn)
                if b > 0:
                    yield ("wait", lambda: ("O", b - 1) in done)
                ky = nb()
                for cc in range(4):
                    tr(bankb(ky)[:, cc * 128:(cc + 1) * 128], yattn[:, cc * 128:(cc + 1) * 128], identb[:],
                       [t_yattn, t_identb], [tb[ky]], cc == 3)
                cp("act", ycatT[:, 4:8, qsl], bankb(ky)[:, 0:512].rearrange("p (c t) -> p c t", c=4),
                   [tb[ky]], [t_ycatA[j]])
                yield H("pe", t_yattn)

            def emit_O(b):
                for i in range(2):
                    gt = 2 * b + i
                    for hf in range(2):
                        k = nb()
                        for kc in range(KC):
                            mm(bank(k)[:, 0:512], ycatT[:, kc, i * 128:(i + 1) * 128], Wout[:, kc, hf * 512:(hf + 1) * 512],
                               kc == 0, kc == KC - 1, [t_ycatS[i] if kc < 4 else t_ycatA[i], t_Wout], [tb[k]], kc == KC - 1)
                        tt("dve", X[:, gt, hf * 512:(hf + 1) * 512], X[:, gt, hf * 512:(hf + 1) * 512], bank(k)[:, 0:512],
                           ALU.add, [tX[gt], tb[k]], [tX[gt]])

            def chain(*gs):
                for g_ in gs:
                    yield from g_

            def all_done(b):
                return all(k_ in done for k_ in (("Sc", b, 0), ("Sc", b, 1), ("Ac", b, 0), ("Ac", b, 1)))

            def stream_NP():
                for b in range(nblk):
                    yield from gen_N(b)
                    if b >= 2:
                        yield ("wait", lambda b=b: all_done(b - 2))
                    yield from gen_P(b)
                    done.add(("P", b))

            def stream_S(i):
                for b in range(nblk):
                    yield ("wait", lambda b=b: ("P", b) in done)
                    yield from gen_S(b, i, ST[i])
                    done.add(("Sc", b, i))

            def stream_A(j):
                for b in range(nblk):
                    yield ("wait", lambda b=b: ("P", b) in done)
                    yield from gen_A(b, j, AT[j])
                    done.add(("Ac", b, j))

            def stream_O():
                for b in range(nblk):
                    yield ("wait", lambda b=b: all_done(b))
                    emit_O(b)
                    done.add(("O", b))
                    yield H("pe", t_ycatS[0])

            run_streams([stream_NP(), stream_S(0), stream_S(1), stream_A(0), stream_A(1), stream_O()])
            if do_ffn:
                c.dma("sp", wb[:], nw_ffn_d, writes=[t_wb])
            build_program.tmix = max(c.tfree.values())
            fence = c.fence_tokens()
        if do_ffn:
            with ExitStack() as fs:
                H2T = c.sb("H2T", [128, KC, L], BF16, fs)
                t_H2T = [Trk("H2T%d" % i, fence) for i in range(4)]
                actT = c.sb("actT", [128, JMAX, L], BF16, fs)
                t_actT = [[Trk("actT%d_%d" % (i, q), fence) for q in range(4)] for i in range(JMAX)]
                Wd = [c.sb("Wd%d" % i, [128, JMAX, D], BF16, fs) for i in range(2)]
                t_Wd = [Trk("Wd%d" % i, fence) for i in range(2)]
                xn2 = c.sb("xn2", [128, D], BF16, fs); t_xn2 = Trk("xn2", fence)
                sg = [c.sb("sg%d" % i, [128, 512], F32, fs) for i in range(2)]
                t_sg = [Trk("sg%d" % i, fence) for i in range(2)]
                ostg = [c.sb("ostg%d" % i, [128, D], F32, fs) for i in range(2)]
                t_ostg = [Trk("ostg%d" % i, fence) for i in range(2)]
                for i in range(1, NGU):
                    gu.append(c.sb("gu%d" % i, [128, 2, KC, 128], BF16, fs))
                    t_gu[i] = Trk("gu%d" % i, fence)

                for j in range(1, NGU):
                    gu_load(j)

                def wd_load(p):
                    j0, J = PASSES[p]
                    c.dma("pool", Wd[p % 2][:, 0:J, :], w_dn_d[:, j0:j0 + J, :], writes=[t_Wd[p % 2]])

                xn2b = c.sb("xn2b", [128, D], BF16, fs); t_xn2b = Trk("xn2b", fence)
                xn2s = [xn2, xn2b]
                t_xn2s = [t_xn2, t_xn2b]

                def norm2_A(gt):
                    xq = xn2s[gt % 2]; t_xq = t_xn2s[gt % 2]
                    s_, t_s = nsm_F()
                    act(xq[:], X[:, gt, :], AF.Square, [tX[gt]], [t_xq, t_s], accum=s_[:, 0:1])
                    ts("dve", s_[:, 1:2], s_[:, 0:1], 1.0 / D, EPS, ALU.mult, ALU.add, [t_s], [t_s])
                    tt("pool", s_[:, 3:4], s_[:, 1:2], mhalf[:, 0:1], ALU.pow, [t_s, t_mhalf], [t_s])
                    stt(xq[:], X[:, gt, :], s_[:, 3:4], wb[:], ALU.mult, ALU.mult, [tX[gt], t_s, t_wb], [t_xq])

                def norm2_B(gt):
                    xq = xn2s[gt % 2]; t_xq = t_xn2s[gt % 2]
                    k = nb()
                    for kc in range(KC):
                        tr(bankb(k)[:, kc * 128:(kc + 1) * 128], xq[:, kc * 128:(kc + 1) * 128], identb[:],
                           [t_xq, t_identb], [tb[k]], kc == KC - 1)
                    cp("act", H2T[:, :, gt * 128:(gt + 1) * 128],
                       bankb(k)[:, 0:1024].rearrange("p (k t) -> p k t", k=KC), [tb[k]], [t_H2T[gt // 4]])

                sgc = [0]

                def ffn_a(j, jj, tbk):
                    slot = j % NGU
                    tsl = slice(tbk * 512, (tbk + 1) * 512)
                    kg = nb()
                    for kc in range(KC):
                        mm(bank(kg), gu[slot][:, 0, kc, :], H2T[:, kc, tsl], kc == 0, kc == KC - 1,
                           [t_gu[slot], t_H2T[tbk]], [tb[kg]], kc == KC - 1)
                    ku = nb()
                    for kc in range(KC):
                        mm(bank(ku), gu[slot][:, 1, kc, :], H2T[:, kc, tsl], kc == 0, kc == KC - 1,
                           [t_gu[slot], t_H2T[tbk]], [tb[ku]], kc == KC - 1)
                    si = sgc[0] % 2
                    sgc[0] += 1
                    act(sg[si][:], bank(kg), AF.Silu, [tb[kg]], [t_sg[si]])
                    tt("dve", actT[:, jj, tsl], sg[si][:], bank(ku), ALU.mult, [t_sg[si], tb[ku]], [t_actT[jj][tbk]])

                def final_tile(gt):
                    s_, t_s = nsm_F()
                    o_ = ostg[gt % 2]; t_o = t_ostg[gt % 2]
                    act(o_[:], X[:, gt, :], AF.Square, [tX[gt]], [t_o, t_s], accum=s_[:, 0:1])
                    ts("dve", s_[:, 1:2], s_[:, 0:1], 1.0 / D, EPS, ALU.mult, ALU.add, [t_s], [t_s])
                    tt("pool", s_[:, 3:4], s_[:, 1:2], mhalf[:, 0:1], ALU.pow, [t_s, t_mhalf], [t_s])
                    stt(o_[:], X[:, gt, :], s_[:, 3:4], wb[:], ALU.mult, ALU.mult, [tX[gt], t_s, t_wb], [t_o])
                    c.dma("sp", out_d[gt * 128:(gt + 1) * 128, :], o_[:], reads=[t_o])

                wd_load(0)

                def stream_norm2():
                    norm2_A(0)
                    yield H("act", tX[1])
                    for gt in range(NT):
                        if gt + 1 < NT:
                            norm2_A(gt + 1)
                            yield H("pe", t_xn2s[gt % 2])
                        norm2_B(gt)
                        done.add(("H2T", gt))
                        yield H("act", tX[min(gt + 2, NT - 1)])

                def stream_ffn0():
                    j0, J = PASSES[0]
                    for tbk in range(4):
                        yield ("wait", lambda tbk=tbk: ("H2T", 4 * tbk + 3) in done)
                        for jj in range(J):
                            ffn_a(j0 + jj, jj, tbk)
                            if tbk == 3 and j0 + jj + NGU < NJ:
                                gu_load(j0 + jj + NGU)
                            yield H("pe", t_H2T[tbk])

                for p, (j0, J) in enumerate(PASSES):
                    if p + 1 < len(PASSES):
                        wd_load(p + 1)
                    last = p == len(PASSES) - 1

                    def ffn_b_tile(gt, p=p, J=J):
                        for hf in range(2):
                            k = nb()
                            for jj in range(J):
                                mm(bank(k), actT[:, jj, gt * 128:(gt + 1) * 128], Wd[p % 2][:, jj, hf * 512:(hf + 1) * 512],
                                   jj == 0, jj == J - 1, [t_actT[jj][gt // 4], t_Wd[p % 2]], [tb[k]], jj == J - 1)
                            tt("dve", X[:, gt, hf * 512:(hf + 1) * 512], X[:, gt, hf * 512:(hf + 1) * 512], bank(k),
                               ALU.add, [tX[gt], tb[k]], [tX[gt]])

                    if p == 0:
                        run_streams([stream_norm2(), stream_ffn0()])
                        c.dma("sp", wb[:], nw_fin_d, writes=[t_wb])
                        for gt in range(NT):
                            ffn_b_tile(gt)
                    elif not last:
                        for jj in range(J):
                            for tbk in range(4):
                                ffn_a(j0 + jj, jj, tbk)
                            if j0 + jj + NGU < NJ:
                                gu_load(j0 + jj + NGU)
                        for gt in range(NT):
                            ffn_b_tile(gt)
                    else:
                        for tbk in range(4):
                            for jj in range(J):
                                ffn_a(j0 + jj, jj, tbk)
                            for gt in range(4 * tbk, 4 * tbk + 4):
                                ffn_b_tile(gt)
                                final_tile(gt)
                c.wait_all("sp", t_ostg)
        else:
            for gt in range(NT):
                c.dma("sp", out_d[gt * 128:(gt + 1) * 128, :], X[:, gt, :], reads=[tX[gt]])
            c.wait_all("sp", tX)
        build_program.stats = (c.n_ins, c.n_wait, c.n_dsem, dict(c.cnt))
        build_program.tend = max(c.tfree.values())
        build_program.log = c.log
    return nc


def _consts():
    identf = np.eye(128, dtype=np.float32)
    u = np.arange(128)
    tri = (u[:, None] <= u[None, :]).astype(np.float32)
    nm = np.where(u[:, None] <= u[None, :], 0.0, NEG).astype(np.float32)
    negmask = np.tile(nm, (1, 4))
    q = u[:, None]
    kk = u[None, :]
    prev = np.where(kk > q, 0.0, NEG)
    cur = np.where(kk <= q, 0.0, NEG)
    mask = np.concatenate([prev, cur], axis=1)
    mask0 = np.concatenate([np.full((128, 128), NEG), cur], axis=1)
    maskb = np.concatenate([mask, mask0], axis=1).astype(np.float32)
    return identf, tri, negmask, maskb


def _prep_shared(inp):
    f = np.float32
    w_in = np.asarray(inp["w_in"], f)[0]
    w_in_l = np.ascontiguousarray(w_in.reshape(KC, 128, D_IN).transpose(1, 0, 2))
    kcols = w_in[:, 2056:2184]
    k2 = np.concatenate([kcols[:, 0:64], kcols[:, 0:64], kcols[:, 64:128], kcols[:, 64:128]], axis=1)
    w_k2_l = np.ascontiguousarray(k2.reshape(KC, 128, 256).transpose(1, 0, 2))
    w_out = np.asarray(inp["w_out"], f)[0]
    w_out_l = np.ascontiguousarray(w_out.reshape(KC, 128, D).transpose(1, 0, 2))
    wg = np.asarray(inp["w_gate"], f)[0].reshape(KC, 128, NJ, 128)
    wu = np.asarray(inp["w_up"], f)[0].reshape(KC, 128, NJ, 128)
    w_gu = np.ascontiguousarray(np.stack([wg, wu], axis=0).transpose(3, 2, 0, 1, 4))
    w_dn = np.asarray(inp["w_down"], f)[0]
    w_dn_l = np.ascontiguousarray(w_dn.reshape(NJ, 128, D).transpose(1, 0, 2))

    def bc(v, n):
        return np.ascontiguousarray(np.broadcast_to(np.asarray(v, f).reshape(1, n), (128, n)))

    convw = np.ascontiguousarray(np.asarray(inp["conv_w"], f)[0].T.reshape(8, 128, 4).transpose(1, 0, 2))
    convb = np.ascontiguousarray(np.asarray(inp["conv_b"], f)[0].reshape(8, 128).T)
    small = np.zeros((128, 48), f)
    small[:, 0:8] = convb
    small[:, 8:16] = bc(inp["dt_bias"][0], 8)
    small[:, 16:24] = bc(inp["a_log"][0], 8)
    small[:, 24:32] = bc(inp["d_skip"][0], 8)
    small[:, 32:40] = bc(inp["attn_sinks"][0], 8)
    identf, tri, negmask, maskb = _consts()
    return {
        "w_in": w_in_l, "w_k2": w_k2_l, "w_out": w_out_l, "w_gu": w_gu, "w_dn": w_dn_l,
        "nw_mix": bc(inp["norm_mix_w"][0], D), "nw_ffn": bc(inp["norm_ffn_w"][0], D),
        "nw_fin": bc(inp["norm_final_w"], D), "convw": convw, "smallp": small,
        "ssmw": bc(inp["ssm_norm_w"][0], 512), "identf": identf, "tri": tri,
        "negmask": negmask, "maskb": maskb,
    }


_NC_CACHE = {}


def kernel(**inputs):
    x = np.asarray(inputs["x"], np.float32)
    shared = _prep_shared(inputs)
    if "nc" not in _NC_CACHE:
        _NC_CACHE["nc"] = build_program()
    nc = _NC_CACHE["nc"]
    in_maps = []
    for b in range(8):
        m = dict(shared)
        m["x"] = np.ascontiguousarray(x[b])
        in_maps.append(m)
    res = run_bass_kernel_spmd(nc, in_maps, core_ids=list(range(8)))
    out = np.stack([np.asarray(r["out"], np.float32) for r in res.results], axis=0)
    return out
```

```python
from contextlib import ExitStack

import numpy as np
import concourse.bass as bass
import concourse.mybir as mybir
from concourse.bass_utils import run_bass_kernel_spmd

F32 = mybir.dt.float32
BF16 = mybir.dt.bfloat16
AF = mybir.ActivationFunctionType
ALU = mybir.AluOpType
AX = mybir.AxisListType

SAME_ENGINE_SYNC = True
SCHED = "rr"

L = 2048
D = 1024
NT = 16
KC = 8
TB = 256
NBLK = L // TB
D_IN = 2312
D_FF = 2816
NJ = D_FF // 128
PASSES = [(0, 6), (6, 6), (12, 5), (17, 5)]
JMAX = 6
EPS = 1e-5
NEG = -30000.0


class Trk:
    __slots__ = ("name", "w", "r", "dsem", "dcnt", "excl")

    def __init__(self, name="", fence=None, excl=False):
        self.name = name
        self.excl = excl
        self.w = None
        self.r = list(fence) if fence else []
        self.dsem = None
        self.dcnt = 0


class Ctx:
    def __init__(self, nc, stack):
        self.nc = nc
        self.stack = stack
        self.eng = {"pe": nc.tensor, "act": nc.scalar, "dve": nc.vector,
                    "pool": nc.gpsimd, "sp": nc.sync}
        self.sem = {}
        self.cnt = {}
        self.seen = {}
        for k in self.eng:
            self.sem[k] = stack.enter_context(nc.semaphore("s_" + k))
            self.cnt[k] = 0
            self.seen[k] = {}
        self.n_dsem = 0
        self.n_wait = 0
        self.n_ins = 0
        self.tfree = {k: 0.0 for k in self.eng}
        self.ttok = {}
        self.step_fin = 0.0
        self.log = None
        self.tag = ""
        self.tokinfo = {}
        self.crit = None

    def _tdeps(self, e, deps):
        t = 0.0
        self.crit = None
        for d in deps:
            if d is None:
                continue
            lat = 0.05 if d[0] is self.sem.get(e) else 0.25
            td = self.ttok.get((id(d[0]), d[1]), 0.0) + lat
            if td > t:
                t = td
                self.crit = self.tokinfo.get((id(d[0]), d[1]))
        return t

    def sb(self, name, shape, dtype, stack=None):
        return (stack or self.stack).enter_context(
            self.nc.sbuf_tensor("sb_" + name, list(shape), dtype))

    def _wait(self, e, deps):
        h = self.eng[e]
        best = {}
        for d in deps:
            if d is None:
                continue
            s, v = d
            if s is self.sem.get(e):
                if e == "pe" or not SAME_ENGINE_SYNC or v > self.cnt[e]:
                    continue
            k = id(s)
            if k not in best or best[k][1] < v:
                best[k] = (s, v)
        for k, (s, v) in best.items():
            if self.seen[e].get(k, 0) >= v:
                continue
            h.wait_ge(s, v)
            self.n_wait += 1
            self.seen[e][k] = v

    def op(self, e, fn, reads=(), writes=(), signal=True, cost=0.1):
        deps = []
        for t in reads:
            deps.append(t.w)
            if t.excl:
                deps.extend(r for r in t.r if r[0] is not self.sem[e])
        for t in writes:
            deps.append(t.w)
            deps.extend(t.r)
        self._wait(e, deps)
        ins = fn()
        self.n_ins += 1
        self.tfree_prev = self.tfree[e]
        start = max(self.tfree[e], self._tdeps(e, deps))
        fin = start + cost
        self.tfree[e] = fin
        if signal:
            self.cnt[e] += 1
            ins.then_inc(self.sem[e], 1)
            tok = (self.sem[e], self.cnt[e])
        else:
            tok = (self.sem[e], self.cnt[e] + 1)
        self.ttok[(id(tok[0]), tok[1])] = fin + (0.1 if e == "pe" else 0.0)
        self.step_fin = max(self.step_fin, fin)
        if self.log is not None:
            try:
                nm = str(ins.ins.name)
                self.tokinfo[(id(tok[0]), tok[1])] = (nm, self.tag, e)
                self.log.append((nm, e, self.tag, start, fin, self.crit, self.tfree_prev))
            except Exception:
                pass
        for t in reads:
            t.r.append(tok)
        for t in writes:
            t.w = tok
            t.r = []
        return ins

    def dma(self, q, out, in_, reads=(), writes=(), owner=None, join=False, nbytes=65536):
        own = owner or (writes[0] if writes else reads[0])
        if own.dsem is None:
            own.dsem = self.stack.enter_context(self.nc.semaphore("d%d" % self.n_dsem))
            self.n_dsem += 1
        deps = []
        for t in reads:
            deps.append(t.w)
        for t in writes:
            if not (join and t.w is not None and t.w[0] is own.dsem):
                deps.append(t.w)
            deps.extend(t.r)
        self._wait(q, deps)
        own.dcnt += 16
        self.eng[q].dma_start(out=out, in_=in_).then_inc(own.dsem, 16)
        self.n_ins += 1
        tok = (own.dsem, own.dcnt)
        start = max(self.tfree[q], self._tdeps(q, deps))
        self.tfree[q] = start + (1.0 if q == "pool" else 0.1)
        self.ttok[(id(tok[0]), tok[1])] = start + 2.5 + nbytes / 2.0e5
        for t in reads:
            t.r.append(tok)
        for t in writes:
            t.w = tok
            t.r = []

    def wait_all(self, e, trks):
        deps = []
        for t in trks:
            deps.append(t.w)
            deps.extend(t.r)
        self._wait(e, deps)

    def fence_tokens(self):
        toks = []
        for e in ("pe", "act", "dve", "pool"):
            if self.cnt[e] > 0:
                toks.append((self.sem[e], self.cnt[e]))
        return toks


def build_program(nblk=NBLK, do_ffn=True, debug=()):
    nc = bass.Bass("TRN2", target_bir_lowering=False)

    def din(name, shape):
        return nc.dram_tensor(name, list(shape), F32, kind="ExternalInput").ap()

    x_d = din("x", [L, D])
    w_in_d = din("w_in", [128, KC, D_IN])
    w_k2_d = din("w_k2", [128, KC, 256])
    w_out_d = din("w_out", [128, KC, D])
    w_gu_d = din("w_gu", [NJ, 128, 2, KC, 128])
    w_dn_d = din("w_dn", [128, NJ, D])
    nw_mix_d = din("nw_mix", [128, D])
    nw_ffn_d = din("nw_ffn", [128, D])
    nw_fin_d = din("nw_fin", [128, D])
    convw_d = din("convw", [128, 8, 4])
    sm_d = din("smallp", [128, 48])
    ssmw_d = din("ssmw", [128, 512])
    identf_d = din("identf", [128, 128])
    tri_d = din("tri", [128, 128])
    negmask_d = din("negmask", [128, 512])
    maskb_d = din("maskb", [128, 512])
    out_d = nc.dram_tensor("out", [L, D], F32, kind="ExternalOutput").ap()
    dbg_outs = {}

    with ExitStack() as st:
        c = Ctx(nc, st)
        op = c.op
        if debug:
            c.log = []

        ps = st.enter_context(nc.psum_tensor("ps", [128, 4096], F32))
        psb = ps.bitcast(BF16) if hasattr(ps, "bitcast") else None
        tb = [Trk("bank%d" % k, excl=True) for k in range(8)]
        bank_ctr = [0]

        def bank(k):
            return ps[:, k * 512:(k + 1) * 512]

        def bankb(k):
            return psb[:, k * 1024:(k + 1) * 1024]

        def nb():
            k = bank_ctr[0] % 8
            bank_ctr[0] += 1
            return k

        def nb2():
            if bank_ctr[0] % 2:
                bank_ctr[0] += 1
            k = bank_ctr[0] % 8
            bank_ctr[0] += 2
            return k

        def nfree(ap):
            n = 1
            for d in ap.shape[1:]:
                n *= int(d)
            return n

        def mm(out, lhsT, rhs, start, stop, reads, writes, signal):
            n = max(nfree(rhs), 64)
            cst = n / 2400.0 * (4.0 if rhs.dtype == F32 else 1.0) + 0.01
            op("pe", lambda: nc.tensor.matmul(out, lhsT=lhsT, rhs=rhs, start=start, stop=stop),
               reads=reads, writes=writes, signal=signal, cost=cst)

        def tr(out, in_, ident, reads, writes, signal):
            op("pe", lambda: nc.tensor.transpose(out, in_, ident),
               reads=reads, writes=writes, signal=signal, cost=0.07)

        def act(out, in_, func, reads, writes, bias=None, scale=None, accum=None):
            kw = {}
            if bias is not None:
                kw["bias"] = bias
            if scale is not None:
                kw["scale"] = scale
            if accum is not None:
                kw["accum_out"] = accum
            op("act", lambda: nc.scalar.activation(out=out, in_=in_, func=func, **kw),
               reads=reads, writes=writes, cost=0.12 + nfree(out) / 1200.0 + (0.1 if accum is not None else 0.0))

        def ecost(e, out, mult=1.0):
            n = nfree(out)
            if e == "dve":
                return 0.07 + mult * n / 960.0
            return 1.0 + n / 400.0

        def tt(e, out, in0, in1, aop, reads, writes):
            h = nc.vector if e == "dve" else nc.gpsimd
            op(e, lambda: h.tensor_tensor(out=out, in0=in0, in1=in1, op=aop),
               reads=reads, writes=writes, cost=ecost(e, out))

        def ts(e, out, in0, s1, s2, op0, op1, reads, writes):
            h = nc.vector if e == "dve" else nc.gpsimd
            if op1 is None:
                op(e, lambda: h.tensor_scalar(out=out, in0=in0, scalar1=s1, scalar2=None, op0=op0),
                   reads=reads, writes=writes, cost=ecost(e, out))
            else:
                op(e, lambda: h.tensor_scalar(out=out, in0=in0, scalar1=s1, scalar2=s2, op0=op0, op1=op1),
                   reads=reads, writes=writes, cost=ecost(e, out))

        def stt(out, in0, scalar, in1, op0, op1, reads, writes):
            op("dve", lambda: nc.vector.scalar_tensor_tensor(out=out, in0=in0, scalar=scalar, in1=in1,
                                                             op0=op0, op1=op1),
               reads=reads, writes=writes, cost=ecost("dve", out))

        def cp(e, out, in_, reads, writes):
            if e == "act":
                op("act", lambda: nc.scalar.copy(out=out, in_=in_), reads=reads, writes=writes,
                   cost=0.12 + nfree(out) / 1200.0)
            else:
                h = nc.vector if e == "dve" else nc.gpsimd
                op(e, lambda: h.tensor_copy(out=out, in_=in_), reads=reads, writes=writes, cost=ecost(e, out))

        def dbg(name, ap, trk, shape):
            if name not in debug:
                return
            d = nc.dram_tensor("dbg_" + name, list(shape), F32, kind="ExternalOutput").ap()
            stg = c.sb("dbgs_" + name, list(shape), F32)
            t = Trk()
            cp("dve", stg[:], ap, [trk], [t])
            c.dma("sp", d, stg[:], reads=[t])
            dbg_outs[name] = t

        X = c.sb("X", [128, NT, D], F32)
        tX = [Trk("X%d" % i) for i in range(NT)]
        identb = c.sb("identb", [128, 128], BF16); t_identb = Trk()
        tri = c.sb("tri", [128, 128], F32); t_tri = Trk()
        negmask = c.sb("negmask", [128, 512], BF16); t_negmask = Trk()
        maskb = c.sb("maskb", [128, 512], BF16); t_maskb = Trk()
        convw = c.sb("convw", [128, 8, 4], F32); t_convw = Trk()
        smallp = c.sb("smallp", [128, 48], F32); t_small = Trk()
        a_b = c.sb("a_b", [128, 8], F32); t_ab = Trk()
        ssmw = c.sb("ssmw", [128, 512], F32); t_ssmw = Trk()
        mhalf = c.sb("mhalf", [128, 2], F32); t_mhalf = Trk()
        wb = c.sb("wb", [128, D], F32); t_wb = Trk()
        NGU = 6
        gu = [c.sb("gu%d" % i, [128, 2, KC, 128], BF16) for i in range(1)]
        t_gu = [Trk("gu%d" % i) for i in range(NGU)]
        convb = smallp[:, 0:8]
        dtb = smallp[:, 8:16]
        alog = smallp[:, 16:24]
        dskip = smallp[:, 24:32]
        sinks = smallp[:, 32:40]

        def stat_tiles(name, n, w=64, stack=None):
            tiles = [c.sb("%s%d" % (name, i), [128, w], F32, stack) for i in range(n)]
            trks = [Trk("%s%d" % (name, i)) for i in range(n)]
            ctr = [0]

            def nxt():
                k = ctr[0] % n
                ctr[0] += 1
                return tiles[k], trks[k]
            return nxt

        nsm_N = stat_tiles("stN", 2, 8)
        nsm_F = stat_tiles("stF", 2, 8)

        def gu_load(j):
            slot = j % NGU
            c.dma("pool", gu[slot][:], w_gu_d[j], writes=[t_gu[slot]])

        done = set()

        def H(eng, *trks):
            return ("h", eng, list(trks))

        def vt_of(s_):
            if SCHED == "bal":
                return s_["vt"]
            if SCHED == "rr":
                return s_["k"]
            h_ = s_["hint"]
            if h_ is None:
                return s_["vt"]
            t_ = c.tfree[h_[0]]
            for q_ in h_[1]:
                if q_.w is not None:
                    t_ = max(t_, c.ttok.get((id(q_.w[0]), q_.w[1]), 0.0))
            return t_

        def run_streams(gens):
            sts = [{"g": g_, "vt": 0.0, "blk": None, "nm": getattr(g_, "__name__", "?"), "k": 0, "hint": None} for g_ in gens]
            while sts:
                progressed = False
                for s_ in sorted(sts, key=vt_of):
                    if s_["blk"] is not None:
                        if not s_["blk"]():
                            continue
                        s_["blk"] = None
                    c.step_fin = 0.0
                    s_["k"] += 1
                    c.tag = "%s#%d" % (s_["nm"], s_["k"])
                    try:
                        r_ = next(s_["g"])
                    except StopIteration:
                        sts.remove(s_)
                        progressed = True
                        break
                    if isinstance(r_, tuple) and r_[0] == "wait":
                        if not r_[1]():
                            s_["blk"] = r_[1]
                    elif isinstance(r_, tuple) and r_[0] == "h":
                        s_["hint"] = (r_[1], r_[2])
                        s_["vt"] = max(s_["vt"], c.step_fin)
                    else:
                        s_["hint"] = None
                        s_["vt"] = max(s_["vt"], c.step_fin)
                    progressed = True
                    break
                if not progressed:
                    raise RuntimeError("emission deadlock")

        c.dma("sp", X[:, 0, :], x_d[0:128, :], writes=[tX[0]])
        c.dma("sp", X[:, 1, :], x_d[128:256, :], writes=[tX[1]])
        c.dma("sp", smallp[:], sm_d, writes=[t_small])
        c.dma("sp", convw[:], convw_d, writes=[t_convw])
        c.dma("sp", tri[:], tri_d, writes=[t_tri])
        c.dma("sp", ssmw[:], ssmw_d, writes=[t_ssmw])
        c.dma("pool", identb[:], identf_d, writes=[t_identb])
        c.dma("pool", negmask[:], negmask_d, writes=[t_negmask])
        op("pool", lambda: nc.gpsimd.memset(mhalf[:], -0.5), writes=[t_mhalf])

        with ExitStack() as ms:
            Win = c.sb("Win", [128, KC, D_IN], BF16, ms)
            t_Wxbc = Trk("Wxbc"); t_Wq = Trk("Wq"); t_Wz = Trk("Wz"); t_Wdt = t_Wq; t_Wv = t_Wq
            Wk2 = c.sb("Wk2", [128, KC, 256], BF16, ms); t_Wk2 = Trk()
            Wout = c.sb("Wout", [128, KC, D], BF16, ms); t_Wout = Trk("Wout")
            hT1 = c.sb("hT", [128, KC, TB], BF16, ms)
            t_hT1 = Trk("hT")
            hT = [hT1, hT1]
            t_hT = [t_hT1, t_hT1]
            xn = c.sb("xn", [128, D], BF16, ms); t_xn = Trk()
            pre2 = [c.sb("pre%d" % i, [128, TB + 3], F32, ms) for i in range(2)]
            t_pre2 = [Trk() for _ in range(2)]
            halo = c.sb("halo", [128, 8, 3], F32, ms)
            t_halo = [Trk() for _ in range(8)]
            acc = [c.sb("acc%d" % i, [128, TB], F32, ms) for i in range(2)]
            t_acc = [Trk() for _ in range(2)]
            sgt = [c.sb("sgt%d" % i, [128, TB], F32, ms) for i in range(2)]
            t_sgt = [Trk() for _ in range(2)]
            post = [c.sb("post%d" % i, [128, 8, TB], BF16, ms) for i in range(2)]
            t_post = [[Trk() for _ in range(8)] for _ in range(2)]
            qT = [c.sb("qT%d" % i, [128, 4, TB], BF16, ms) for i in range(2)]
            t_qT = [[Trk() for _ in range(4)] for _ in range(2)]
            kT = [c.sb("kT%d" % i, [128, 2, 128 + TB], BF16, ms) for i in range(2)]
            t_kT = [[Trk() for _ in range(2)] for _ in range(2)]
            vtok = [c.sb("vtok%d" % i, [128, 3, 128], BF16, ms) for i in range(2)]
            t_vtok = [Trk() for _ in range(2)]
            sz = [c.sb("sz%d" % i, [128, 2, 512], F32, ms) for i in range(2)]
            t_sz = [[Trk() for _ in range(2)] for _ in range(2)]
            dtraw = [c.sb("dtraw%d" % i, [128, 2, 8], F32, ms) for i in range(2)]
            t_dtraw = [[Trk() for _ in range(2)] for _ in range(2)]
            class NS:
                pass

            def Sset(q):
                T = NS()
                T.cbs = c.sb("cbs%d" % q, [128, 256], F32, ms); T.t_cbs = Trk()
                T.MT = c.sb("MT%d" % q, [128, 1024], BF16, ms); T.t_MT = Trk()
                T.xdt = c.sb("xdt%d" % q, [128, 512], BF16, ms); T.t_xdt = Trk()
                T.xdte = c.sb("xdte%d" % q, [128, 512], BF16, ms); T.t_xdte = Trk()
                T.xsD = c.sb("xsD%d" % q, [128, 512], BF16, ms); T.t_xsD = Trk()
                T.Btok = c.sb("Btok%d" % q, [128, 256], BF16, ms); T.t_Btok = Trk()
                T.ybuf = c.sb("ybuf%d" % q, [128, 512], F32, ms); T.t_ybuf = Trk()
                T.ysn = c.sb("ysn%d" % q, [128, 512], BF16, ms); T.t_ysn = Trk()
                T.nsm1 = stat_tiles("stS1_%d" % q, 1, 64, ms)
                T.nsm2 = stat_tiles("stS2_%d" % q, 1, 32, ms)
                return T

            def Aset(q):
                T = NS()
                T.Pexp = c.sb("Pexp%d" % q, [128, 1024], BF16, ms); T.t_Pexp = Trk()
                T.PT = c.sb("PT%d" % q, [128, 1024], BF16, ms); T.t_PT = Trk()
                T.yattn = c.sb("yattn%d" % q, [128, 512], BF16, ms); T.t_yattn = Trk()
                T.nsm = stat_tiles("stA_%d" % q, 1, 32, ms)
                return T

            ST = [Sset(0), Sset(1)]
            AT = [Aset(0), Aset(1)]
            S_f = c.sb("S_f", [128, 512], F32, ms); t_Sf = Trk()
            S_b = c.sb("S_b", [128, 512], BF16, ms); t_Sb = Trk()
            ycatT = c.sb("ycatT", [128, 8, TB], BF16, ms)
            t_ycatS = [Trk() for _ in range(2)]
            t_ycatA = [Trk() for _ in range(2)]

            c.dma("sp", wb[:], nw_mix_d, writes=[t_wb])
            for kc in range(KC):
                c.dma("pool", Win[:, kc, 512:1536], w_in_d[:, kc, 512:1536], writes=[t_Wxbc], join=True)
            for kc in range(KC):
                c.dma("pool", Win[:, kc, 1536:2312], w_in_d[:, kc, 1536:2312], writes=[t_Wq], join=True)
            c.dma("pool", Wk2[:], w_k2_d, writes=[t_Wk2])
            c.dma("pool", maskb[:], maskb_d, writes=[t_maskb])
            for kc in range(KC):
                c.dma("pool", Win[:, kc, 0:512], w_in_d[:, kc, 0:512], writes=[t_Wz], join=True)
            c.wait_all("sp", [t_Wxbc])
            for i in range(2, NT):
                c.dma("sp", X[:, i, :], x_d[i * 128:(i + 1) * 128, :], writes=[tX[i]])
            for kc in range(KC):
                c.dma("pool", Wout[:, kc, :], w_out_d[:, kc, :], writes=[t_Wout], join=True)
            if do_ffn:
                gu_load(0)

            act(a_b[:], alog, AF.Exp, [t_small], [t_ab])
            ts("dve", a_b[:], a_b[:], -1.0, None, ALU.mult, None, [t_ab], [t_ab])
            op("dve", lambda: nc.vector.memset(S_f[:], 0.0), writes=[t_Sf])
            op("dve", lambda: nc.vector.memset(S_b[:], 0.0), writes=[t_Sb])
            op("pool", lambda: nc.gpsimd.memset(halo[:], 0.0), writes=t_halo)
            for i in range(2):
                op("pool", lambda: nc.gpsimd.memset(kT[i][:], 0.0), writes=t_kT[i])
                op("pool", lambda: nc.gpsimd.memset(vtok[i][:], 0.0), writes=[t_vtok[i]])

            def rms_stats(src_ap, t_src, n, junk_ap, t_jk, nsm):
                s_, t_s = nsm()
                act(junk_ap, src_ap, AF.Square, [t_src], [t_jk, t_s], accum=s_[:, 0:1])
                ts("dve", s_[:, 1:2], s_[:, 0:1], 1.0 / n, EPS, ALU.mult, ALU.add, [t_s], [t_s])
                tt("pool", s_[:, 3:4], s_[:, 1:2], mhalf[:, 0:1], ALU.pow, [t_s, t_mhalf], [t_s])
                return s_, t_s

            def gen_N(b):
                s = b % 2
                for i in range(2):
                    gt = 2 * b + i
                    s_, t_s = rms_stats(X[:, gt, :], tX[gt], D, xn[:], t_xn, nsm_N)
                    yield H("dve", t_s)
                    stt(xn[:], X[:, gt, :], s_[:, 3:4], wb[:], ALU.mult, ALU.mult,
                        [tX[gt], t_s, t_wb], [t_xn])
                    k = nb()
                    for kc in range(KC):
                        tr(bankb(k)[:, kc * 128:(kc + 1) * 128], xn[:, kc * 128:(kc + 1) * 128], identb[:],
                           [t_xn, t_identb], [tb[k]], kc == KC - 1)
                    cp("act", hT[s][:, :, i * 128:(i + 1) * 128],
                       bankb(k)[:, 0:1024].rearrange("p (k t) -> p k t", k=KC), [tb[k]], [t_hT[s]])
                    yield H("act", tX[min(2 * b + i + 1, NT - 1)])

            def gen_P(b):
                s = b % 2
                o = 1 - s
                for pr in range(4):
                    ocs = (2 * pr, 2 * pr + 1)
                    ks = []
                    for q_, oc in enumerate(ocs):
                        k = nb()
                        ks.append(k)
                        c0 = 512 + oc * 128
                        for kc in range(KC):
                            mm(bank(k)[:, 0:TB], Win[:, kc, c0:c0 + 128], hT[s][:, kc, :], kc == 0, kc == KC - 1,
                               [t_Wxbc, t_hT[s]], [tb[k]], kc == KC - 1)
                    for q_, oc in enumerate(ocs):
                        cp("dve", pre2[q_][:, 0:3], halo[:, oc, :], [t_halo[oc]], [t_pre2[q_]])
                    for q_, oc in enumerate(ocs):
                        cp("act", pre2[q_][:, 3:TB + 3], bank(ks[q_])[:, 0:TB], [tb[ks[q_]]], [t_pre2[q_]])
                    for q_, oc in enumerate(ocs):
                        cp("dve", halo[:, oc, :], pre2[q_][:, TB:TB + 3], [t_pre2[q_]], [t_halo[oc]])
                    for q_, oc in enumerate(ocs):
                        ts("dve", acc[q_][:], pre2[q_][:, 0:TB], convw[:, oc, 0:1], convb[:, oc:oc + 1], ALU.mult, ALU.add,
                           [t_pre2[q_], t_convw, t_small], [t_acc[q_]])
                    for kk in range(1, 4):
                        for q_, oc in enumerate(ocs):
                            stt(acc[q_][:], pre2[q_][:, kk:kk + TB], convw[:, oc, kk:kk + 1], acc[q_][:], ALU.mult, ALU.add,
                                [t_pre2[q_], t_convw, t_acc[q_]], [t_acc[q_]])
                    for q_, oc in enumerate(ocs):
                        act(sgt[q_][:], acc[q_][:], AF.Exp, [t_acc[q_]], [t_sgt[q_]], scale=-1.0)
                    for q_, oc in enumerate(ocs):
                        act(sgt[q_][:], sgt[q_][:], AF.Ln, [t_sgt[q_]], [t_sgt[q_]], bias=1.0)
                    for q_, oc in enumerate(ocs):
                        act(sgt[q_][:], sgt[q_][:], AF.Exp, [t_sgt[q_]], [t_sgt[q_]], scale=-1.0)
                    for q_, oc in enumerate(ocs):
                        tt("dve", post[s][:, oc, :], acc[q_][:], sgt[q_][:], ALU.mult, [t_acc[q_], t_sgt[q_]], [t_post[s][oc]])
                    yield H("pe", t_hT[s])
                for oc in range(4):
                    k = nb()
                    c0 = 1544 + oc * 128
                    for kc in range(KC):
                        mm(bank(k)[:, 0:TB], Win[:, kc, c0:c0 + 128], hT[s][:, kc, :], kc == 0, kc == KC - 1,
                           [t_Wq, t_hT[s]], [tb[k]], kc == KC - 1)
                    cp("act", qT[s][:, oc, :], bank(k)[:, 0:TB], [tb[k]], [t_qT[s][oc]])
                    if oc % 2:
                        yield H("pe", t_hT[s])
                for g in range(2):
                    k = nb()
                    for kc in range(KC):
                        mm(bank(k)[:, 0:TB], Wk2[:, kc, g * 128:(g + 1) * 128], hT[s][:, kc, :], kc == 0, kc == KC - 1,
                           [t_Wk2, t_hT[s]], [tb[k]], kc == KC - 1)
                    if b > 0:
                        cp("dve", kT[s][:, g, 0:128], kT[o][:, g, TB:TB + 128], [t_kT[o][g]], [t_kT[s][g]])
                    cp("act", kT[s][:, g, 128:128 + TB], bank(k)[:, 0:TB], [tb[k]], [t_kT[s][g]])
                if b > 0:
                    cp("dve", vtok[s][:, 0, :], vtok[o][:, 2, :], [t_vtok[o]], [t_vtok[s]])
                yield H("pe", t_hT[s])
                for i in range(2):
                    k = nb()
                    for kc in range(KC):
                        mm(bank(k)[:, 0:512], hT[s][:, kc, i * 128:(i + 1) * 128], Win[:, kc, 0:512], kc == 0, kc == KC - 1,
                           [t_Wz, t_hT[s]], [tb[k]], kc == KC - 1)
                    act(sz[s][:, i, :], bank(k)[:, 0:512], AF.Exp, [tb[k]], [t_sz[s][i]], scale=-1.0)
                    act(sz[s][:, i, :], sz[s][:, i, :], AF.Ln, [t_sz[s][i]], [t_sz[s][i]], bias=1.0)
                    act(sz[s][:, i, :], sz[s][:, i, :], AF.Exp, [t_sz[s][i]], [t_sz[s][i]], scale=-1.0)
                    tt("dve", sz[s][:, i, :], sz[s][:, i, :], bank(k)[:, 0:512], ALU.mult, [t_sz[s][i], tb[k]], [t_sz[s][i]])
                    yield H("pe", t_hT[s])
                    k = nb()
                    for kc in range(KC):
                        mm(bank(k)[:, 0:8], hT[s][:, kc, i * 128:(i + 1) * 128], Win[:, kc, 1536:1544], kc == 0, kc == KC - 1,
                           [t_Wdt, t_hT[s]], [tb[k]], False)
                    for kc in range(KC):
                        mm(bank(k)[:, 128:256], hT[s][:, kc, i * 128:(i + 1) * 128], Win[:, kc, 2184:2312], kc == 0, kc == KC - 1,
                           [t_Wv, t_hT[s]], [tb[k]], kc == KC - 1)
                    tt("dve", dtraw[s][:, i, :], bank(k)[:, 0:8], dtb, ALU.add, [tb[k], t_small], [t_dtraw[s][i]])
                    cp("dve", vtok[s][:, 1 + i, :], bank(k)[:, 128:256], [tb[k]], [t_vtok[s]])
                    yield H("pe", t_hT[s])

            def gen_S(b, i, T):
                par = b % 2
                post_ = post[par]
                tp_ = t_post[par]
                gc = 2 * b + i
                tsl = slice(i * 128, (i + 1) * 128)
                s1, t_s1 = T.nsm1()
                s2, t_s2 = T.nsm2()
                MT = T.MT; t_MT = T.t_MT
                ts("dve", s1[:, 0:8], dtraw[par][:, i, :], 60.0, None, ALU.min, None, [t_dtraw[par][i]], [t_s1])
                act(s1[:, 8:16], s1[:, 0:8], AF.Exp, [t_s1], [t_s1])
                act(s1[:, 16:24], s1[:, 8:16], AF.Ln, [t_s1], [t_s1], bias=1.0)
                tt("dve", s1[:, 24:32], s1[:, 16:24], a_b[:], ALU.mult, [t_s1, t_ab], [t_s1])
                dt_ = s1[:, 16:24]
                dA = s1[:, 24:32]
                kcb = nb()
                for g in range(2):
                    mm(bank(kcb)[:, g * 128:(g + 1) * 128], post_[:, 4 + g, tsl], post_[:, 6 + g, tsl], True, True,
                       [tp_[4 + g], tp_[6 + g]], [tb[kcb]], g == 1)
                cp("act", T.cbs[:], bank(kcb)[:, 0:256], [tb[kcb]], [T.t_cbs])
                yield H("pe", t_s1)
                ka = nb2()
                for hf in range(2):
                    for hh in range(4):
                        h = hf * 4 + hh
                        mm(bank(ka + hf)[:, hh * 128:(hh + 1) * 128], s1[:, 24 + h:25 + h].to_broadcast([128, 128]), tri[:],
                           hh == 0, False, [t_tri, t_s1], [tb[ka + hf]], False)
                    mm(bank(ka + hf), identb[:], negmask[:], False, True,
                       [t_identb, t_negmask], [tb[ka + hf]], True)
                Ab = ps[:, ka * 512:ka * 512 + 1024].rearrange("p (h t) -> p h t", t=128)
                Alast = Ab[:, :, 127:128].rearrange("p h o -> p (h o)")
                t_A = [tb[ka], tb[ka + 1]]
                kcu = nb()
                mm(bank(kcu)[:, 0:8], tri[:], dA, True, True, [t_tri, t_s1], [tb[kcu]], True)
                cp("dve", s1[:, 32:40], bank(kcu)[:, 0:8], [tb[kcu]], [t_s1])
                ts("dve", s1[:, 40:48], bank(kcu)[:, 0:8], -1.0, None, ALU.mult, None, [tb[kcu]], [t_s1])
                act(s1[:, 48:56], bank(kcu)[:, 0:8], AF.Exp, [tb[kcu]], [t_s1])
                tt("dve", s1[:, 56:64], Alast, s1[:, 32:40], ALU.subtract, t_A + [t_s1], [t_s1])
                act(s2[:, 0:8], s1[:, 56:64], AF.Exp, [t_s1], [t_s2])
                act(s2[:, 8:16], Alast, AF.Exp, t_A, [t_s2])
                tt("dve", s2[:, 16:24], dt_, s2[:, 0:8], ALU.mult, [t_s1, t_s2], [t_s2])
                for h in range(8):
                    tbh = tb[ka + h // 4]
                    act(Ab[:, h, :], Ab[:, h, :], AF.Exp, [tbh, t_s1], [tbh], bias=s1[:, 40 + h:41 + h])
                for g in range(2):
                    tt("dve", MT[:, g * 512:(g + 1) * 512].rearrange("p (r t) -> p r t", r=4),
                       Ab[:, 4 * g:4 * g + 4, :],
                       T.cbs[:, g * 128:(g + 1) * 128].unsqueeze(1).to_broadcast([128, 4, 128]),
                       ALU.mult, [tb[ka + g], T.t_cbs], [t_MT])
                yield H("pe", t_s1, t_s2)
                kx = nb()
                for cc in range(4):
                    tr(bankb(kx)[:, cc * 128:(cc + 1) * 128], post_[:, cc, tsl], identb[:],
                       [tp_[cc], t_identb], [tb[kx]], False)
                for g in range(2):
                    tr(bankb(kx)[:, 512 + g * 128:512 + (g + 1) * 128], post_[:, 4 + g, tsl], identb[:],
                       [tp_[4 + g], t_identb], [tb[kx]], g == 1)
                xs_ps = bankb(kx)[:, 0:512].rearrange("p (h d) -> p h d", h=8)
                tt("dve", T.xdt[:].rearrange("p (h d) -> p h d", h=8), xs_ps,
                   dt_.unsqueeze(2).to_broadcast([128, 8, 64]), ALU.mult, [tb[kx], t_s1], [T.t_xdt])
                tt("dve", T.xdte[:].rearrange("p (h d) -> p h d", h=8), xs_ps,
                   s2[:, 16:24].unsqueeze(2).to_broadcast([128, 8, 64]), ALU.mult, [tb[kx], t_s2], [T.t_xdte])
                tt("dve", T.xsD[:].rearrange("p (h d) -> p h d", h=8), xs_ps,
                   dskip.unsqueeze(2).to_broadcast([128, 8, 64]), ALU.mult, [tb[kx], t_small], [T.t_xsD])
                cp("act", T.Btok[:], bankb(kx)[:, 512:768], [tb[kx]], [T.t_Btok])
                if gc > 0:
                    yield ("wait", lambda: ("Sst", gc - 1) in done)
                yield H("pe", t_MT, T.t_xdt, T.t_xdte, T.t_xsD, T.t_Btok, t_Sb)
                kcs = nb()
                for g in range(2):
                    mm(bank(kcs)[:, g * 256:(g + 1) * 256], T.Btok[:, g * 128:(g + 1) * 128],
                       T.xdte[:, g * 256:(g + 1) * 256], True, True, [T.t_Btok, T.t_xdte], [tb[kcs]], g == 1)
                ko = nb()
                for g in range(2):
                    mm(bank(ko)[:, g * 256:(g + 1) * 256], post_[:, 6 + g, tsl], S_b[:, g * 256:(g + 1) * 256],
                       True, True, [tp_[6 + g], t_Sb], [tb[ko]], g == 1)
                kd = nb()
                mm(bank(kd)[:, 0:512], identb[:], T.xsD[:], True, False, [t_identb, T.t_xsD], [tb[kd]], False)
                for h in range(8):
                    mm(bank(kd)[:, h * 64:(h + 1) * 64], MT[:, h * 128:(h + 1) * 128], T.xdt[:, h * 64:(h + 1) * 64],
                       False, h == 7, [t_MT, T.t_xdt], [tb[kd]], h == 7)
                ybuf = T.ybuf; t_ybuf = T.t_ybuf
                tt("dve", ybuf[:].rearrange("p (h d) -> p h d", h=8),
                   bank(ko)[:, 0:512].rearrange("p (h d) -> p h d", h=8),
                   s1[:, 48:56].unsqueeze(2).to_broadcast([128, 8, 64]), ALU.mult, [tb[ko], t_s1], [t_ybuf])
                tt("dve", ybuf[:], ybuf[:], bank(kd)[:, 0:512], ALU.add, [t_ybuf, tb[kd]], [t_ybuf])
                tt("dve", S_f[:].rearrange("p (h d) -> p h d", h=8), S_f[:].rearrange("p (h d) -> p h d", h=8),
                   s2[:, 8:16].unsqueeze(2).to_broadcast([128, 8, 64]), ALU.mult, [t_Sf, t_s2], [t_Sf])
                tt("dve", S_f[:], S_f[:], bank(kcs)[:, 0:512], ALU.add, [t_Sf, tb[kcs]], [t_Sf])
                cp("act", S_b[:], S_f[:], [t_Sf], [t_Sb])
                done.add(("Sst", gc))
                yield H("pool", t_ybuf)
                ysn = T.ysn; t_ysn = T.t_ysn
                tt("pool", ybuf[:], ybuf[:], sz[par][:, i, :], ALU.mult, [t_ybuf, t_sz[par][i]], [t_ybuf])
                for g in range(2):
                    act(ysn[:, g * 256:(g + 1) * 256], ybuf[:, g * 256:(g + 1) * 256], AF.Square,
                        [t_ybuf], [t_ysn, t_s2], accum=s2[:, 24 + g:25 + g])
                ts("dve", s2[:, 26:28], s2[:, 24:26], 1.0 / 256, EPS, ALU.mult, ALU.add, [t_s2], [t_s2])
                tt("pool", s2[:, 30:32], s2[:, 26:28], mhalf[:, 0:2], ALU.pow, [t_s2, t_mhalf], [t_s2])
                if b > 0:
                    yield ("wait", lambda: ("O", b - 1) in done)
                yield H("dve", t_s2)
                for g in range(2):
                    stt(ysn[:, g * 256:(g + 1) * 256], ybuf[:, g * 256:(g + 1) * 256], s2[:, 30 + g:31 + g],
                        ssmw[:, g * 256:(g + 1) * 256], ALU.mult, ALU.mult, [t_ybuf, t_s2, t_ssmw], [t_ysn])
                ky = nb()
                for cc in range(4):
                    tr(bankb(ky)[:, cc * 128:(cc + 1) * 128], ysn[:, cc * 128:(cc + 1) * 128], identb[:],
                       [t_ysn, t_identb], [tb[ky]], cc == 3)
                cp("act", ycatT[:, 0:4, tsl], bankb(ky)[:, 0:512].rearrange("p (c t) -> p c t", c=4),
                   [tb[ky]], [t_ycatS[i]])
                yield H("dve", t_dtraw[par][i])

            def gen_A(b, j, T):
                par = b % 2
                qT_ = qT[par]; kT_ = kT[par]; v_ = vtok[par]
                Pexp = T.Pexp; t_Pexp = T.t_Pexp; PT = T.PT; t_PT = T.t_PT; yattn = T.yattn; t_yattn = T.t_yattn
                ga = 2 * b + j
                qsl = slice(j * 128, (j + 1) * 128)
                ksl = slice(j * 128, j * 128 + 256)
                msk = maskb[:, 256:512] if ga == 0 else maskb[:, 0:256]
                for g in range(2):
                    s3, t_s3 = T.nsm()
                    ka = nb2()
                    sc = ps[:, ka * 512:ka * 512 + 1024]
                    t_sc = [tb[ka], tb[ka + 1]]
                    for r in range(4):
                        hf = r % 2
                        psl = slice(hf * 64, (hf + 1) * 64)
                        kk = ka + r // 2
                        mm(sc[:, r * 256:(r + 1) * 256], qT_[psl, 2 * g + r // 2, qsl], kT_[psl, g, ksl], True, False,
                           [t_qT[par][2 * g + r // 2], t_kT[par][g]], [tb[kk]], False)
                        mm(sc[:, r * 256:(r + 1) * 256], identb[:], msk, False, True,
                           [t_identb, t_maskb], [tb[kk]], True)
                    op("dve", lambda: nc.vector.tensor_reduce(
                        out=s3[:, 0:4], in_=sc.rearrange("p (r k) -> p r k", r=4), axis=AX.X, op=ALU.max),
                        reads=t_sc, writes=[t_s3], cost=1.2)
                    stt(s3[:, 4:8], s3[:, 0:4], 0.125, sinks[:, g * 4:(g + 1) * 4], ALU.mult, ALU.max,
                        [t_s3, t_small], [t_s3])
                    ts("dve", s3[:, 8:12], s3[:, 4:8], -1.0, None, ALU.mult, None, [t_s3], [t_s3])
                    for r in range(4):
                        act(Pexp[:, r * 256:(r + 1) * 256], sc[:, r * 256:(r + 1) * 256], AF.Exp,
                            [tb[ka + r // 2], t_s3], [t_Pexp, t_s3], bias=s3[:, 8 + r:9 + r], scale=0.125,
                            accum=s3[:, 12 + r:13 + r])
                    yield H("dve", t_s3, t_Pexp)
                    tt("dve", s3[:, 16:20], sinks[:, g * 4:(g + 1) * 4], s3[:, 4:8], ALU.subtract,
                       [t_small, t_s3], [t_s3])
                    act(s3[:, 20:24], s3[:, 16:20], AF.Exp, [t_s3], [t_s3])
                    tt("dve", s3[:, 24:28], s3[:, 12:16], s3[:, 20:24], ALU.add, [t_s3], [t_s3])
                    op("dve", lambda: nc.vector.reciprocal(out=s3[:, 28:32], in_=s3[:, 24:28]),
                       reads=[t_s3], writes=[t_s3])
                    kt = nb()
                    for r in range(4):
                        for kk in range(2):
                            idx = r * 2 + kk
                            tr(bankb(kt)[:, idx * 128:(idx + 1) * 128],
                               Pexp[:, r * 256 + kk * 128:r * 256 + (kk + 1) * 128], identb[:],
                               [t_Pexp, t_identb], [tb[kt]], idx == 7)
                    cp("act", PT[:], bankb(kt)[:, 0:1024], [tb[kt]], [t_PT])
                    yield H("pe", t_PT, t_s3)
                    kv = nb()
                    for r in range(4):
                        o_ = bank(kv)[:, r * 64:(r + 1) * 64]
                        mm(o_, PT[:, (r * 2) * 128:(r * 2 + 1) * 128], v_[:, j, g * 64:(g + 1) * 64], True, False,
                           [t_PT, t_vtok[par]], [tb[kv]], False)
                        mm(o_, PT[:, (r * 2 + 1) * 128:(r * 2 + 2) * 128], v_[:, j + 1, g * 64:(g + 1) * 64], False, True,
                           [t_PT, t_vtok[par]], [tb[kv]], r == 3)
                    tt("dve", yattn[:, g * 256:(g + 1) * 256].rearrange("p (r d) -> p r d", r=4),
                       bank(kv)[:, 0:256].rearrange("p (r d) -> p r d", r=4),
                       s3[:, 28:32].unsqueeze(2).to_broadcast([128, 4, 64]), ALU.mult,
                       [tb[kv], t_s3], [t_yattn])
                    yield H("pe", t_yattn)
                if b > 0:
                    yield ("wait", lambda: ("O", b - 1) in done)
                ky = nb()
                for cc in range(4):
                    tr(bankb(ky)[:, cc * 128:(cc + 1) * 128], yattn[:, cc * 128:(cc + 1) * 128], identb[:],
                       [t_yattn, t_identb], [tb[ky]], cc == 3)
                cp("act", ycatT[:, 4:8, qsl], bankb(ky)[:, 0:512].rearrange("p (c t) -> p c t", c=4),
                   [tb[ky]], [t_ycatA[j]])
                yield H("pe", t_yattn)

            def emit_O(b):
                for i in range(2):
                    gt = 2 * b + i
                    for hf in range(2):
                        k = nb()
                        for kc in range(KC):
                            mm(bank(k)[:, 0:512], ycatT[:, kc, i * 128:(i + 1) * 128], Wout[:, kc, hf * 512:(hf + 1) * 512],
                               kc == 0, kc == KC - 1, [t_ycatS[i] if kc < 4 else t_ycatA[i], t_Wout], [tb[k]], kc == KC - 1)
                        tt("dve", X[:, gt, hf * 512:(hf + 1) * 512], X[:, gt, hf * 512:(hf + 1) * 512], bank(k)[:, 0:512],
                           ALU.add, [tX[gt], tb[k]], [tX[gt]])

            def chain(*gs):
                for g_ in gs:
                    yield from g_

            def all_done(b):
                return all(k_ in done for k_ in (("Sc", b, 0), ("Sc", b, 1), ("Ac", b, 0), ("Ac", b, 1)))

            def stream_NP():
                for b in range(nblk):
                    yield from gen_N(b)
                    if b >= 2:
                        yield ("wait", lambda b=b: all_done(b - 2))
                    yield from gen_P(b)
                    done.add(("P", b))

            def stream_S(i):
                for b in range(nblk):
                    yield ("wait", lambda b=b: ("P", b) in done)
                    yield from gen_S(b, i, ST[i])
                    done.add(("Sc", b, i))

            def stream_A(j):
                for b in range(nblk):
                    yield ("wait", lambda b=b: ("P", b) in done)
                    yield from gen_A(b, j, AT[j])
                    done.add(("Ac", b, j))

            def stream_O():
                for b in range(nblk):
                    yield ("wait", lambda b=b: all_done(b))
                    emit_O(b)
                    done.add(("O", b))
                    yield H("pe", t_ycatS[0])

            run_streams([stream_NP(), stream_S(0), stream_S(1), stream_A(0), stream_A(1), stream_O()])
            if do_ffn:
                c.dma("sp", wb[:], nw_ffn_d, writes=[t_wb])
            build_program.tmix = max(c.tfree.values())
            fence = c.fence_tokens()
        if do_ffn:
            with ExitStack() as fs:
                H2T = c.sb("H2T", [128, KC, L], BF16, fs)
                t_H2T = [Trk("H2T%d" % i, fence) for i in range(4)]
                actT = c.sb("actT", [128, JMAX, L], BF16, fs)
                t_actT = [[Trk("actT%d_%d" % (i, q), fence) for q in range(4)] for i in range(JMAX)]
                Wd = [c.sb("Wd%d" % i, [128, JMAX, D], BF16, fs) for i in range(2)]
                t_Wd = [Trk("Wd%d" % i, fence) for i in range(2)]
                xn2 = c.sb("xn2", [128, D], BF16, fs); t_xn2 = Trk("xn2", fence)
                sg = [c.sb("sg%d" % i, [128, 512], F32, fs) for i in range(2)]
                t_sg = [Trk("sg%d" % i, fence) for i in range(2)]
                ostg = [c.sb("ostg%d" % i, [128, D], F32, fs) for i in range(2)]
                t_ostg = [Trk("ostg%d" % i, fence) for i in range(2)]
                for i in range(1, NGU):
                    gu.append(c.sb("gu%d" % i, [128, 2, KC, 128], BF16, fs))
                    t_gu[i] = Trk("gu%d" % i, fence)

                for j in range(1, NGU):
                    gu_load(j)

                def wd_load(p):
                    j0, J = PASSES[p]
                    c.dma("pool", Wd[p % 2][:, 0:J, :], w_dn_d[:, j0:j0 + J, :], writes=[t_Wd[p % 2]])

                xn2b = c.sb("xn2b", [128, D], BF16, fs); t_xn2b = Trk("xn2b", fence)
                xn2s = [xn2, xn2b]
                t_xn2s = [t_xn2, t_xn2b]

                def norm2_A(gt):
                    xq = xn2s[gt % 2]; t_xq = t_xn2s[gt % 2]
                    s_, t_s = nsm_F()
                    act(xq[:], X[:, gt, :], AF.Square, [tX[gt]], [t_xq, t_s], accum=s_[:, 0:1])
                    ts("dve", s_[:, 1:2], s_[:, 0:1], 1.0 / D, EPS, ALU.mult, ALU.add, [t_s], [t_s])
                    tt("pool", s_[:, 3:4], s_[:, 1:2], mhalf[:, 0:1], ALU.pow, [t_s, t_mhalf], [t_s])
                    stt(xq[:], X[:, gt, :], s_[:, 3:4], wb[:], ALU.mult, ALU.mult, [tX[gt], t_s, t_wb], [t_xq])

                def norm2_B(gt):
                    xq = xn2s[gt % 2]; t_xq = t_xn2s[gt % 2]
                    k = nb()
                    for kc in range(KC):
                        tr(bankb(k)[:, kc * 128:(kc + 1) * 128], xq[:, kc * 128:(kc + 1) * 128], identb[:],
                           [t_xq, t_identb], [tb[k]], kc == KC - 1)
                    cp("act", H2T[:, :, gt * 128:(gt + 1) * 128],
                       bankb(k)[:, 0:1024].rearrange("p (k t) -> p k t", k=KC), [tb[k]], [t_H2T[gt // 4]])

                sgc = [0]

                def ffn_a(j, jj, tbk):
                    slot = j % NGU
                    tsl = slice(tbk * 512, (tbk + 1) * 512)
                    kg = nb()
                    for kc in range(KC):
                        mm(bank(kg), gu[slot][:, 0, kc, :], H2T[:, kc, tsl], kc == 0, kc == KC - 1,
                           [t_gu[slot], t_H2T[tbk]], [tb[kg]], kc == KC - 1)
                    ku = nb()
                    for kc in range(KC):
                        mm(bank(ku), gu[slot][:, 1, kc, :], H2T[:, kc, tsl], kc == 0, kc == KC - 1,
                           [t_gu[slot], t_H2T[tbk]], [tb[ku]], kc == KC - 1)
                    si = sgc[0] % 2
                    sgc[0] += 1
                    act(sg[si][:], bank(kg), AF.Silu, [tb[kg]], [t_sg[si]])
                    tt("dve", actT[:, jj, tsl], sg[si][:], bank(ku), ALU.mult, [t_sg[si], tb[ku]], [t_actT[jj][tbk]])

                def final_tile(gt):
                    s_, t_s = nsm_F()
                    o_ = ostg[gt % 2]; t_o = t_ostg[gt % 2]
                    act(o_[:], X[:, gt, :], AF.Square, [tX[gt]], [t_o, t_s], accum=s_[:, 0:1])
                    ts("dve", s_[:, 1:2], s_[:, 0:1], 1.0 / D, EPS, ALU.mult, ALU.add, [t_s], [t_s])
                    tt("pool", s_[:, 3:4], s_[:, 1:2], mhalf[:, 0:1], ALU.pow, [t_s, t_mhalf], [t_s])
                    stt(o_[:], X[:, gt, :], s_[:, 3:4], wb[:], ALU.mult, ALU.mult, [tX[gt], t_s, t_wb], [t_o])
                    c.dma("sp", out_d[gt * 128:(gt + 1) * 128, :], o_[:], reads=[t_o])

                wd_load(0)

                def stream_norm2():
                    norm2_A(0)
                    yield H("act", tX[1])
                    for gt in range(NT):
                        if gt + 1 < NT:
                            norm2_A(gt + 1)
                            yield H("pe", t_xn2s[gt % 2])
                        norm2_B(gt)
                        done.add(("H2T", gt))
                        yield H("act", tX[min(gt + 2, NT - 1)])

                def stream_ffn0():
                    j0, J = PASSES[0]
                    for tbk in range(4):
                        yield ("wait", lambda tbk=tbk: ("H2T", 4 * tbk + 3) in done)
                        for jj in range(J):
                            ffn_a(j0 + jj, jj, tbk)
                            if tbk == 3 and j0 + jj + NGU < NJ:
                                gu_load(j0 + jj + NGU)
                            yield H("pe", t_H2T[tbk])

                for p, (j0, J) in enumerate(PASSES):
                    if p + 1 < len(PASSES):
                        wd_load(p + 1)
                    last = p == len(PASSES) - 1

                    def ffn_b_tile(gt, p=p, J=J):
                        for hf in range(2):
                            k = nb()
                            for jj in range(J):
                                mm(bank(k), actT[:, jj, gt * 128:(gt + 1) * 128], Wd[p % 2][:, jj, hf * 512:(hf + 1) * 512],
                                   jj == 0, jj == J - 1, [t_actT[jj][gt // 4], t_Wd[p % 2]], [tb[k]], jj == J - 1)
                            tt("dve", X[:, gt, hf * 512:(hf + 1) * 512], X[:, gt, hf * 512:(hf + 1) * 512], bank(k),
                               ALU.add, [tX[gt], tb[k]], [tX[gt]])

                    if p == 0:
                        run_streams([stream_norm2(), stream_ffn0()])
                        c.dma("sp", wb[:], nw_fin_d, writes=[t_wb])
                        for gt in range(NT):
                            ffn_b_tile(gt)
                    elif not last:
                        for jj in range(J):
                            for tbk in range(4):
                                ffn_a(j0 + jj, jj, tbk)
                            if j0 + jj + NGU < NJ:
                                gu_load(j0 + jj + NGU)
                        for gt in range(NT):
                            ffn_b_tile(gt)
                    else:
                        for tbk in range(4):
                            for jj in range(J):
                                ffn_a(j0 + jj, jj, tbk)
                            for gt in range(4 * tbk, 4 * tbk + 4):
                                ffn_b_tile(gt)
                                final_tile(gt)
                c.wait_all("sp", t_ostg)
        else:
            for gt in range(NT):
                c.dma("sp", out_d[gt * 128:(gt + 1) * 128, :], X[:, gt, :], reads=[tX[gt]])
            c.wait_all("sp", tX)
        build_program.stats = (c.n_ins, c.n_wait, c.n_dsem, dict(c.cnt))
        build_program.tend = max(c.tfree.values())
        build_program.log = c.log
    return nc


def _consts():
    identf = np.eye(128, dtype=np.float32)
    u = np.arange(128)
    tri = (u[:, None] <= u[None, :]).astype(np.float32)
    nm = np.where(u[:, None] <= u[None, :], 0.0, NEG).astype(np.float32)
    negmask = np.tile(nm, (1, 4))
    q = u[:, None]
    kk = u[None, :]
    prev = np.where(kk > q, 0.0, NEG)
    cur = np.where(kk <= q, 0.0, NEG)
    mask = np.concatenate([prev, cur], axis=1)
    mask0 = np.concatenate([np.full((128, 128), NEG), cur], axis=1)
    maskb = np.concatenate([mask, mask0], axis=1).astype(np.float32)
    return identf, tri, negmask, maskb


def _prep_shared(inp):
    f = np.float32
    w_in = np.asarray(inp["w_in"], f)[0]
    w_in_l = np.ascontiguousarray(w_in.reshape(KC, 128, D_IN).transpose(1, 0, 2))
    kcols = w_in[:, 2056:2184]
    k2 = np.concatenate([kcols[:, 0:64], kcols[:, 0:64], kcols[:, 64:128], kcols[:, 64:128]], axis=1)
    w_k2_l = np.ascontiguousarray(k2.reshape(KC, 128, 256).transpose(1, 0, 2))
    w_out = np.asarray(inp["w_out"], f)[0]
    w_out_l = np.ascontiguousarray(w_out.reshape(KC, 128, D).transpose(1, 0, 2))
    wg = np.asarray(inp["w_gate"], f)[0].reshape(KC, 128, NJ, 128)
    wu = np.asarray(inp["w_up"], f)[0].reshape(KC, 128, NJ, 128)
    w_gu = np.ascontiguousarray(np.stack([wg, wu], axis=0).transpose(3, 2, 0, 1, 4))
    w_dn = np.asarray(inp["w_down"], f)[0]
    w_dn_l = np.ascontiguousarray(w_dn.reshape(NJ, 128, D).transpose(1, 0, 2))

    def bc(v, n):
        return np.ascontiguousarray(np.broadcast_to(np.asarray(v, f).reshape(1, n), (128, n)))

    convw = np.ascontiguousarray(np.asarray(inp["conv_w"], f)[0].T.reshape(8, 128, 4).transpose(1, 0, 2))
    convb = np.ascontiguousarray(np.asarray(inp["conv_b"], f)[0].reshape(8, 128).T)
    small = np.zeros((128, 48), f)
    small[:, 0:8] = convb
    small[:, 8:16] = bc(inp["dt_bias"][0], 8)
    small[:, 16:24] = bc(inp["a_log"][0], 8)
    small[:, 24:32] = bc(inp["d_skip"][0], 8)
    small[:, 32:40] = bc(inp["attn_sinks"][0], 8)
    identf, tri, negmask, maskb = _consts()
    return {
        "w_in": w_in_l, "w_k2": w_k2_l, "w_out": w_out_l, "w_gu": w_gu, "w_dn": w_dn_l,
        "nw_mix": bc(inp["norm_mix_w"][0], D), "nw_ffn": bc(inp["norm_ffn_w"][0], D),
        "nw_fin": bc(inp["norm_final_w"], D), "convw": convw, "smallp": small,
        "ssmw": bc(inp["ssm_norm_w"][0], 512), "identf": identf, "tri": tri,
        "negmask": negmask, "maskb": maskb,
    }


_NC_CACHE = {}


def kernel(**inputs):
    x = np.asarray(inputs["x"], np.float32)
    shared = _prep_shared(inputs)
    if "nc" not in _NC_CACHE:
        _NC_CACHE["nc"] = build_program()
    nc = _NC_CACHE["nc"]
    in_maps = []
    for b in range(8):
        m = dict(shared)
        m["x"] = np.ascontiguousarray(x[b])
        in_maps.append(m)
    res = run_bass_kernel_spmd(nc, in_maps, core_ids=list(range(8)))
    out = np.stack([np.asarray(r["out"], np.float32) for r in res.results], axis=0)
    return out
```

```python
from contextlib import ExitStack

import numpy as np
import concourse.bass as bass
import concourse.mybir as mybir
from concourse.bass_utils import run_bass_kernel_spmd

F32 = mybir.dt.float32
BF16 = mybir.dt.bfloat16
AF = mybir.ActivationFunctionType
ALU = mybir.AluOpType
AX = mybir.AxisListType

SAME_ENGINE_SYNC = True

L = 2048
D = 1024
NT = 16
KC = 8
TB = 256
NBLK = L // TB
D_IN = 2312
D_FF = 2816
NJ = D_FF // 128
PASSES = [(0, 6), (6, 6), (12, 5), (17, 5)]
JMAX = 6
EPS = 1e-5
NEG = -30000.0


class Trk:
    __slots__ = ("name", "w", "r", "dsem", "dcnt", "excl")

    def __init__(self, name="", fence=None, excl=False):
        self.name = name
        self.excl = excl
        self.w = None
        self.r = list(fence) if fence else []
        self.dsem = None
        self.dcnt = 0


class Ctx:
    def __init__(self, nc, stack):
        self.nc = nc
        self.stack = stack
        self.eng = {"pe": nc.tensor, "act": nc.scalar, "dve": nc.vector,
                    "pool": nc.gpsimd, "sp": nc.sync}
        self.sem = {}
        self.cnt = {}
        self.seen = {}
        for k in self.eng:
            self.sem[k] = stack.enter_context(nc.semaphore("s_" + k))
            self.cnt[k] = 0
            self.seen[k] = {}
        self.n_dsem = 0
        self.n_wait = 0
        self.n_ins = 0
        self.tfree = {k: 0.0 for k in self.eng}
        self.ttok = {}
        self.step_fin = 0.0
        self.log = None
        self.tag = ""
        self.tokinfo = {}
        self.crit = None

    def _tdeps(self, e, deps):
        t = 0.0
        self.crit = None
        for d in deps:
            if d is None:
                continue
            lat = 0.05 if d[0] is self.sem.get(e) else 0.25
            td = self.ttok.get((id(d[0]), d[1]), 0.0) + lat
            if td > t:
                t = td
                self.crit = self.tokinfo.get((id(d[0]), d[1]))
        return t

    def sb(self, name, shape, dtype, stack=None):
        return (stack or self.stack).enter_context(
            self.nc.sbuf_tensor("sb_" + name, list(shape), dtype))

    def _wait(self, e, deps):
        h = self.eng[e]
        best = {}
        for d in deps:
            if d is None:
                continue
            s, v = d
            if s is self.sem.get(e):
                if e == "pe" or not SAME_ENGINE_SYNC or v > self.cnt[e]:
                    continue
            k = id(s)
            if k not in best or best[k][1] < v:
                best[k] = (s, v)
        for k, (s, v) in best.items():
            if self.seen[e].get(k, 0) >= v:
                continue
            h.wait_ge(s, v)
            self.n_wait += 1
            self.seen[e][k] = v

    def op(self, e, fn, reads=(), writes=(), signal=True, cost=0.1):
        deps = []
        for t in reads:
            deps.append(t.w)
            if t.excl:
                deps.extend(r for r in t.r if r[0] is not self.sem[e])
        for t in writes:
            deps.append(t.w)
            deps.extend(t.r)
        self._wait(e, deps)
        ins = fn()
        self.n_ins += 1
        self.tfree_prev = self.tfree[e]
        start = max(self.tfree[e], self._tdeps(e, deps))
        fin = start + cost
        self.tfree[e] = fin
        if signal:
            self.cnt[e] += 1
            ins.then_inc(self.sem[e], 1)
            tok = (self.sem[e], self.cnt[e])
        else:
            tok = (self.sem[e], self.cnt[e] + 1)
        self.ttok[(id(tok[0]), tok[1])] = fin + (0.1 if e == "pe" else 0.0)
        self.step_fin = max(self.step_fin, fin)
        if self.log is not None:
            try:
                nm = str(ins.ins.name)
                self.tokinfo[(id(tok[0]), tok[1])] = (nm, self.tag, e)
                self.log.append((nm, e, self.tag, start, fin, self.crit, self.tfree_prev))
            except Exception:
                pass
        for t in reads:
            t.r.append(tok)
        for t in writes:
            t.w = tok
            t.r = []
        return ins

    def dma(self, q, out, in_, reads=(), writes=(), owner=None, join=False, nbytes=65536):
        own = owner or (writes[0] if writes else reads[0])
        if own.dsem is None:
            own.dsem = self.stack.enter_context(self.nc.semaphore("d%d" % self.n_dsem))
            self.n_dsem += 1
        deps = []
        for t in reads:
            deps.append(t.w)
        for t in writes:
            if not (join and t.w is not None and t.w[0] is own.dsem):
                deps.append(t.w)
            deps.extend(t.r)
        self._wait(q, deps)
        own.dcnt += 16
        self.eng[q].dma_start(out=out, in_=in_).then_inc(own.dsem, 16)
        self.n_ins += 1
        tok = (own.dsem, own.dcnt)
        start = max(self.tfree[q], self._tdeps(q, deps))
        self.tfree[q] = start + (1.0 if q == "pool" else 0.1)
        self.ttok[(id(tok[0]), tok[1])] = start + 2.5 + nbytes / 2.0e5
        for t in reads:
            t.r.append(tok)
        for t in writes:
            t.w = tok
            t.r = []

    def wait_all(self, e, trks):
        deps = []
        for t in trks:
            deps.append(t.w)
            deps.extend(t.r)
        self._wait(e, deps)

    def fence_tokens(self):
        toks = []
        for e in ("pe", "act", "dve", "pool"):
            if self.cnt[e] > 0:
                toks.append((self.sem[e], self.cnt[e]))
        return toks


def build_program(nblk=NBLK, do_ffn=True, debug=()):
    nc = bass.Bass("TRN2", target_bir_lowering=False)

    def din(name, shape):
        return nc.dram_tensor(name, list(shape), F32, kind="ExternalInput").ap()

    x_d = din("x", [L, D])
    w_in_d = din("w_in", [128, KC, D_IN])
    w_k2_d = din("w_k2", [128, KC, 256])
    w_out_d = din("w_out", [128, KC, D])
    w_gu_d = din("w_gu", [NJ, 128, 2, KC, 128])
    w_dn_d = din("w_dn", [128, NJ, D])
    nw_mix_d = din("nw_mix", [128, D])
    nw_ffn_d = din("nw_ffn", [128, D])
    nw_fin_d = din("nw_fin", [128, D])
    convw_d = din("convw", [128, 8, 4])
    sm_d = din("smallp", [128, 48])
    ssmw_d = din("ssmw", [128, 512])
    identf_d = din("identf", [128, 128])
    tri_d = din("tri", [128, 128])
    negmask_d = din("negmask", [128, 512])
    maskb_d = din("maskb", [128, 512])
    out_d = nc.dram_tensor("out", [L, D], F32, kind="ExternalOutput").ap()
    dbg_outs = {}

    with ExitStack() as st:
        c = Ctx(nc, st)
        op = c.op
        if debug:
            c.log = []

        ps = st.enter_context(nc.psum_tensor("ps", [128, 4096], F32))
        psb = ps.bitcast(BF16) if hasattr(ps, "bitcast") else None
        tb = [Trk("bank%d" % k, excl=True) for k in range(8)]
        bank_ctr = [0]

        def bank(k):
            return ps[:, k * 512:(k + 1) * 512]

        def bankb(k):
            return psb[:, k * 1024:(k + 1) * 1024]

        def nb():
            k = bank_ctr[0] % 8
            bank_ctr[0] += 1
            return k

        def nb2():
            if bank_ctr[0] % 2:
                bank_ctr[0] += 1
            k = bank_ctr[0] % 8
            bank_ctr[0] += 2
            return k

        def nfree(ap):
            n = 1
            for d in ap.shape[1:]:
                n *= int(d)
            return n

        def mm(out, lhsT, rhs, start, stop, reads, writes, signal):
            n = max(nfree(rhs), 64)
            cst = n / 2400.0 * (4.0 if rhs.dtype == F32 else 1.0) + 0.01
            op("pe", lambda: nc.tensor.matmul(out, lhsT=lhsT, rhs=rhs, start=start, stop=stop),
               reads=reads, writes=writes, signal=signal, cost=cst)

        def tr(out, in_, ident, reads, writes, signal):
            op("pe", lambda: nc.tensor.transpose(out, in_, ident),
               reads=reads, writes=writes, signal=signal, cost=0.07)

        def act(out, in_, func, reads, writes, bias=None, scale=None, accum=None):
            kw = {}
            if bias is not None:
                kw["bias"] = bias
            if scale is not None:
                kw["scale"] = scale
            if accum is not None:
                kw["accum_out"] = accum
            op("act", lambda: nc.scalar.activation(out=out, in_=in_, func=func, **kw),
               reads=reads, writes=writes, cost=0.12 + nfree(out) / 1200.0 + (0.1 if accum is not None else 0.0))

        def ecost(e, out, mult=1.0):
            n = nfree(out)
            if e == "dve":
                return 0.07 + mult * n / 960.0
            return 1.0 + n / 400.0

        def tt(e, out, in0, in1, aop, reads, writes):
            h = nc.vector if e == "dve" else nc.gpsimd
            op(e, lambda: h.tensor_tensor(out=out, in0=in0, in1=in1, op=aop),
               reads=reads, writes=writes, cost=ecost(e, out))

        def ts(e, out, in0, s1, s2, op0, op1, reads, writes):
            h = nc.vector if e == "dve" else nc.gpsimd
            if op1 is None:
                op(e, lambda: h.tensor_scalar(out=out, in0=in0, scalar1=s1, scalar2=None, op0=op0),
                   reads=reads, writes=writes, cost=ecost(e, out))
            else:
                op(e, lambda: h.tensor_scalar(out=out, in0=in0, scalar1=s1, scalar2=s2, op0=op0, op1=op1),
                   reads=reads, writes=writes, cost=ecost(e, out))

        def stt(out, in0, scalar, in1, op0, op1, reads, writes):
            op("dve", lambda: nc.vector.scalar_tensor_tensor(out=out, in0=in0, scalar=scalar, in1=in1,
                                                             op0=op0, op1=op1),
               reads=reads, writes=writes, cost=ecost("dve", out))

        def cp(e, out, in_, reads, writes):
            if e == "act":
                op("act", lambda: nc.scalar.copy(out=out, in_=in_), reads=reads, writes=writes,
                   cost=0.12 + nfree(out) / 1200.0)
            else:
                h = nc.vector if e == "dve" else nc.gpsimd
                op(e, lambda: h.tensor_copy(out=out, in_=in_), reads=reads, writes=writes, cost=ecost(e, out))

        def dbg(name, ap, trk, shape):
            if name not in debug:
                return
            d = nc.dram_tensor("dbg_" + name, list(shape), F32, kind="ExternalOutput").ap()
            stg = c.sb("dbgs_" + name, list(shape), F32)
            t = Trk()
            cp("dve", stg[:], ap, [trk], [t])
            c.dma("sp", d, stg[:], reads=[t])
            dbg_outs[name] = t

        X = c.sb("X", [128, NT, D], F32)
        tX = [Trk("X%d" % i) for i in range(NT)]
        identb = c.sb("identb", [128, 128], BF16); t_identb = Trk()
        tri = c.sb("tri", [128, 128], F32); t_tri = Trk()
        negmask = c.sb("negmask", [128, 512], BF16); t_negmask = Trk()
        maskb = c.sb("maskb", [128, 512], BF16); t_maskb = Trk()
        convw = c.sb("convw", [128, 8, 4], F32); t_convw = Trk()
        smallp = c.sb("smallp", [128, 48], F32); t_small = Trk()
        a_b = c.sb("a_b", [128, 8], F32); t_ab = Trk()
        ssmw = c.sb("ssmw", [128, 512], F32); t_ssmw = Trk()
        mhalf = c.sb("mhalf", [128, 2], F32); t_mhalf = Trk()
        wb = c.sb("wb", [128, D], F32); t_wb = Trk()
        NGU = 6
        gu = [c.sb("gu%d" % i, [128, 2, KC, 128], BF16) for i in range(1)]
        t_gu = [Trk("gu%d" % i) for i in range(NGU)]
        convb = smallp[:, 0:8]
        dtb = smallp[:, 8:16]
        alog = smallp[:, 16:24]
        dskip = smallp[:, 24:32]
        sinks = smallp[:, 32:40]

        def stat_tiles(name, n, w=64, stack=None):
            tiles = [c.sb("%s%d" % (name, i), [128, w], F32, stack) for i in range(n)]
            trks = [Trk("%s%d" % (name, i)) for i in range(n)]
            ctr = [0]

            def nxt():
                k = ctr[0] % n
                ctr[0] += 1
                return tiles[k], trks[k]
            return nxt

        nsm_N = stat_tiles("stN", 2, 8)
        nsm_F = stat_tiles("stF", 2, 8)

        def gu_load(j):
            slot = j % NGU
            c.dma("pool", gu[slot][:], w_gu_d[j], writes=[t_gu[slot]])

        done = set()

        def H(eng, *trks):
            return ("h", eng, list(trks))

        def vt_of(s_):
            h_ = s_["hint"]
            if h_ is None:
                return s_["vt"]
            t_ = c.tfree[h_[0]]
            for q_ in h_[1]:
                if q_.w is not None:
                    t_ = max(t_, c.ttok.get((id(q_.w[0]), q_.w[1]), 0.0))
            return t_

        def run_streams(gens):
            sts = [{"g": g_, "vt": 0.0, "blk": None, "nm": getattr(g_, "__name__", "?"), "k": 0, "hint": None} for g_ in gens]
            while sts:
                progressed = False
                for s_ in sorted(sts, key=vt_of):
                    if s_["blk"] is not None:
                        if not s_["blk"]():
                            continue
                        s_["blk"] = None
                    c.step_fin = 0.0
                    s_["k"] += 1
                    c.tag = "%s#%d" % (s_["nm"], s_["k"])
                    try:
                        r_ = next(s_["g"])
                    except StopIteration:
                        sts.remove(s_)
                        progressed = True
                        break
                    if isinstance(r_, tuple) and r_[0] == "wait":
                        if not r_[1]():
                            s_["blk"] = r_[1]
                    elif isinstance(r_, tuple) and r_[0] == "h":
                        s_["hint"] = (r_[1], r_[2])
                        s_["vt"] = max(s_["vt"], c.step_fin)
                    else:
                        s_["hint"] = None
                        s_["vt"] = max(s_["vt"], c.step_fin)
                    progressed = True
                    break
                if not progressed:
                    raise RuntimeError("emission deadlock")

        c.dma("sp", X[:, 0, :], x_d[0:128, :], writes=[tX[0]])
        c.dma("sp", X[:, 1, :], x_d[128:256, :], writes=[tX[1]])
        c.dma("sp", smallp[:], sm_d, writes=[t_small])
        c.dma("sp", convw[:], convw_d, writes=[t_convw])
        c.dma("sp", tri[:], tri_d, writes=[t_tri])
        c.dma("sp", ssmw[:], ssmw_d, writes=[t_ssmw])
        c.dma("pool", identb[:], identf_d, writes=[t_identb])
        c.dma("pool", negmask[:], negmask_d, writes=[t_negmask])
        op("pool", lambda: nc.gpsimd.memset(mhalf[:], -0.5), writes=[t_mhalf])

        with ExitStack() as ms:
            Win = c.sb("Win", [128, KC, D_IN], BF16, ms)
            t_Wxbc = Trk("Wxbc"); t_Wq = Trk("Wq"); t_Wz = Trk("Wz"); t_Wdt = t_Wq; t_Wv = t_Wq
            Wk2 = c.sb("Wk2", [128, KC, 256], BF16, ms); t_Wk2 = Trk()
            Wout = c.sb("Wout", [128, KC, D], BF16, ms); t_Wout = Trk("Wout")
            hT1 = c.sb("hT", [128, KC, TB], BF16, ms)
            t_hT1 = Trk("hT")
            hT = [hT1, hT1]
            t_hT = [t_hT1, t_hT1]
            xn = c.sb("xn", [128, D], BF16, ms); t_xn = Trk()
            pre2t = c.sb("pre2", [128, 2, TB + 3], F32, ms)
            pre2 = [pre2t[:, 0, :], pre2t[:, 1, :]]
            t_pre2 = [Trk() for _ in range(2)]
            halo = c.sb("halo", [128, 8, 3], F32, ms)
            t_halo = [Trk() for _ in range(8)]
            acct = c.sb("acc", [128, 2, TB], F32, ms)
            acc = [acct[:, 0, :], acct[:, 1, :]]
            t_acc = [Trk() for _ in range(2)]
            sgtt = c.sb("sgt", [128, 2, TB], F32, ms)
            t_sgt1 = Trk()
            post = [c.sb("post%d" % i, [128, 8, TB], BF16, ms) for i in range(2)]
            t_post = [[Trk() for _ in range(8)] for _ in range(2)]
            qT = [c.sb("qT%d" % i, [128, 4, TB], BF16, ms) for i in range(2)]
            t_qT = [[Trk() for _ in range(4)] for _ in range(2)]
            kT = [c.sb("kT%d" % i, [128, 2, 128 + TB], BF16, ms) for i in range(2)]
            t_kT = [[Trk() for _ in range(2)] for _ in range(2)]
            vtok = [c.sb("vtok%d" % i, [128, 3, 128], BF16, ms) for i in range(2)]
            t_vtok = [Trk() for _ in range(2)]
            sz = [c.sb("sz%d" % i, [128, 2, 512], F32, ms) for i in range(2)]
            t_sz = [[Trk() for _ in range(2)] for _ in range(2)]
            dtraw = [c.sb("dtraw%d" % i, [128, 2, 8], F32, ms) for i in range(2)]
            t_dtraw = [[Trk() for _ in range(2)] for _ in range(2)]
            class NS:
                pass

            def Sset(q):
                T = NS()
                T.cbs = c.sb("cbs%d" % q, [128, 256], F32, ms); T.t_cbs = Trk()
                T.MT = c.sb("MT%d" % q, [128, 1024], BF16, ms); T.t_MT = Trk()
                T.xdt = c.sb("xdt%d" % q, [128, 512], BF16, ms); T.t_xdt = Trk()
                T.xdte = c.sb("xdte%d" % q, [128, 512], BF16, ms); T.t_xdte = Trk()
                T.xsD = c.sb("xsD%d" % q, [128, 512], BF16, ms); T.t_xsD = Trk()
                T.Btok = c.sb("Btok%d" % q, [128, 256], BF16, ms); T.t_Btok = Trk()
                T.ybuf = c.sb("ybuf%d" % q, [128, 512], F32, ms); T.t_ybuf = Trk()
                T.ysn = c.sb("ysn%d" % q, [128, 512], BF16, ms); T.t_ysn = Trk()
                T.nsm1 = stat_tiles("stS1_%d" % q, 1, 64, ms)
                T.nsm2 = stat_tiles("stS2_%d" % q, 1, 32, ms)
                return T

            def Aset(q):
                T = NS()
                T.Pexp = c.sb("Pexp%d" % q, [128, 1024], BF16, ms); T.t_Pexp = Trk()
                T.PT = c.sb("PT%d" % q, [128, 1024], BF16, ms); T.t_PT = Trk()
                T.yattn = c.sb("yattn%d" % q, [128, 512], BF16, ms); T.t_yattn = Trk()
                T.nsm = stat_tiles("stA_%d" % q, 1, 32, ms)
                return T

            ST = [Sset(0), Sset(1)]
            AT = [Aset(0), Aset(1)]
            S_f = c.sb("S_f", [128, 512], F32, ms); t_Sf = Trk()
            S_b = c.sb("S_b", [128, 512], BF16, ms); t_Sb = Trk()
            ycatT = c.sb("ycatT", [128, 8, TB], BF16, ms)
            t_ycatS = [Trk() for _ in range(2)]
            t_ycatA = [Trk() for _ in range(2)]

            c.dma("sp", wb[:], nw_mix_d, writes=[t_wb])
            for kc in range(KC):
                c.dma("pool", Win[:, kc, 512:1536], w_in_d[:, kc, 512:1536], writes=[t_Wxbc], join=True)
            for kc in range(KC):
                c.dma("pool", Win[:, kc, 1536:2312], w_in_d[:, kc, 1536:2312], writes=[t_Wq], join=True)
            c.dma("pool", Wk2[:], w_k2_d, writes=[t_Wk2])
            c.dma("pool", maskb[:], maskb_d, writes=[t_maskb])
            for kc in range(KC):
                c.dma("pool", Win[:, kc, 0:512], w_in_d[:, kc, 0:512], writes=[t_Wz], join=True)
            c.wait_all("sp", [t_Wxbc])
            for i in range(2, NT):
                c.dma("sp", X[:, i, :], x_d[i * 128:(i + 1) * 128, :], writes=[tX[i]])
            for kc in range(KC):
                c.dma("pool", Wout[:, kc, :], w_out_d[:, kc, :], writes=[t_Wout], join=True)
            if do_ffn:
                gu_load(0)

            act(a_b[:], alog, AF.Exp, [t_small], [t_ab])
            ts("dve", a_b[:], a_b[:], -1.0, None, ALU.mult, None, [t_ab], [t_ab])
            op("dve", lambda: nc.vector.memset(S_f[:], 0.0), writes=[t_Sf])
            op("dve", lambda: nc.vector.memset(S_b[:], 0.0), writes=[t_Sb])
            op("pool", lambda: nc.gpsimd.memset(halo[:], 0.0), writes=t_halo)
            for i in range(2):
                op("pool", lambda: nc.gpsimd.memset(kT[i][:], 0.0), writes=t_kT[i])
                op("pool", lambda: nc.gpsimd.memset(vtok[i][:], 0.0), writes=[t_vtok[i]])

            def rms_stats(src_ap, t_src, n, junk_ap, t_jk, nsm):
                s_, t_s = nsm()
                act(junk_ap, src_ap, AF.Square, [t_src], [t_jk, t_s], accum=s_[:, 0:1])
                ts("dve", s_[:, 1:2], s_[:, 0:1], 1.0 / n, EPS, ALU.mult, ALU.add, [t_s], [t_s])
                tt("pool", s_[:, 3:4], s_[:, 1:2], mhalf[:, 0:1], ALU.pow, [t_s, t_mhalf], [t_s])
                return s_, t_s

            def gen_N(b):
                s = b % 2
                for i in range(2):
                    gt = 2 * b + i
                    s_, t_s = rms_stats(X[:, gt, :], tX[gt], D, xn[:], t_xn, nsm_N)
                    yield H("dve", t_s)
                    stt(xn[:], X[:, gt, :], s_[:, 3:4], wb[:], ALU.mult, ALU.mult,
                        [tX[gt], t_s, t_wb], [t_xn])
                    k = nb()
                    for kc in range(KC):
                        tr(bankb(k)[:, kc * 128:(kc + 1) * 128], xn[:, kc * 128:(kc + 1) * 128], identb[:],
                           [t_xn, t_identb], [tb[k]], kc == KC - 1)
                    cp("act", hT[s][:, :, i * 128:(i + 1) * 128],
                       bankb(k)[:, 0:1024].rearrange("p (k t) -> p k t", k=KC), [tb[k]], [t_hT[s]])
                    yield H("act", tX[min(2 * b + i + 1, NT - 1)])

            def gen_P(b):
                s = b % 2
                o = 1 - s
                for pr in range(4):
                    ocs = (2 * pr, 2 * pr + 1)
                    ka_ = nb2()
                    for q_, oc in enumerate(ocs):
                        k = ka_ + q_
                        c0 = 512 + oc * 128
                        for kc in range(KC):
                            mm(bank(k)[:, 0:TB], Win[:, kc, c0:c0 + 128], hT[s][:, kc, :], kc == 0, kc == KC - 1,
                               [t_Wxbc, t_hT[s]], [tb[k]], kc == KC - 1)
                    for q_, oc in enumerate(ocs):
                        cp("dve", pre2[q_][:, 0:3], halo[:, oc, :], [t_halo[oc]], [t_pre2[q_]])
                    cp("act", pre2t[:, :, 3:TB + 3],
                       ps[:, ka_ * 512:ka_ * 512 + 1024].rearrange("p (b c) -> p b c", b=2)[:, :, 0:TB],
                       [tb[ka_], tb[ka_ + 1]], [t_pre2[0], t_pre2[1]])
                    for q_, oc in enumerate(ocs):
                        cp("dve", halo[:, oc, :], pre2[q_][:, TB:TB + 3], [t_pre2[q_]], [t_halo[oc]])
                    for q_, oc in enumerate(ocs):
                        ts("dve", acc[q_], pre2[q_][:, 0:TB], convw[:, oc, 0:1], convb[:, oc:oc + 1], ALU.mult, ALU.add,
                           [t_pre2[q_], t_convw, t_small], [t_acc[q_]])
                    for kk in range(1, 4):
                        for q_, oc in enumerate(ocs):
                            stt(acc[q_], pre2[q_][:, kk:kk + TB], convw[:, oc, kk:kk + 1], acc[q_], ALU.mult, ALU.add,
                                [t_pre2[q_], t_convw, t_acc[q_]], [t_acc[q_]])
                    act(sgtt[:], acct[:], AF.Exp, t_acc, [t_sgt1], scale=-1.0)
                    act(sgtt[:], sgtt[:], AF.Ln, [t_sgt1], [t_sgt1], bias=1.0)
                    act(sgtt[:], sgtt[:], AF.Exp, [t_sgt1], [t_sgt1], scale=-1.0)
                    tt("dve", post[s][:, 2 * pr:2 * pr + 2, :], acct[:], sgtt[:], ALU.mult, t_acc + [t_sgt1],
                       [t_post[s][ocs[0]], t_post[s][ocs[1]]])
                    yield H("pe", t_hT[s])
                for oc in range(4):
                    k = nb()
                    c0 = 1544 + oc * 128
                    for kc in range(KC):
                        mm(bank(k)[:, 0:TB], Win[:, kc, c0:c0 + 128], hT[s][:, kc, :], kc == 0, kc == KC - 1,
                           [t_Wq, t_hT[s]], [tb[k]], kc == KC - 1)
                    cp("act", qT[s][:, oc, :], bank(k)[:, 0:TB], [tb[k]], [t_qT[s][oc]])
                    if oc % 2:
                        yield H("pe", t_hT[s])
                for g in range(2):
                    k = nb()
                    for kc in range(KC):
                        mm(bank(k)[:, 0:TB], Wk2[:, kc, g * 128:(g + 1) * 128], hT[s][:, kc, :], kc == 0, kc == KC - 1,
                           [t_Wk2, t_hT[s]], [tb[k]], kc == KC - 1)
                    if b > 0:
                        cp("dve", kT[s][:, g, 0:128], kT[o][:, g, TB:TB + 128], [t_kT[o][g]], [t_kT[s][g]])
                    cp("act", kT[s][:, g, 128:128 + TB], bank(k)[:, 0:TB], [tb[k]], [t_kT[s][g]])
                if b > 0:
                    cp("dve", vtok[s][:, 0, :], vtok[o][:, 2, :], [t_vtok[o]], [t_vtok[s]])
                yield H("pe", t_hT[s])
                for i in range(2):
                    k = nb()
                    for kc in range(KC):
                        mm(bank(k)[:, 0:512], hT[s][:, kc, i * 128:(i + 1) * 128], Win[:, kc, 0:512], kc == 0, kc == KC - 1,
                           [t_Wz, t_hT[s]], [tb[k]], kc == KC - 1)
                    act(sz[s][:, i, :], bank(k)[:, 0:512], AF.Exp, [tb[k]], [t_sz[s][i]], scale=-1.0)
                    act(sz[s][:, i, :], sz[s][:, i, :], AF.Ln, [t_sz[s][i]], [t_sz[s][i]], bias=1.0)
                    act(sz[s][:, i, :], sz[s][:, i, :], AF.Exp, [t_sz[s][i]], [t_sz[s][i]], scale=-1.0)
                    tt("dve", sz[s][:, i, :], sz[s][:, i, :], bank(k)[:, 0:512], ALU.mult, [t_sz[s][i], tb[k]], [t_sz[s][i]])
                    yield H("pe", t_hT[s])
                    k = nb()
                    for kc in range(KC):
                        mm(bank(k)[:, 0:8], hT[s][:, kc, i * 128:(i + 1) * 128], Win[:, kc, 1536:1544], kc == 0, kc == KC - 1,
                           [t_Wdt, t_hT[s]], [tb[k]], False)
                    for kc in range(KC):
                        mm(bank(k)[:, 128:256], hT[s][:, kc, i * 128:(i + 1) * 128], Win[:, kc, 2184:2312], kc == 0, kc == KC - 1,
                           [t_Wv, t_hT[s]], [tb[k]], kc == KC - 1)
                    tt("dve", dtraw[s][:, i, :], bank(k)[:, 0:8], dtb, ALU.add, [tb[k], t_small], [t_dtraw[s][i]])
                    cp("dve", vtok[s][:, 1 + i, :], bank(k)[:, 128:256], [tb[k]], [t_vtok[s]])
                    yield H("pe", t_hT[s])

            def gen_S(b, i, T):
                par = b % 2
                post_ = post[par]
                tp_ = t_post[par]
                gc = 2 * b + i
                tsl = slice(i * 128, (i + 1) * 128)
                s1, t_s1 = T.nsm1()
                s2, t_s2 = T.nsm2()
                MT = T.MT; t_MT = T.t_MT
                ts("dve", s1[:, 0:8], dtraw[par][:, i, :], 60.0, None, ALU.min, None, [t_dtraw[par][i]], [t_s1])
                act(s1[:, 8:16], s1[:, 0:8], AF.Exp, [t_s1], [t_s1])
                act(s1[:, 16:24], s1[:, 8:16], AF.Ln, [t_s1], [t_s1], bias=1.0)
                tt("dve", s1[:, 24:32], s1[:, 16:24], a_b[:], ALU.mult, [t_s1, t_ab], [t_s1])
                dt_ = s1[:, 16:24]
                dA = s1[:, 24:32]
                kcb = nb()
                for g in range(2):
                    mm(bank(kcb)[:, g * 128:(g + 1) * 128], post_[:, 4 + g, tsl], post_[:, 6 + g, tsl], True, True,
                       [tp_[4 + g], tp_[6 + g]], [tb[kcb]], g == 1)
                cp("act", T.cbs[:], bank(kcb)[:, 0:256], [tb[kcb]], [T.t_cbs])
                yield H("pe", t_s1)
                ka = nb2()
                for hf in range(2):
                    for hh in range(4):
                        h = hf * 4 + hh
                        mm(bank(ka + hf)[:, hh * 128:(hh + 1) * 128], s1[:, 24 + h:25 + h].to_broadcast([128, 128]), tri[:],
                           hh == 0, False, [t_tri, t_s1], [tb[ka + hf]], False)
                    mm(bank(ka + hf), identb[:], negmask[:], False, True,
                       [t_identb, t_negmask], [tb[ka + hf]], True)
                Ab = ps[:, ka * 512:ka * 512 + 1024].rearrange("p (h t) -> p h t", t=128)
                Alast = Ab[:, :, 127:128].rearrange("p h o -> p (h o)")
                t_A = [tb[ka], tb[ka + 1]]
                kcu = nb()
                mm(bank(kcu)[:, 0:8], tri[:], dA, True, True, [t_tri, t_s1], [tb[kcu]], True)
                cp("dve", s1[:, 32:40], bank(kcu)[:, 0:8], [tb[kcu]], [t_s1])
                ts("dve", s1[:, 40:48], bank(kcu)[:, 0:8], -1.0, None, ALU.mult, None, [tb[kcu]], [t_s1])
                act(s1[:, 48:56], bank(kcu)[:, 0:8], AF.Exp, [tb[kcu]], [t_s1])
                tt("dve", s1[:, 56:64], Alast, s1[:, 32:40], ALU.subtract, t_A + [t_s1], [t_s1])
                act(s2[:, 0:8], s1[:, 56:64], AF.Exp, [t_s1], [t_s2])
                act(s2[:, 8:16], Alast, AF.Exp, t_A, [t_s2])
                tt("dve", s2[:, 16:24], dt_, s2[:, 0:8], ALU.mult, [t_s1, t_s2], [t_s2])
                for h in range(8):
                    tbh = tb[ka + h // 4]
                    act(Ab[:, h, :], Ab[:, h, :], AF.Exp, [tbh, t_s1], [tbh], bias=s1[:, 40 + h:41 + h])
                for g in range(2):
                    tt("dve", MT[:, g * 512:(g + 1) * 512].rearrange("p (r t) -> p r t", r=4),
                       Ab[:, 4 * g:4 * g + 4, :],
                       T.cbs[:, g * 128:(g + 1) * 128].unsqueeze(1).to_broadcast([128, 4, 128]),
                       ALU.mult, [tb[ka + g], T.t_cbs], [t_MT])
                yield H("pe", t_s1, t_s2)
                kx = nb()
                for cc in range(4):
                    tr(bankb(kx)[:, cc * 128:(cc + 1) * 128], post_[:, cc, tsl], identb[:],
                       [tp_[cc], t_identb], [tb[kx]], False)
                for g in range(2):
                    tr(bankb(kx)[:, 512 + g * 128:512 + (g + 1) * 128], post_[:, 4 + g, tsl], identb[:],
                       [tp_[4 + g], t_identb], [tb[kx]], g == 1)
                xs_ps = bankb(kx)[:, 0:512].rearrange("p (h d) -> p h d", h=8)
                tt("dve", T.xdt[:].rearrange("p (h d) -> p h d", h=8), xs_ps,
                   dt_.unsqueeze(2).to_broadcast([128, 8, 64]), ALU.mult, [tb[kx], t_s1], [T.t_xdt])
                tt("dve", T.xdte[:].rearrange("p (h d) -> p h d", h=8), xs_ps,
                   s2[:, 16:24].unsqueeze(2).to_broadcast([128, 8, 64]), ALU.mult, [tb[kx], t_s2], [T.t_xdte])
                tt("dve", T.xsD[:].rearrange("p (h d) -> p h d", h=8), xs_ps,
                   dskip.unsqueeze(2).to_broadcast([128, 8, 64]), ALU.mult, [tb[kx], t_small], [T.t_xsD])
                cp("act", T.Btok[:], bankb(kx)[:, 512:768], [tb[kx]], [T.t_Btok])
                if gc > 0:
                    yield ("wait", lambda: ("Sst", gc - 1) in done)
                yield H("pe", t_MT, T.t_xdt, T.t_xdte, T.t_xsD, T.t_Btok, t_Sb)
                kcs = nb()
                for g in range(2):
                    mm(bank(kcs)[:, g * 256:(g + 1) * 256], T.Btok[:, g * 128:(g + 1) * 128],
                       T.xdte[:, g * 256:(g + 1) * 256], True, True, [T.t_Btok, T.t_xdte], [tb[kcs]], g == 1)
                ko = nb()
                for g in range(2):
                    mm(bank(ko)[:, g * 256:(g + 1) * 256], post_[:, 6 + g, tsl], S_b[:, g * 256:(g + 1) * 256],
                       True, True, [tp_[6 + g], t_Sb], [tb[ko]], g == 1)
                kd = nb()
                mm(bank(kd)[:, 0:512], identb[:], T.xsD[:], True, False, [t_identb, T.t_xsD], [tb[kd]], False)
                for h in range(8):
                    mm(bank(kd)[:, h * 64:(h + 1) * 64], MT[:, h * 128:(h + 1) * 128], T.xdt[:, h * 64:(h + 1) * 64],
                       False, h == 7, [t_MT, T.t_xdt], [tb[kd]], h == 7)
                ybuf = T.ybuf; t_ybuf = T.t_ybuf
                tt("dve", ybuf[:].rearrange("p (h d) -> p h d", h=8),
                   bank(ko)[:, 0:512].rearrange("p (h d) -> p h d", h=8),
                   s1[:, 48:56].unsqueeze(2).to_broadcast([128, 8, 64]), ALU.mult, [tb[ko], t_s1], [t_ybuf])
                tt("dve", ybuf[:], ybuf[:], bank(kd)[:, 0:512], ALU.add, [t_ybuf, tb[kd]], [t_ybuf])
                tt("dve", S_f[:].rearrange("p (h d) -> p h d", h=8), S_f[:].rearrange("p (h d) -> p h d", h=8),
                   s2[:, 8:16].unsqueeze(2).to_broadcast([128, 8, 64]), ALU.mult, [t_Sf, t_s2], [t_Sf])
                tt("dve", S_f[:], S_f[:], bank(kcs)[:, 0:512], ALU.add, [t_Sf, tb[kcs]], [t_Sf])
                cp("act", S_b[:], S_f[:], [t_Sf], [t_Sb])
                done.add(("Sst", gc))
                yield H("pool", t_ybuf)
                ysn = T.ysn; t_ysn = T.t_ysn
                tt("pool", ybuf[:], ybuf[:], sz[par][:, i, :], ALU.mult, [t_ybuf, t_sz[par][i]], [t_ybuf])
                for g in range(2):
                    act(ysn[:, g * 256:(g + 1) * 256], ybuf[:, g * 256:(g + 1) * 256], AF.Square,
                        [t_ybuf], [t_ysn, t_s2], accum=s2[:, 24 + g:25 + g])
                ts("dve", s2[:, 26:28], s2[:, 24:26], 1.0 / 256, EPS, ALU.mult, ALU.add, [t_s2], [t_s2])
                tt("pool", s2[:, 30:32], s2[:, 26:28], mhalf[:, 0:2], ALU.pow, [t_s2, t_mhalf], [t_s2])
                if b > 0:
                    yield ("wait", lambda: ("O", b - 1) in done)
                yield H("dve", t_s2)
                for g in range(2):
                    stt(ysn[:, g * 256:(g + 1) * 256], ybuf[:, g * 256:(g + 1) * 256], s2[:, 30 + g:31 + g],
                        ssmw[:, g * 256:(g + 1) * 256], ALU.mult, ALU.mult, [t_ybuf, t_s2, t_ssmw], [t_ysn])
                ky = nb()
                for cc in range(4):
                    tr(bankb(ky)[:, cc * 128:(cc + 1) * 128], ysn[:, cc * 128:(cc + 1) * 128], identb[:],
                       [t_ysn, t_identb], [tb[ky]], cc == 3)
                cp("act", ycatT[:, 0:4, tsl], bankb(ky)[:, 0:512].rearrange("p (c t) -> p c t", c=4),
                   [tb[ky]], [t_ycatS[i]])
                yield H("dve", t_dtraw[par][i])

            def gen_A(b, j, T):
                par = b % 2
                qT_ = qT[par]; kT_ = kT[par]; v_ = vtok[par]
                Pexp = T.Pexp; t_Pexp = T.t_Pexp; PT = T.PT; t_PT = T.t_PT; yattn = T.yattn; t_yattn = T.t_yattn
                ga = 2 * b + j
                qsl = slice(j * 128, (j + 1) * 128)
                ksl = slice(j * 128, j * 128 + 256)
                msk = maskb[:, 256:512] if ga == 0 else maskb[:, 0:256]
                for g in range(2):
                    s3, t_s3 = T.nsm()
                    ka = nb2()
                    sc = ps[:, ka * 512:ka * 512 + 1024]
                    t_sc = [tb[ka], tb[ka + 1]]
                    for r in range(4):
                        hf = r % 2
                        psl = slice(hf * 64, (hf + 1) * 64)
                        kk = ka + r // 2
                        mm(sc[:, r * 256:(r + 1) * 256], qT_[psl, 2 * g + r // 2, qsl], kT_[psl, g, ksl], True, False,
                           [t_qT[par][2 * g + r // 2], t_kT[par][g]], [tb[kk]], False)
                        mm(sc[:, r * 256:(r + 1) * 256], identb[:], msk, False, True,
                           [t_identb, t_maskb], [tb[kk]], True)
                    op("dve", lambda: nc.vector.tensor_reduce(
                        out=s3[:, 0:4], in_=sc.rearrange("p (r k) -> p r k", r=4), axis=AX.X, op=ALU.max),
                        reads=t_sc, writes=[t_s3], cost=1.2)
                    stt(s3[:, 4:8], s3[:, 0:4], 0.125, sinks[:, g * 4:(g + 1) * 4], ALU.mult, ALU.max,
                        [t_s3, t_small], [t_s3])
                    ts("dve", s3[:, 8:12], s3[:, 4:8], -1.0, None, ALU.mult, None, [t_s3], [t_s3])
                    for r in range(4):
                        act(Pexp[:, r * 256:(r + 1) * 256], sc[:, r * 256:(r + 1) * 256], AF.Exp,
                            [tb[ka + r // 2], t_s3], [t_Pexp, t_s3], bias=s3[:, 8 + r:9 + r], scale=0.125,
                            accum=s3[:, 12 + r:13 + r])
                    yield H("dve", t_s3, t_Pexp)
                    tt("dve", s3[:, 16:20], sinks[:, g * 4:(g + 1) * 4], s3[:, 4:8], ALU.subtract,
                       [t_small, t_s3], [t_s3])
                    act(s3[:, 20:24], s3[:, 16:20], AF.Exp, [t_s3], [t_s3])
                    tt("dve", s3[:, 24:28], s3[:, 12:16], s3[:, 20:24], ALU.add, [t_s3], [t_s3])
                    op("dve", lambda: nc.vector.reciprocal(out=s3[:, 28:32], in_=s3[:, 24:28]),
                       reads=[t_s3], writes=[t_s3])
                    kt = nb()
                    for r in range(4):
                        for kk in range(2):
                            idx = r * 2 + kk
                            tr(bankb(kt)[:, idx * 128:(idx + 1) * 128],
                               Pexp[:, r * 256 + kk * 128:r * 256 + (kk + 1) * 128], identb[:],
                               [t_Pexp, t_identb], [tb[kt]], idx == 7)
                    cp("act", PT[:], bankb(kt)[:, 0:1024], [tb[kt]], [t_PT])
                    yield H("pe", t_PT, t_s3)
                    kv = nb()
                    for r in range(4):
                        o_ = bank(kv)[:, r * 64:(r + 1) * 64]
                        mm(o_, PT[:, (r * 2) * 128:(r * 2 + 1) * 128], v_[:, j, g * 64:(g + 1) * 64], True, False,
                           [t_PT, t_vtok[par]], [tb[kv]], False)
                        mm(o_, PT[:, (r * 2 + 1) * 128:(r * 2 + 2) * 128], v_[:, j + 1, g * 64:(g + 1) * 64], False, True,
                           [t_PT, t_vtok[par]], [tb[kv]], r == 3)
                    tt("dve", yattn[:, g * 256:(g + 1) * 256].rearrange("p (r d) -> p r d", r=4),
                       bank(kv)[:, 0:256].rearrange("p (r d) -> p r d", r=4),
                       s3[:, 28:32].unsqueeze(2).to_broadcast([128, 4, 64]), ALU.mult,
                       [tb[kv], t_s3], [t_yattn])
                    yield H("pe", t_yattn)
                if b > 0:
                    yield ("wait", lambda: ("O", b - 1) in done)
                ky = nb()
                for cc in range(4):
                    tr(bankb(ky)[:, cc * 128:(cc + 1) * 128], yattn[:, cc * 128:(cc + 1) * 128], identb[:],
                       [t_yattn, t_identb], [tb[ky]], cc == 3)
                cp("act", ycatT[:, 4:8, qsl], bankb(ky)[:, 0:512].rearrange("p (c t) -> p c t", c=4),
                   [tb[ky]], [t_ycatA[j]])
                yield H("pe", t_yattn)

            def emit_O(b):
                for i in range(2):
                    gt = 2 * b + i
                    for hf in range(2):
                        k = nb()
                        for kc in range(KC):
                            mm(bank(k)[:, 0:512], ycatT[:, kc, i * 128:(i + 1) * 128], Wout[:, kc, hf * 512:(hf + 1) * 512],
                               kc == 0, kc == KC - 1, [t_ycatS[i] if kc < 4 else t_ycatA[i], t_Wout], [tb[k]], kc == KC - 1)
                        tt("dve", X[:, gt, hf * 512:(hf + 1) * 512], X[:, gt, hf * 512:(hf + 1) * 512], bank(k)[:, 0:512],
                           ALU.add, [tX[gt], tb[k]], [tX[gt]])

            def chain(*gs):
                for g_ in gs:
                    yield from g_

            def all_done(b):
                return all(k_ in done for k_ in (("Sc", b, 0), ("Sc", b, 1), ("Ac", b, 0), ("Ac", b, 1)))

            def stream_NP():
                for b in range(nblk):
                    yield from gen_N(b)
                    if b >= 2:
                        yield ("wait", lambda b=b: all_done(b - 2))
                    yield from gen_P(b)
                    done.add(("P", b))

            def stream_S(i):
                for b in range(nblk):
                    yield ("wait", lambda b=b: ("P", b) in done)
                    yield from gen_S(b, i, ST[i])
                    done.add(("Sc", b, i))

            def stream_A(j):
                for b in range(nblk):
                    yield ("wait", lambda b=b: ("P", b) in done)
                    yield from gen_A(b, j, AT[j])
                    done.add(("Ac", b, j))

            def stream_O():
                for b in range(nblk):
                    yield ("wait", lambda b=b: all_done(b))
                    emit_O(b)
                    done.add(("O", b))
                    yield H("pe", t_ycatS[0])

            run_streams([stream_NP(), stream_S(0), stream_S(1), stream_A(0), stream_A(1), stream_O()])
            if do_ffn:
                c.dma("sp", wb[:], nw_ffn_d, writes=[t_wb])
            build_program.tmix = max(c.tfree.values())
            fence = c.fence_tokens()
        if do_ffn:
            with ExitStack() as fs:
                H2T = c.sb("H2T", [128, KC, L], BF16, fs)
                t_H2T = [Trk("H2T%d" % i, fence) for i in range(4)]
                actT = c.sb("actT", [128, JMAX, L], BF16, fs)
                t_actT = [[Trk("actT%d_%d" % (i, q), fence) for q in range(4)] for i in range(JMAX)]
                Wd = [c.sb("Wd%d" % i, [128, JMAX, D], BF16, fs) for i in range(2)]
                t_Wd = [Trk("Wd%d" % i, fence) for i in range(2)]
                xn2 = c.sb("xn2", [128, D], BF16, fs); t_xn2 = Trk("xn2", fence)
                sg = [c.sb("sg%d" % i, [128, 512], F32, fs) for i in range(2)]
                t_sg = [Trk("sg%d" % i, fence) for i in range(2)]
                ostg = [c.sb("ostg%d" % i, [128, D], F32, fs) for i in range(2)]
                t_ostg = [Trk("ostg%d" % i, fence) for i in range(2)]
                for i in range(1, NGU):
                    gu.append(c.sb("gu%d" % i, [128, 2, KC, 128], BF16, fs))
                    t_gu[i] = Trk("gu%d" % i, fence)

                for j in range(1, NGU):
                    gu_load(j)

                def wd_load(p):
                    j0, J = PASSES[p]
                    c.dma("pool", Wd[p % 2][:, 0:J, :], w_dn_d[:, j0:j0 + J, :], writes=[t_Wd[p % 2]])

                xn2b = c.sb("xn2b", [128, D], BF16, fs); t_xn2b = Trk("xn2b", fence)
                xn2s = [xn2, xn2b]
                t_xn2s = [t_xn2, t_xn2b]

                def norm2_A(gt):
                    xq = xn2s[gt % 2]; t_xq = t_xn2s[gt % 2]
                    s_, t_s = nsm_F()
                    act(xq[:], X[:, gt, :], AF.Square, [tX[gt]], [t_xq, t_s], accum=s_[:, 0:1])
                    ts("dve", s_[:, 1:2], s_[:, 0:1], 1.0 / D, EPS, ALU.mult, ALU.add, [t_s], [t_s])
                    tt("pool", s_[:, 3:4], s_[:, 1:2], mhalf[:, 0:1], ALU.pow, [t_s, t_mhalf], [t_s])
                    stt(xq[:], X[:, gt, :], s_[:, 3:4], wb[:], ALU.mult, ALU.mult, [tX[gt], t_s, t_wb], [t_xq])

                def norm2_B(gt):
                    xq = xn2s[gt % 2]; t_xq = t_xn2s[gt % 2]
                    k = nb()
                    for kc in range(KC):
                        tr(bankb(k)[:, kc * 128:(kc + 1) * 128], xq[:, kc * 128:(kc + 1) * 128], identb[:],
                           [t_xq, t_identb], [tb[k]], kc == KC - 1)
                    cp("act", H2T[:, :, gt * 128:(gt + 1) * 128],
                       bankb(k)[:, 0:1024].rearrange("p (k t) -> p k t", k=KC), [tb[k]], [t_H2T[gt // 4]])

                sgc = [0]

                def ffn_a(j, jj, tbk):
                    slot = j % NGU
                    tsl = slice(tbk * 512, (tbk + 1) * 512)
                    kg = nb()
                    for kc in range(KC):
                        mm(bank(kg), gu[slot][:, 0, kc, :], H2T[:, kc, tsl], kc == 0, kc == KC - 1,
                           [t_gu[slot], t_H2T[tbk]], [tb[kg]], kc == KC - 1)
                    ku = nb()
                    for kc in range(KC):
                        mm(bank(ku), gu[slot][:, 1, kc, :], H2T[:, kc, tsl], kc == 0, kc == KC - 1,
                           [t_gu[slot], t_H2T[tbk]], [tb[ku]], kc == KC - 1)
                    si = sgc[0] % 2
                    sgc[0] += 1
                    act(sg[si][:], bank(kg), AF.Silu, [tb[kg]], [t_sg[si]])
                    tt("dve", actT[:, jj, tsl], sg[si][:], bank(ku), ALU.mult, [t_sg[si], tb[ku]], [t_actT[jj][tbk]])

                def final_tile(gt):
                    s_, t_s = nsm_F()
                    o_ = ostg[gt % 2]; t_o = t_ostg[gt % 2]
                    act(o_[:], X[:, gt, :], AF.Square, [tX[gt]], [t_o, t_s], accum=s_[:, 0:1])
                    ts("dve", s_[:, 1:2], s_[:, 0:1], 1.0 / D, EPS, ALU.mult, ALU.add, [t_s], [t_s])
                    tt("pool", s_[:, 3:4], s_[:, 1:2], mhalf[:, 0:1], ALU.pow, [t_s, t_mhalf], [t_s])
                    stt(o_[:], X[:, gt, :], s_[:, 3:4], wb[:], ALU.mult, ALU.mult, [tX[gt], t_s, t_wb], [t_o])
                    c.dma("sp", out_d[gt * 128:(gt + 1) * 128, :], o_[:], reads=[t_o])

                wd_load(0)

                def stream_norm2():
                    norm2_A(0)
                    yield H("act", tX[1])
                    for gt in range(NT):
                        if gt + 1 < NT:
                            norm2_A(gt + 1)
                            yield H("pe", t_xn2s[gt % 2])
                        norm2_B(gt)
                        done.add(("H2T", gt))
                        yield H("act", tX[min(gt + 2, NT - 1)])

                def stream_ffn0():
                    j0, J = PASSES[0]
                    for tbk in range(4):
                        yield ("wait", lambda tbk=tbk: ("H2T", 4 * tbk + 3) in done)
                        for jj in range(J):
                            ffn_a(j0 + jj, jj, tbk)
                            if tbk == 3 and j0 + jj + NGU < NJ:
                                gu_load(j0 + jj + NGU)
                            yield H("pe", t_H2T[tbk])

                for p, (j0, J) in enumerate(PASSES):
                    if p + 1 < len(PASSES):
                        wd_load(p + 1)
                    last = p == len(PASSES) - 1

                    def ffn_b_tile(gt, p=p, J=J):
                        for hf in range(2):
                            k = nb()
                            for jj in range(J):
                                mm(bank(k), actT[:, jj, gt * 128:(gt + 1) * 128], Wd[p % 2][:, jj, hf * 512:(hf + 1) * 512],
                                   jj == 0, jj == J - 1, [t_actT[jj][gt // 4], t_Wd[p % 2]], [tb[k]], jj == J - 1)
                            tt("dve", X[:, gt, hf * 512:(hf + 1) * 512], X[:, gt, hf * 512:(hf + 1) * 512], bank(k),
                               ALU.add, [tX[gt], tb[k]], [tX[gt]])

                    if p == 0:
                        run_streams([stream_norm2(), stream_ffn0()])
                        c.dma("sp", wb[:], nw_fin_d, writes=[t_wb])
                        for gt in range(NT):
                            ffn_b_tile(gt)
                    elif not last:
                        for jj in range(J):
                            for tbk in range(4):
                                ffn_a(j0 + jj, jj, tbk)
                            if j0 + jj + NGU < NJ:
                                gu_load(j0 + jj + NGU)
                        for gt in range(NT):
                            ffn_b_tile(gt)
                    else:
                        for tbk in range(4):
                            for jj in range(J):
                                ffn_a(j0 + jj, jj, tbk)
                            for gt in range(4 * tbk, 4 * tbk + 4):
                                ffn_b_tile(gt)
                                final_tile(gt)
                c.wait_all("sp", t_ostg)
        else:
            for gt in range(NT):
                c.dma("sp", out_d[gt * 128:(gt + 1) * 128, :], X[:, gt, :], reads=[tX[gt]])
            c.wait_all("sp", tX)
        build_program.stats = (c.n_ins, c.n_wait, c.n_dsem, dict(c.cnt))
        build_program.tend = max(c.tfree.values())
        build_program.log = c.log
    return nc


def _consts():
    identf = np.eye(128, dtype=np.float32)
    u = np.arange(128)
    tri = (u[:, None] <= u[None, :]).astype(np.float32)
    nm = np.where(u[:, None] <= u[None, :], 0.0, NEG).astype(np.float32)
    negmask = np.tile(nm, (1, 4))
    q = u[:, None]
    kk = u[None, :]
    prev = np.where(kk > q, 0.0, NEG)
    cur = np.where(kk <= q, 0.0, NEG)
    mask = np.concatenate([prev, cur], axis=1)
    mask0 = np.concatenate([np.full((128, 128), NEG), cur], axis=1)
    maskb = np.concatenate([mask, mask0], axis=1).astype(np.float32)
    return identf, tri, negmask, maskb


def _prep_shared(inp):
    f = np.float32
    w_in = np.asarray(inp["w_in"], f)[0]
    w_in_l = np.ascontiguousarray(w_in.reshape(KC, 128, D_IN).transpose(1, 0, 2))
    kcols = w_in[:, 2056:2184]
    k2 = np.concatenate([kcols[:, 0:64], kcols[:, 0:64], kcols[:, 64:128], kcols[:, 64:128]], axis=1)
    w_k2_l = np.ascontiguousarray(k2.reshape(KC, 128, 256).transpose(1, 0, 2))
    w_out = np.asarray(inp["w_out"], f)[0]
    w_out_l = np.ascontiguousarray(w_out.reshape(KC, 128, D).transpose(1, 0, 2))
    wg = np.asarray(inp["w_gate"], f)[0].reshape(KC, 128, NJ, 128)
    wu = np.asarray(inp["w_up"], f)[0].reshape(KC, 128, NJ, 128)
    w_gu = np.ascontiguousarray(np.stack([wg, wu], axis=0).transpose(3, 2, 0, 1, 4))
    w_dn = np.asarray(inp["w_down"], f)[0]
    w_dn_l = np.ascontiguousarray(w_dn.reshape(NJ, 128, D).transpose(1, 0, 2))

    def bc(v, n):
        return np.ascontiguousarray(np.broadcast_to(np.asarray(v, f).reshape(1, n), (128, n)))

    convw = np.ascontiguousarray(np.asarray(inp["conv_w"], f)[0].T.reshape(8, 128, 4).transpose(1, 0, 2))
    convb = np.ascontiguousarray(np.asarray(inp["conv_b"], f)[0].reshape(8, 128).T)
    small = np.zeros((128, 48), f)
    small[:, 0:8] = convb
    small[:, 8:16] = bc(inp["dt_bias"][0], 8)
    small[:, 16:24] = bc(inp["a_log"][0], 8)
    small[:, 24:32] = bc(inp["d_skip"][0], 8)
    small[:, 32:40] = bc(inp["attn_sinks"][0], 8)
    identf, tri, negmask, maskb = _consts()
    return {
        "w_in": w_in_l, "w_k2": w_k2_l, "w_out": w_out_l, "w_gu": w_gu, "w_dn": w_dn_l,
        "nw_mix": bc(inp["norm_mix_w"][0], D), "nw_ffn": bc(inp["norm_ffn_w"][0], D),
        "nw_fin": bc(inp["norm_final_w"], D), "convw": convw, "smallp": small,
        "ssmw": bc(inp["ssm_norm_w"][0], 512), "identf": identf, "tri": tri,
        "negmask": negmask, "maskb": maskb,
    }


_NC_CACHE = {}


def kernel(**inputs):
    x = np.asarray(inputs["x"], np.float32)
    shared = _prep_shared(inputs)
    if "nc" not in _NC_CACHE:
        _NC_CACHE["nc"] = build_program()
    nc = _NC_CACHE["nc"]
    in_maps = []
    for b in range(8):
        m = dict(shared)
        m["x"] = np.ascontiguousarray(x[b])
        in_maps.append(m)
    res = run_bass_kernel_spmd(nc, in_maps, core_ids=list(range(8)))
    out = np.stack([np.asarray(r["out"], np.float32) for r in res.results], axis=0)
    return out
```

```python
from contextlib import ExitStack

import numpy as np
import concourse.bass as bass
import concourse.mybir as mybir
from concourse.bass_utils import run_bass_kernel_spmd

F32 = mybir.dt.float32
BF16 = mybir.dt.bfloat16
AF = mybir.ActivationFunctionType
ALU = mybir.AluOpType
AX = mybir.AxisListType

SAME_ENGINE_SYNC = True
V_LAT = 0.1
V_SLAT = 0.0
V_PEFIX = 0
V_SLACK = -1.0
V_ORDER = 'NSSAAO'
V_DVE = 1.3
V_ACT = 1.0
V_PE = 1.0
V_SEED = 0
V_JIT = 0.0
import random as _rnd
_RNG = _rnd.Random(V_SEED)

L = 2048
D = 1024
NT = 16
KC = 8
TB = 256
NBLK = L // TB
D_IN = 2312
D_FF = 2816
NJ = D_FF // 128
PASSES = [(0, 6), (6, 6), (12, 5), (17, 5)]
JMAX = 6
EPS = 1e-5
NEG = -30000.0


class Trk:
    __slots__ = ("name", "w", "r", "dsem", "dcnt", "excl")

    def __init__(self, name="", fence=None, excl=False):
        self.name = name
        self.excl = excl
        self.w = None
        self.r = list(fence) if fence else []
        self.dsem = None
        self.dcnt = 0


class Ctx:
    def __init__(self, nc, stack):
        self.nc = nc
        self.stack = stack
        self.eng = {"pe": nc.tensor, "act": nc.scalar, "dve": nc.vector,
                    "pool": nc.gpsimd, "sp": nc.sync}
        self.sem = {}
        self.cnt = {}
        self.seen = {}
        for k in self.eng:
            self.sem[k] = stack.enter_context(nc.semaphore("s_" + k))
            self.cnt[k] = 0
            self.seen[k] = {}
        self.n_dsem = 0
        self.n_wait = 0
        self.n_ins = 0
        self.tfree = {k: 0.0 for k in self.eng}
        self.ttok = {}
        self.step_fin = 0.0
        self.log = None
        self.tag = ""
        self.tokinfo = {}
        self.crit = None

    def _tdeps(self, e, deps):
        t = 0.0
        self.crit = None
        for d in deps:
            if d is None:
                continue
            if d[0] is self.sem.get(e):
                if V_PEFIX and e == "pe":
                    continue
                lat = V_SLAT
            else:
                lat = V_LAT
            td = self.ttok.get((id(d[0]), d[1]), 0.0) + lat
            if td > t:
                t = td
                self.crit = self.tokinfo.get((id(d[0]), d[1]))
        return t

    def sb(self, name, shape, dtype, stack=None):
        return (stack or self.stack).enter_context(
            self.nc.sbuf_tensor("sb_" + name, list(shape), dtype))

    def _wait(self, e, deps):
        h = self.eng[e]
        best = {}
        for d in deps:
            if d is None:
                continue
            s, v = d
            if s is self.sem.get(e):
                if e == "pe" or not SAME_ENGINE_SYNC or v > self.cnt[e]:
                    continue
            k = id(s)
            if k not in best or best[k][1] < v:
                best[k] = (s, v)
        for k, (s, v) in best.items():
            if self.seen[e].get(k, 0) >= v:
                continue
            h.wait_ge(s, v)
            self.n_wait += 1
            self.seen[e][k] = v

    def op(self, e, fn, reads=(), writes=(), signal=True, cost=0.1):
        deps = []
        for t in reads:
            deps.append(t.w)
            if t.excl:
                deps.extend(r for r in t.r if r[0] is not self.sem[e])
        for t in writes:
            deps.append(t.w)
            deps.extend(t.r)
        self._wait(e, deps)
        ins = fn()
        self.n_ins += 1
        self.tfree_prev = self.tfree[e]
        start = max(self.tfree[e], self._tdeps(e, deps))
        fin = start + cost * (V_DVE if e == 'dve' else V_ACT if e == 'act' else V_PE if e == 'pe' else 1.0)
        self.tfree[e] = fin
        if signal:
            self.cnt[e] += 1
            ins.then_inc(self.sem[e], 1)
            tok = (self.sem[e], self.cnt[e])
        else:
            tok = (self.sem[e], self.cnt[e] + 1)
        self.ttok[(id(tok[0]), tok[1])] = fin + (0.1 if e == "pe" else 0.0)
        self.step_fin = max(self.step_fin, fin)
        if self.log is not None:
            try:
                nm = str(ins.ins.name)
                self.tokinfo[(id(tok[0]), tok[1])] = (nm, self.tag, e)
                self.log.append((nm, e, self.tag, start, fin, self.crit, self.tfree_prev))
            except Exception:
                pass
        for t in reads:
            t.r.append(tok)
        for t in writes:
            t.w = tok
            t.r = []
        return ins

    def dma(self, q, out, in_, reads=(), writes=(), owner=None, join=False, nbytes=65536):
        own = owner or (writes[0] if writes else reads[0])
        if own.dsem is None:
            own.dsem = self.stack.enter_context(self.nc.semaphore("d%d" % self.n_dsem))
            self.n_dsem += 1
        deps = []
        for t in reads:
            deps.append(t.w)
        for t in writes:
            if not (join and t.w is not None and t.w[0] is own.dsem):
                deps.append(t.w)
            deps.extend(t.r)
        self._wait(q, deps)
        own.dcnt += 16
        self.eng[q].dma_start(out=out, in_=in_).then_inc(own.dsem, 16)
        self.n_ins += 1
        tok = (own.dsem, own.dcnt)
        start = max(self.tfree[q], self._tdeps(q, deps))
        self.tfree[q] = start + (1.0 if q == "pool" else 0.1)
        self.ttok[(id(tok[0]), tok[1])] = start + 2.5 + nbytes / 2.0e5
        for t in reads:
            t.r.append(tok)
        for t in writes:
            t.w = tok
            t.r = []

    def wait_all(self, e, trks):
        deps = []
        for t in trks:
            deps.append(t.w)
            deps.extend(t.r)
        self._wait(e, deps)

    def fence_tokens(self):
        toks = []
        for e in ("pe", "act", "dve", "pool"):
            if self.cnt[e] > 0:
                toks.append((self.sem[e], self.cnt[e]))
        return toks


def build_program(nblk=NBLK, do_ffn=True, debug=()):
    nc = bass.Bass("TRN2", target_bir_lowering=False)

    def din(name, shape):
        return nc.dram_tensor(name, list(shape), F32, kind="ExternalInput").ap()

    x_d = din("x", [L, D])
    w_in_d = din("w_in", [128, KC, D_IN])
    w_k2_d = din("w_k2", [128, KC, 256])
    w_out_d = din("w_out", [128, KC, D])
    w_gu_d = din("w_gu", [NJ, 128, 2, KC, 128])
    w_dn_d = din("w_dn", [128, NJ, D])
    nw_mix_d = din("nw_mix", [128, D])
    nw_ffn_d = din("nw_ffn", [128, D])
    nw_fin_d = din("nw_fin", [128, D])
    convw_d = din("convw", [128, 8, 4])
    sm_d = din("smallp", [128, 48])
    ssmw_d = din("ssmw", [128, 512])
    identf_d = din("identf", [128, 128])
    tri_d = din("tri", [128, 128])
    negmask_d = din("negmask", [128, 512])
    maskb_d = din("maskb", [128, 512])
    out_d = nc.dram_tensor("out", [L, D], F32, kind="ExternalOutput").ap()
    dbg_outs = {}

    with ExitStack() as st:
        c = Ctx(nc, st)
        op = c.op
        if debug:
            c.log = []

        ps = st.enter_context(nc.psum_tensor("ps", [128, 4096], F32))
        psb = ps.bitcast(BF16) if hasattr(ps, "bitcast") else None
        tb = [Trk("bank%d" % k, excl=True) for k in range(8)]
        bank_ctr = [0]

        def bank(k):
            return ps[:, k * 512:(k + 1) * 512]

        def bankb(k):
            return psb[:, k * 1024:(k + 1) * 1024]

        def nb():
            k = bank_ctr[0] % 8
            bank_ctr[0] += 1
            return k

        def nb2():
            if bank_ctr[0] % 2:
                bank_ctr[0] += 1
            k = bank_ctr[0] % 8
            bank_ctr[0] += 2
            return k

        def nfree(ap):
            n = 1
            for d in ap.shape[1:]:
                n *= int(d)
            return n

        def mm(out, lhsT, rhs, start, stop, reads, writes, signal):
            n = max(nfree(rhs), 64)
            cst = n / 2400.0 * (4.0 if rhs.dtype == F32 else 1.0) + 0.01
            op("pe", lambda: nc.tensor.matmul(out, lhsT=lhsT, rhs=rhs, start=start, stop=stop),
               reads=reads, writes=writes, signal=signal, cost=cst)

        def tr(out, in_, ident, reads, writes, signal):
            op("pe", lambda: nc.tensor.transpose(out, in_, ident),
               reads=reads, writes=writes, signal=signal, cost=0.07)

        def act(out, in_, func, reads, writes, bias=None, scale=None, accum=None):
            kw = {}
            if bias is not None:
                kw["bias"] = bias
            if scale is not None:
                kw["scale"] = scale
            if accum is not None:
                kw["accum_out"] = accum
            op("act", lambda: nc.scalar.activation(out=out, in_=in_, func=func, **kw),
               reads=reads, writes=writes, cost=0.12 + nfree(out) / 1200.0 + (0.1 if accum is not None else 0.0))

        def ecost(e, out, mult=1.0):
            n = nfree(out)
            if e == "dve":
                return 0.07 + mult * n / 960.0
            return 1.0 + n / 400.0

        def tt(e, out, in0, in1, aop, reads, writes):
            h = nc.vector if e == "dve" else nc.gpsimd
            op(e, lambda: h.tensor_tensor(out=out, in0=in0, in1=in1, op=aop),
               reads=reads, writes=writes, cost=ecost(e, out))

        def ts(e, out, in0, s1, s2, op0, op1, reads, writes):
            h = nc.vector if e == "dve" else nc.gpsimd
            if op1 is None:
                op(e, lambda: h.tensor_scalar(out=out, in0=in0, scalar1=s1, scalar2=None, op0=op0),
                   reads=reads, writes=writes, cost=ecost(e, out))
            else:
                op(e, lambda: h.tensor_scalar(out=out, in0=in0, scalar1=s1, scalar2=s2, op0=op0, op1=op1),
                   reads=reads, writes=writes, cost=ecost(e, out))

        def stt(out, in0, scalar, in1, op0, op1, reads, writes):
            op("dve", lambda: nc.vector.scalar_tensor_tensor(out=out, in0=in0, scalar=scalar, in1=in1,
                                                             op0=op0, op1=op1),
               reads=reads, writes=writes, cost=ecost("dve", out))

        def cp(e, out, in_, reads, writes):
            if e == "act":
                op("act", lambda: nc.scalar.copy(out=out, in_=in_), reads=reads, writes=writes,
                   cost=0.12 + nfree(out) / 1200.0)
            else:
                h = nc.vector if e == "dve" else nc.gpsimd
                op(e, lambda: h.tensor_copy(out=out, in_=in_), reads=reads, writes=writes, cost=ecost(e, out))

        def dbg(name, ap, trk, shape):
            if name not in debug:
                return
            d = nc.dram_tensor("dbg_" + name, list(shape), F32, kind="ExternalOutput").ap()
            stg = c.sb("dbgs_" + name, list(shape), F32)
            t = Trk()
            cp("dve", stg[:], ap, [trk], [t])
            c.dma("sp", d, stg[:], reads=[t])
            dbg_outs[name] = t

        X = c.sb("X", [128, NT, D], F32)
        tX = [Trk("X%d" % i) for i in range(NT)]
        identb = c.sb("identb", [128, 128], BF16); t_identb = Trk()
        tri = c.sb("tri", [128, 128], F32); t_tri = Trk()
        negmask = c.sb("negmask", [128, 512], BF16); t_negmask = Trk()
        maskb = c.sb("maskb", [128, 512], BF16); t_maskb = Trk()
        convw = c.sb("convw", [128, 8, 4], F32); t_convw = Trk()
        smallp = c.sb("smallp", [128, 48], F32); t_small = Trk()
        a_b = c.sb("a_b", [128, 8], F32); t_ab = Trk()
        ssmw = c.sb("ssmw", [128, 512], F32); t_ssmw = Trk()
        mhalf = c.sb("mhalf", [128, 2], F32); t_mhalf = Trk()
        wb = c.sb("wb", [128, D], F32); t_wb = Trk()
        NGU = 6
        gu = [c.sb("gu%d" % i, [128, 2, KC, 128], BF16) for i in range(1)]
        t_gu = [Trk("gu%d" % i) for i in range(NGU)]
        convb = smallp[:, 0:8]
        dtb = smallp[:, 8:16]
        alog = smallp[:, 16:24]
        dskip = smallp[:, 24:32]
        sinks = smallp[:, 32:40]

        def stat_tiles(name, n, w=64, stack=None):
            tiles = [c.sb("%s%d" % (name, i), [128, w], F32, stack) for i in range(n)]
            trks = [Trk("%s%d" % (name, i)) for i in range(n)]
            ctr = [0]

            def nxt():
                k = ctr[0] % n
                ctr[0] += 1
                return tiles[k], trks[k]
            return nxt

        nsm_N = stat_tiles("stN", 2, 8)
        nsm_F = stat_tiles("stF", 2, 8)

        def gu_load(j):
            slot = j % NGU
            c.dma("pool", gu[slot][:], w_gu_d[j], writes=[t_gu[slot]])

        done = set()

        def H(eng, *trks):
            return ("h", eng, list(trks))

        def vt_of(s_):
            h_ = s_["hint"]
            if h_ is None:
                return s_["vt"]
            t_ = c.tfree[h_[0]]
            for q_ in h_[1]:
                if q_.w is not None:
                    t_ = max(t_, c.ttok.get((id(q_.w[0]), q_.w[1]), 0.0))
            if V_SLACK >= 0:
                t_ = max(t_, s_["vt"] - V_SLACK)
            if V_JIT > 0:
                t_ += _RNG.random() * V_JIT
            return t_

        def run_streams(gens):
            sts = [{"g": g_, "vt": 0.0, "blk": None, "nm": getattr(g_, "__name__", "?"), "k": 0, "hint": None} for g_ in gens]
            while sts:
                progressed = False
                for s_ in sorted(sts, key=vt_of):
                    if s_["blk"] is not None:
                        if not s_["blk"]():
                            continue
                        s_["blk"] = None
                    c.step_fin = 0.0
                    s_["k"] += 1
                    c.tag = "%s#%d" % (s_["nm"], s_["k"])
                    try:
                        r_ = next(s_["g"])
                    except StopIteration:
                        sts.remove(s_)
                        progressed = True
                        break
                    if isinstance(r_, tuple) and r_[0] == "wait":
                        if not r_[1]():
                            s_["blk"] = r_[1]
                    elif isinstance(r_, tuple) and r_[0] == "h":
                        s_["hint"] = (r_[1], r_[2])
                        s_["vt"] = max(s_["vt"], c.step_fin)
                    else:
                        s_["hint"] = None
                        s_["vt"] = max(s_["vt"], c.step_fin)
                    progressed = True
                    break
                if not progressed:
                    raise RuntimeError("emission deadlock")

        c.dma("sp", X[:, 0, :], x_d[0:128, :], writes=[tX[0]])
        c.dma("sp", X[:, 1, :], x_d[128:256, :], writes=[tX[1]])
        c.dma("sp", smallp[:], sm_d, writes=[t_small])
        c.dma("sp", convw[:], convw_d, writes=[t_convw])
        c.dma("sp", tri[:], tri_d, writes=[t_tri])
        c.dma("sp", ssmw[:], ssmw_d, writes=[t_ssmw])
        c.dma("pool", identb[:], identf_d, writes=[t_identb])
        c.dma("pool", negmask[:], negmask_d, writes=[t_negmask])
        op("pool", lambda: nc.gpsimd.memset(mhalf[:], -0.5), writes=[t_mhalf])

        with ExitStack() as ms:
            Win = c.sb("Win", [128, KC, D_IN], BF16, ms)
            t_Wxbc = Trk("Wxbc"); t_Wq = Trk("Wq"); t_Wz = Trk("Wz"); t_Wdt = t_Wq; t_Wv = t_Wq
            Wk2 = c.sb("Wk2", [128, KC, 256], BF16, ms); t_Wk2 = Trk()
            Wout = c.sb("Wout", [128, KC, D], BF16, ms); t_Wout = Trk("Wout")
            hT1 = c.sb("hT", [128, KC, TB], BF16, ms)
            t_hT1 = Trk("hT")
            hT = [hT1, hT1]
            t_hT = [t_hT1, t_hT1]
            xn = c.sb("xn", [128, D], BF16, ms); t_xn = Trk()
            pre2 = [c.sb("pre%d" % i, [128, TB + 3], F32, ms) for i in range(2)]
            t_pre2 = [Trk() for _ in range(2)]
            halo = c.sb("halo", [128, 8, 3], F32, ms)
            t_halo = [Trk() for _ in range(8)]
            acc = [c.sb("acc%d" % i, [128, TB], F32, ms) for i in range(2)]
            t_acc = [Trk() for _ in range(2)]
            sgt = [c.sb("sgt%d" % i, [128, TB], F32, ms) for i in range(2)]
            t_sgt = [Trk() for _ in range(2)]
            post = [c.sb("post%d" % i, [128, 8, TB], BF16, ms) for i in range(2)]
            t_post = [[Trk() for _ in range(8)] for _ in range(2)]
            qT = [c.sb("qT%d" % i, [128, 4, TB], BF16, ms) for i in range(2)]
            t_qT = [[Trk() for _ in range(4)] for _ in range(2)]
            kT = [c.sb("kT%d" % i, [128, 2, 128 + TB], BF16, ms) for i in range(2)]
            t_kT = [[Trk() for _ in range(2)] for _ in range(2)]
            vtok = [c.sb("vtok%d" % i, [128, 3, 128], BF16, ms) for i in range(2)]
            t_vtok = [Trk() for _ in range(2)]
            sz = [c.sb("sz%d" % i, [128, 2, 512], F32, ms) for i in range(2)]
            t_sz = [[Trk() for _ in range(2)] for _ in range(2)]
            dtraw = [c.sb("dtraw%d" % i, [128, 2, 8], F32, ms) for i in range(2)]
            t_dtraw = [[Trk() for _ in range(2)] for _ in range(2)]
            class NS:
                pass

            def Sset(q):
                T = NS()
                T.cbs = c.sb("cbs%d" % q, [128, 256], F32, ms); T.t_cbs = Trk()
                T.MT = c.sb("MT%d" % q, [128, 1024], BF16, ms); T.t_MT = Trk()
                T.xdt = c.sb("xdt%d" % q, [128, 512], BF16, ms); T.t_xdt = Trk()
                T.xdte = c.sb("xdte%d" % q, [128, 512], BF16, ms); T.t_xdte = Trk()
                T.xsD = c.sb("xsD%d" % q, [128, 512], BF16, ms); T.t_xsD = Trk()
                T.Btok = c.sb("Btok%d" % q, [128, 256], BF16, ms); T.t_Btok = Trk()
                T.ybuf = c.sb("ybuf%d" % q, [128, 512], F32, ms); T.t_ybuf = Trk()
                T.ysn = c.sb("ysn%d" % q, [128, 512], BF16, ms); T.t_ysn = Trk()
                T.nsm1 = stat_tiles("stS1_%d" % q, 1, 64, ms)
                T.nsm2 = stat_tiles("stS2_%d" % q, 1, 32, ms)
                return T

            def Aset(q):
                T = NS()
                T.Pexp = c.sb("Pexp%d" % q, [128, 1024], BF16, ms); T.t_Pexp = Trk()
                T.PT = c.sb("PT%d" % q, [128, 1024], BF16, ms); T.t_PT = Trk()
                T.yattn = c.sb("yattn%d" % q, [128, 512], BF16, ms); T.t_yattn = Trk()
                T.nsm = stat_tiles("stA_%d" % q, 1, 32, ms)
                return T

            ST = [Sset(0), Sset(1)]
            AT = [Aset(0), Aset(1)]
            S_f = c.sb("S_f", [128, 512], F32, ms); t_Sf = Trk()
            S_b = c.sb("S_b", [128, 512], BF16, ms); t_Sb = Trk()
            ycatT = c.sb("ycatT", [128, 8, TB], BF16, ms)
            t_ycatS = [Trk() for _ in range(2)]
            t_ycatA = [Trk() for _ in range(2)]

            c.dma("sp", wb[:], nw_mix_d, writes=[t_wb])
            for kc in range(KC):
                c.dma("pool", Win[:, kc, 512:1536], w_in_d[:, kc, 512:1536], writes=[t_Wxbc], join=True)
            for kc in range(KC):
                c.dma("pool", Win[:, kc, 1536:2312], w_in_d[:, kc, 1536:2312], writes=[t_Wq], join=True)
            c.dma("pool", Wk2[:], w_k2_d, writes=[t_Wk2])
            c.dma("pool", maskb[:], maskb_d, writes=[t_maskb])
            for kc in range(KC):
                c.dma("pool", Win[:, kc, 0:512], w_in_d[:, kc, 0:512], writes=[t_Wz], join=True)
            c.wait_all("sp", [t_Wxbc])
            for i in range(2, NT):
                c.dma("sp", X[:, i, :], x_d[i * 128:(i + 1) * 128, :], writes=[tX[i]])
            for kc in range(KC):
                c.dma("pool", Wout[:, kc, :], w_out_d[:, kc, :], writes=[t_Wout], join=True)
            if do_ffn:
                gu_load(0)

            act(a_b[:], alog, AF.Exp, [t_small], [t_ab])
            ts("dve", a_b[:], a_b[:], -1.0, None, ALU.mult, None, [t_ab], [t_ab])
            op("dve", lambda: nc.vector.memset(S_f[:], 0.0), writes=[t_Sf])
            op("dve", lambda: nc.vector.memset(S_b[:], 0.0), writes=[t_Sb])
            op("pool", lambda: nc.gpsimd.memset(halo[:], 0.0), writes=t_halo)
            for i in range(2):
                op("pool", lambda: nc.gpsimd.memset(kT[i][:], 0.0), writes=t_kT[i])
                op("pool", lambda: nc.gpsimd.memset(vtok[i][:], 0.0), writes=[t_vtok[i]])

            def rms_stats(src_ap, t_src, n, junk_ap, t_jk, nsm):
                s_, t_s = nsm()
                act(junk_ap, src_ap, AF.Square, [t_src], [t_jk, t_s], accum=s_[:, 0:1])
                ts("dve", s_[:, 1:2], s_[:, 0:1], 1.0 / n, EPS, ALU.mult, ALU.add, [t_s], [t_s])
                tt("pool", s_[:, 3:4], s_[:, 1:2], mhalf[:, 0:1], ALU.pow, [t_s, t_mhalf], [t_s])
                return s_, t_s

            def gen_N(b):
                s = b % 2
                for i in range(2):
                    gt = 2 * b + i
                    s_, t_s = rms_stats(X[:, gt, :], tX[gt], D, xn[:], t_xn, nsm_N)
                    yield H("dve", t_s)
                    stt(xn[:], X[:, gt, :], s_[:, 3:4], wb[:], ALU.mult, ALU.mult,
                        [tX[gt], t_s, t_wb], [t_xn])
                    k = nb()
                    for kc in range(KC):
                        tr(bankb(k)[:, kc * 128:(kc + 1) * 128], xn[:, kc * 128:(kc + 1) * 128], identb[:],
                           [t_xn, t_identb], [tb[k]], kc == KC - 1)
                    cp("act", hT[s][:, :, i * 128:(i + 1) * 128],
                       bankb(k)[:, 0:1024].rearrange("p (k t) -> p k t", k=KC), [tb[k]], [t_hT[s]])
                    yield H("act", tX[min(2 * b + i + 1, NT - 1)])

            def gen_P(b):
                s = b % 2
                o = 1 - s
                for pr in range(4):
                    ocs = (2 * pr, 2 * pr + 1)
                    ks = []
                    for q_, oc in enumerate(ocs):
                        k = nb()
                        ks.append(k)
                        c0 = 512 + oc * 128
                        for kc in range(KC):
                            mm(bank(k)[:, 0:TB], Win[:, kc, c0:c0 + 128], hT[s][:, kc, :], kc == 0, kc == KC - 1,
                               [t_Wxbc, t_hT[s]], [tb[k]], kc == KC - 1)
                    for q_, oc in enumerate(ocs):
                        cp("dve", pre2[q_][:, 0:3], halo[:, oc, :], [t_halo[oc]], [t_pre2[q_]])
                    for q_, oc in enumerate(ocs):
                        cp("act", pre2[q_][:, 3:TB + 3], bank(ks[q_])[:, 0:TB], [tb[ks[q_]]], [t_pre2[q_]])
                    for q_, oc in enumerate(ocs):
                        cp("dve", halo[:, oc, :], pre2[q_][:, TB:TB + 3], [t_pre2[q_]], [t_halo[oc]])
                    for q_, oc in enumerate(ocs):
                        ts("dve", acc[q_][:], pre2[q_][:, 0:TB], convw[:, oc, 0:1], convb[:, oc:oc + 1], ALU.mult, ALU.add,
                           [t_pre2[q_], t_convw, t_small], [t_acc[q_]])
                    for kk in range(1, 4):
                        for q_, oc in enumerate(ocs):
                            stt(acc[q_][:], pre2[q_][:, kk:kk + TB], convw[:, oc, kk:kk + 1], acc[q_][:], ALU.mult, ALU.add,
                                [t_pre2[q_], t_convw, t_acc[q_]], [t_acc[q_]])
                    for q_, oc in enumerate(ocs):
                        act(sgt[q_][:], acc[q_][:], AF.Exp, [t_acc[q_]], [t_sgt[q_]], scale=-1.0)
                    for q_, oc in enumerate(ocs):
                        act(sgt[q_][:], sgt[q_][:], AF.Ln, [t_sgt[q_]], [t_sgt[q_]], bias=1.0)
                    for q_, oc in enumerate(ocs):
                        act(sgt[q_][:], sgt[q_][:], AF.Exp, [t_sgt[q_]], [t_sgt[q_]], scale=-1.0)
                    for q_, oc in enumerate(ocs):
                        tt("dve", post[s][:, oc, :], acc[q_][:], sgt[q_][:], ALU.mult, [t_acc[q_], t_sgt[q_]], [t_post[s][oc]])
                    yield H("pe", t_hT[s])
                for oc in range(4):
                    k = nb()
                    c0 = 1544 + oc * 128
                    for kc in range(KC):
                        mm(bank(k)[:, 0:TB], Win[:, kc, c0:c0 + 128], hT[s][:, kc, :], kc == 0, kc == KC - 1,
                           [t_Wq, t_hT[s]], [tb[k]], kc == KC - 1)
                    cp("act", qT[s][:, oc, :], bank(k)[:, 0:TB], [tb[k]], [t_qT[s][oc]])
                    if oc % 2:
                        yield H("pe", t_hT[s])
                for g in range(2):
                    k = nb()
                    for kc in range(KC):
                        mm(bank(k)[:, 0:TB], Wk2[:, kc, g * 128:(g + 1) * 128], hT[s][:, kc, :], kc == 0, kc == KC - 1,
                           [t_Wk2, t_hT[s]], [tb[k]], kc == KC - 1)
                    if b > 0:
                        cp("dve", kT[s][:, g, 0:128], kT[o][:, g, TB:TB + 128], [t_kT[o][g]], [t_kT[s][g]])
                    cp("act", kT[s][:, g, 128:128 + TB], bank(k)[:, 0:TB], [tb[k]], [t_kT[s][g]])
                if b > 0:
                    cp("dve", vtok[s][:, 0, :], vtok[o][:, 2, :], [t_vtok[o]], [t_vtok[s]])
                yield H("pe", t_hT[s])
                for i in range(2):
                    k = nb()
                    for kc in range(KC):
                        mm(bank(k)[:, 0:512], hT[s][:, kc, i * 128:(i + 1) * 128], Win[:, kc, 0:512], kc == 0, kc == KC - 1,
                           [t_Wz, t_hT[s]], [tb[k]], kc == KC - 1)
                    act(sz[s][:, i, :], bank(k)[:, 0:512], AF.Exp, [tb[k]], [t_sz[s][i]], scale=-1.0)
                    act(sz[s][:, i, :], sz[s][:, i, :], AF.Ln, [t_sz[s][i]], [t_sz[s][i]], bias=1.0)
                    act(sz[s][:, i, :], sz[s][:, i, :], AF.Exp, [t_sz[s][i]], [t_sz[s][i]], scale=-1.0)
                    tt("dve", sz[s][:, i, :], sz[s][:, i, :], bank(k)[:, 0:512], ALU.mult, [t_sz[s][i], tb[k]], [t_sz[s][i]])
                    yield H("pe", t_hT[s])
                    k = nb()
                    for kc in range(KC):
                        mm(bank(k)[:, 0:8], hT[s][:, kc, i * 128:(i + 1) * 128], Win[:, kc, 1536:1544], kc == 0, kc == KC - 1,
                           [t_Wdt, t_hT[s]], [tb[k]], False)
                    for kc in range(KC):
                        mm(bank(k)[:, 128:256], hT[s][:, kc, i * 128:(i + 1) * 128], Win[:, kc, 2184:2312], kc == 0, kc == KC - 1,
                           [t_Wv, t_hT[s]], [tb[k]], kc == KC - 1)
                    tt("dve", dtraw[s][:, i, :], bank(k)[:, 0:8], dtb, ALU.add, [tb[k], t_small], [t_dtraw[s][i]])
                    cp("dve", vtok[s][:, 1 + i, :], bank(k)[:, 128:256], [tb[k]], [t_vtok[s]])
                    yield H("pe", t_hT[s])

            def gen_S(b, i, T):
                par = b % 2
                post_ = post[par]
                tp_ = t_post[par]
                gc = 2 * b + i
                tsl = slice(i * 128, (i + 1) * 128)
                s1, t_s1 = T.nsm1()
                s2, t_s2 = T.nsm2()
                MT = T.MT; t_MT = T.t_MT
                ts("dve", s1[:, 0:8], dtraw[par][:, i, :], 60.0, None, ALU.min, None, [t_dtraw[par][i]], [t_s1])
                act(s1[:, 8:16], s1[:, 0:8], AF.Exp, [t_s1], [t_s1])
                act(s1[:, 16:24], s1[:, 8:16], AF.Ln, [t_s1], [t_s1], bias=1.0)
                tt("dve", s1[:, 24:32], s1[:, 16:24], a_b[:], ALU.mult, [t_s1, t_ab], [t_s1])
                dt_ = s1[:, 16:24]
                dA = s1[:, 24:32]
                kcb = nb()
                for g in range(2):
                    mm(bank(kcb)[:, g * 128:(g + 1) * 128], post_[:, 4 + g, tsl], post_[:, 6 + g, tsl], True, True,
                       [tp_[4 + g], tp_[6 + g]], [tb[kcb]], g == 1)
                cp("act", T.cbs[:], bank(kcb)[:, 0:256], [tb[kcb]], [T.t_cbs])
                yield H("pe", t_s1)
                ka = nb2()
                for hf in range(2):
                    for hh in range(4):
                        h = hf * 4 + hh
                        mm(bank(ka + hf)[:, hh * 128:(hh + 1) * 128], s1[:, 24 + h:25 + h].to_broadcast([128, 128]), tri[:],
                           hh == 0, False, [t_tri, t_s1], [tb[ka + hf]], False)
                    mm(bank(ka + hf), identb[:], negmask[:], False, True,
                       [t_identb, t_negmask], [tb[ka + hf]], True)
                Ab = ps[:, ka * 512:ka * 512 + 1024].rearrange("p (h t) -> p h t", t=128)
                Alast = Ab[:, :, 127:128].rearrange("p h o -> p (h o)")
                t_A = [tb[ka], tb[ka + 1]]
                kcu = nb()
                mm(bank(kcu)[:, 0:8], tri[:], dA, True, True, [t_tri, t_s1], [tb[kcu]], True)
                cp("dve", s1[:, 32:40], bank(kcu)[:, 0:8], [tb[kcu]], [t_s1])
                ts("dve", s1[:, 40:48], bank(kcu)[:, 0:8], -1.0, None, ALU.mult, None, [tb[kcu]], [t_s1])
                act(s1[:, 48:56], bank(kcu)[:, 0:8], AF.Exp, [tb[kcu]], [t_s1])
                tt("dve", s1[:, 56:64], Alast, s1[:, 32:40], ALU.subtract, t_A + [t_s1], [t_s1])
                act(s2[:, 0:8], s1[:, 56:64], AF.Exp, [t_s1], [t_s2])
                act(s2[:, 8:16], Alast, AF.Exp, t_A, [t_s2])
                tt("dve", s2[:, 16:24], dt_, s2[:, 0:8], ALU.mult, [t_s1, t_s2], [t_s2])
                for h in range(8):
                    tbh = tb[ka + h // 4]
                    act(Ab[:, h, :], Ab[:, h, :], AF.Exp, [tbh, t_s1], [tbh], bias=s1[:, 40 + h:41 + h])
                for g in range(2):
                    tt("dve", MT[:, g * 512:(g + 1) * 512].rearrange("p (r t) -> p r t", r=4),
                       Ab[:, 4 * g:4 * g + 4, :],
                       T.cbs[:, g * 128:(g + 1) * 128].unsqueeze(1).to_broadcast([128, 4, 128]),
                       ALU.mult, [tb[ka + g], T.t_cbs], [t_MT])
                yield H("pe", t_s1, t_s2)
                kx = nb()
                for cc in range(4):
                    tr(bankb(kx)[:, cc * 128:(cc + 1) * 128], post_[:, cc, tsl], identb[:],
                       [tp_[cc], t_identb], [tb[kx]], False)
                for g in range(2):
                    tr(bankb(kx)[:, 512 + g * 128:512 + (g + 1) * 128], post_[:, 4 + g, tsl], identb[:],
                       [tp_[4 + g], t_identb], [tb[kx]], g == 1)
                xs_ps = bankb(kx)[:, 0:512].rearrange("p (h d) -> p h d", h=8)
                tt("dve", T.xdt[:].rearrange("p (h d) -> p h d", h=8), xs_ps,
                   dt_.unsqueeze(2).to_broadcast([128, 8, 64]), ALU.mult, [tb[kx], t_s1], [T.t_xdt])
                tt("dve", T.xdte[:].rearrange("p (h d) -> p h d", h=8), xs_ps,
                   s2[:, 16:24].unsqueeze(2).to_broadcast([128, 8, 64]), ALU.mult, [tb[kx], t_s2], [T.t_xdte])
                tt("dve", T.xsD[:].rearrange("p (h d) -> p h d", h=8), xs_ps,
                   dskip.unsqueeze(2).to_broadcast([128, 8, 64]), ALU.mult, [tb[kx], t_small], [T.t_xsD])
                cp("act", T.Btok[:], bankb(kx)[:, 512:768], [tb[kx]], [T.t_Btok])
                if gc > 0:
                    yield ("wait", lambda: ("Sst", gc - 1) in done)
                yield H("pe", t_MT, T.t_xdt, T.t_xdte, T.t_xsD, T.t_Btok, t_Sb)
                kcs = nb()
                for g in range(2):
                    mm(bank(kcs)[:, g * 256:(g + 1) * 256], T.Btok[:, g * 128:(g + 1) * 128],
                       T.xdte[:, g * 256:(g + 1) * 256], True, True, [T.t_Btok, T.t_xdte], [tb[kcs]], g == 1)
                ko = nb()
                for g in range(2):
                    mm(bank(ko)[:, g * 256:(g + 1) * 256], post_[:, 6 + g, tsl], S_b[:, g * 256:(g + 1) * 256],
                       True, True, [tp_[6 + g], t_Sb], [tb[ko]], g == 1)
                kd = nb()
                mm(bank(kd)[:, 0:512], identb[:], T.xsD[:], True, False, [t_identb, T.t_xsD], [tb[kd]], False)
                for h in range(8):
                    mm(bank(kd)[:, h * 64:(h + 1) * 64], MT[:, h * 128:(h + 1) * 128], T.xdt[:, h * 64:(h + 1) * 64],
                       False, h == 7, [t_MT, T.t_xdt], [tb[kd]], h == 7)
                ybuf = T.ybuf; t_ybuf = T.t_ybuf
                tt("dve", ybuf[:].rearrange("p (h d) -> p h d", h=8),
                   bank(ko)[:, 0:512].rearrange("p (h d) -> p h d", h=8),
                   s1[:, 48:56].unsqueeze(2).to_broadcast([128, 8, 64]), ALU.mult, [tb[ko], t_s1], [t_ybuf])
                tt("dve", ybuf[:], ybuf[:], bank(kd)[:, 0:512], ALU.add, [t_ybuf, tb[kd]], [t_ybuf])
                tt("dve", S_f[:].rearrange("p (h d) -> p h d", h=8), S_f[:].rearrange("p (h d) -> p h d", h=8),
                   s2[:, 8:16].unsqueeze(2).to_broadcast([128, 8, 64]), ALU.mult, [t_Sf, t_s2], [t_Sf])
                tt("dve", S_f[:], S_f[:], bank(kcs)[:, 0:512], ALU.add, [t_Sf, tb[kcs]], [t_Sf])
                cp("act", S_b[:], S_f[:], [t_Sf], [t_Sb])
                done.add(("Sst", gc))
                yield H("pool", t_ybuf)
                ysn = T.ysn; t_ysn = T.t_ysn
                tt("pool", ybuf[:], ybuf[:], sz[par][:, i, :], ALU.mult, [t_ybuf, t_sz[par][i]], [t_ybuf])
                for g in range(2):
                    act(ysn[:, g * 256:(g + 1) * 256], ybuf[:, g * 256:(g + 1) * 256], AF.Square,
                        [t_ybuf], [t_ysn, t_s2], accum=s2[:, 24 + g:25 + g])
                ts("dve", s2[:, 26:28], s2[:, 24:26], 1.0 / 256, EPS, ALU.mult, ALU.add, [t_s2], [t_s2])
                tt("pool", s2[:, 30:32], s2[:, 26:28], mhalf[:, 0:2], ALU.pow, [t_s2, t_mhalf], [t_s2])
                if b > 0:
                    yield ("wait", lambda: ("O", b - 1) in done)
                yield H("dve", t_s2)
                for g in range(2):
                    stt(ysn[:, g * 256:(g + 1) * 256], ybuf[:, g * 256:(g + 1) * 256], s2[:, 30 + g:31 + g],
                        ssmw[:, g * 256:(g + 1) * 256], ALU.mult, ALU.mult, [t_ybuf, t_s2, t_ssmw], [t_ysn])
                ky = nb()
                for cc in range(4):
                    tr(bankb(ky)[:, cc * 128:(cc + 1) * 128], ysn[:, cc * 128:(cc + 1) * 128], identb[:],
                       [t_ysn, t_identb], [tb[ky]], cc == 3)
                cp("act", ycatT[:, 0:4, tsl], bankb(ky)[:, 0:512].rearrange("p (c t) -> p c t", c=4),
                   [tb[ky]], [t_ycatS[i]])
                yield H("dve", t_dtraw[par][i])

            def gen_A(b, j, T):
                par = b % 2
                qT_ = qT[par]; kT_ = kT[par]; v_ = vtok[par]
                Pexp = T.Pexp; t_Pexp = T.t_Pexp; PT = T.PT; t_PT = T.t_PT; yattn = T.yattn; t_yattn = T.t_yattn
                ga = 2 * b + j
                qsl = slice(j * 128, (j + 1) * 128)
                ksl = slice(j * 128, j * 128 + 256)
                msk = maskb[:, 256:512] if ga == 0 else maskb[:, 0:256]
                for g in range(2):
                    s3, t_s3 = T.nsm()
                    ka = nb2()
                    sc = ps[:, ka * 512:ka * 512 + 1024]
                    t_sc = [tb[ka], tb[ka + 1]]
                    for r in range(4):
                        hf = r % 2
                        psl = slice(hf * 64, (hf + 1) * 64)
                        kk = ka + r // 2
                        mm(sc[:, r * 256:(r + 1) * 256], qT_[psl, 2 * g + r // 2, qsl], kT_[psl, g, ksl], True, False,
                           [t_qT[par][2 * g + r // 2], t_kT[par][g]], [tb[kk]], False)
                        mm(sc[:, r * 256:(r + 1) * 256], identb[:], msk, False, True,
                           [t_identb, t_maskb], [tb[kk]], True)
                    op("dve", lambda: nc.vector.tensor_reduce(
                        out=s3[:, 0:4], in_=sc.rearrange("p (r k) -> p r k", r=4), axis=AX.X, op=ALU.max),
                        reads=t_sc, writes=[t_s3], cost=1.2)
                    stt(s3[:, 4:8], s3[:, 0:4], 0.125, sinks[:, g * 4:(g + 1) * 4], ALU.mult, ALU.max,
                        [t_s3, t_small], [t_s3])
                    ts("dve", s3[:, 8:12], s3[:, 4:8], -1.0, None, ALU.mult, None, [t_s3], [t_s3])
                    for r in range(4):
                        act(Pexp[:, r * 256:(r + 1) * 256], sc[:, r * 256:(r + 1) * 256], AF.Exp,
                            [tb[ka + r // 2], t_s3], [t_Pexp, t_s3], bias=s3[:, 8 + r:9 + r], scale=0.125,
                            accum=s3[:, 12 + r:13 + r])
                    yield H("dve", t_s3, t_Pexp)
                    tt("dve", s3[:, 16:20], sinks[:, g * 4:(g + 1) * 4], s3[:, 4:8], ALU.subtract,
                       [t_small, t_s3], [t_s3])
                    act(s3[:, 20:24], s3[:, 16:20], AF.Exp, [t_s3], [t_s3])
                    tt("dve", s3[:, 24:28], s3[:, 12:16], s3[:, 20:24], ALU.add, [t_s3], [t_s3])
                    op("dve", lambda: nc.vector.reciprocal(out=s3[:, 28:32], in_=s3[:, 24:28]),
                       reads=[t_s3], writes=[t_s3])
                    kt = nb()
                    for r in range(4):
                        for kk in range(2):
                            idx = r * 2 + kk
                            tr(bankb(kt)[:, idx * 128:(idx + 1) * 128],
                               Pexp[:, r * 256 + kk * 128:r * 256 + (kk + 1) * 128], identb[:],
                               [t_Pexp, t_identb], [tb[kt]], idx == 7)
                    cp("act", PT[:], bankb(kt)[:, 0:1024], [tb[kt]], [t_PT])
                    yield H("pe", t_PT, t_s3)
                    kv = nb()
                    for r in range(4):
                        o_ = bank(kv)[:, r * 64:(r + 1) * 64]
                        mm(o_, PT[:, (r * 2) * 128:(r * 2 + 1) * 128], v_[:, j, g * 64:(g + 1) * 64], True, False,
                           [t_PT, t_vtok[par]], [tb[kv]], False)
                        mm(o_, PT[:, (r * 2 + 1) * 128:(r * 2 + 2) * 128], v_[:, j + 1, g * 64:(g + 1) * 64], False, True,
                           [t_PT, t_vtok[par]], [tb[kv]], r == 3)
                    tt("dve", yattn[:, g * 256:(g + 1) * 256].rearrange("p (r d) -> p r d", r=4),
                       bank(kv)[:, 0:256].rearrange("p (r d) -> p r d", r=4),
                       s3[:, 28:32].unsqueeze(2).to_broadcast([128, 4, 64]), ALU.mult,
                       [tb[kv], t_s3], [t_yattn])
                    yield H("pe", t_yattn)
                if b > 0:
                    yield ("wait", lambda: ("O", b - 1) in done)
                ky = nb()
                for cc in range(4):
                    tr(bankb(ky)[:, cc * 128:(cc + 1) * 128], yattn[:, cc * 128:(cc + 1) * 128], identb[:],
                       [t_yattn, t_identb], [tb[ky]], cc == 3)
                cp("act", ycatT[:, 4:8, qsl], bankb(ky)[:, 0:512].rearrange("p (c t) -> p c t", c=4),
                   [tb[ky]], [t_ycatA[j]])
                yield H("pe", t_yattn)

            def emit_O(b):
                for i in range(2):
                    gt = 2 * b + i
                    for hf in range(2):
                        k = nb()
                        for kc in range(KC):
                            mm(bank(k)[:, 0:512], ycatT[:, kc, i * 128:(i + 1) * 128], Wout[:, kc, hf * 512:(hf + 1) * 512],
                               kc == 0, kc == KC - 1, [t_ycatS[i] if kc < 4 else t_ycatA[i], t_Wout], [tb[k]], kc == KC - 1)
                        tt("dve", X[:, gt, hf * 512:(hf + 1) * 512], X[:, gt, hf * 512:(hf + 1) * 512], bank(k)[:, 0:512],
                           ALU.add, [tX[gt], tb[k]], [tX[gt]])

            def chain(*gs):
                for g_ in gs:
                    yield from g_

            def all_done(b):
                return all(k_ in done for k_ in (("Sc", b, 0), ("Sc", b, 1), ("Ac", b, 0), ("Ac", b, 1)))

            def stream_NP():
                for b in range(nblk):
                    yield from gen_N(b)
                    if b >= 2:
                        yield ("wait", lambda b=b: all_done(b - 2))
                    yield from gen_P(b)
                    done.add(("P", b))

            def stream_S(i):
                for b in range(nblk):
                    yield ("wait", lambda b=b: ("P", b) in done)
                    yield from gen_S(b, i, ST[i])
                    done.add(("Sc", b, i))

            def stream_A(j):
                for b in range(nblk):
                    yield ("wait", lambda b=b: ("P", b) in done)
                    yield from gen_A(b, j, AT[j])
                    done.add(("Ac", b, j))

            def stream_O():
                for b in range(nblk):
                    yield ("wait", lambda b=b: all_done(b))
                    emit_O(b)
                    done.add(("O", b))
                    yield H("pe", t_ycatS[0])

            _mk = {"N": [stream_NP], "S": [lambda: stream_S(0), lambda: stream_S(1)], "A": [lambda: stream_A(0), lambda: stream_A(1)], "O": [stream_O]}
            _lst = []
            for ch in V_ORDER:
                _lst.append(_mk[ch].pop(0)())
            run_streams(_lst)
            if do_ffn:
                c.dma("sp", wb[:], nw_ffn_d, writes=[t_wb])
            build_program.tmix = max(c.tfree.values())
            fence = c.fence_tokens()
        if do_ffn:
            with ExitStack() as fs:
                H2T = c.sb("H2T", [128, KC, L], BF16, fs)
                t_H2T = [Trk("H2T%d" % i, fence) for i in range(4)]
                actT = c.sb("actT", [128, JMAX, L], BF16, fs)
                t_actT = [[Trk("actT%d_%d" % (i, q), fence) for q in range(4)] for i in range(JMAX)]
                Wd = [c.sb("Wd%d" % i, [128, JMAX, D], BF16, fs) for i in range(2)]
                t_Wd = [Trk("Wd%d" % i, fence) for i in range(2)]
                xn2 = c.sb("xn2", [128, D], BF16, fs); t_xn2 = Trk("xn2", fence)
                sg = [c.sb("sg%d" % i, [128, 512], F32, fs) for i in range(2)]
                t_sg = [Trk("sg%d" % i, fence) for i in range(2)]
                ostg = [c.sb("ostg%d" % i, [128, D], F32, fs) for i in range(2)]
                t_ostg = [Trk("ostg%d" % i, fence) for i in range(2)]
                for i in range(1, NGU):
                    gu.append(c.sb("gu%d" % i, [128, 2, KC, 128], BF16, fs))
                    t_gu[i] = Trk("gu%d" % i, fence)

                for j in range(1, NGU):
                    gu_load(j)

                def wd_load(p):
                    j0, J = PASSES[p]
                    c.dma("pool", Wd[p % 2][:, 0:J, :], w_dn_d[:, j0:j0 + J, :], writes=[t_Wd[p % 2]])

                xn2b = c.sb("xn2b", [128, D], BF16, fs); t_xn2b = Trk("xn2b", fence)
                xn2s = [xn2, xn2b]
                t_xn2s = [t_xn2, t_xn2b]

                def norm2_A(gt):
                    xq = xn2s[gt % 2]; t_xq = t_xn2s[gt % 2]
                    s_, t_s = nsm_F()
                    act(xq[:], X[:, gt, :], AF.Square, [tX[gt]], [t_xq, t_s], accum=s_[:, 0:1])
                    ts("dve", s_[:, 1:2], s_[:, 0:1], 1.0 / D, EPS, ALU.mult, ALU.add, [t_s], [t_s])
                    tt("pool", s_[:, 3:4], s_[:, 1:2], mhalf[:, 0:1], ALU.pow, [t_s, t_mhalf], [t_s])
                    stt(xq[:], X[:, gt, :], s_[:, 3:4], wb[:], ALU.mult, ALU.mult, [tX[gt], t_s, t_wb], [t_xq])

                def norm2_B(gt):
                    xq = xn2s[gt % 2]; t_xq = t_xn2s[gt % 2]
                    k = nb()
                    for kc in range(KC):
                        tr(bankb(k)[:, kc * 128:(kc + 1) * 128], xq[:, kc * 128:(kc + 1) * 128], identb[:],
                           [t_xq, t_identb], [tb[k]], kc == KC - 1)
                    cp("act", H2T[:, :, gt * 128:(gt + 1) * 128],
                       bankb(k)[:, 0:1024].rearrange("p (k t) -> p k t", k=KC), [tb[k]], [t_H2T[gt // 4]])

                sgc = [0]

                def ffn_a(j, jj, tbk):
                    slot = j % NGU
                    tsl = slice(tbk * 512, (tbk + 1) * 512)
                    kg = nb()
                    for kc in range(KC):
                        mm(bank(kg), gu[slot][:, 0, kc, :], H2T[:, kc, tsl], kc == 0, kc == KC - 1,
                           [t_gu[slot], t_H2T[tbk]], [tb[kg]], kc == KC - 1)
                    ku = nb()
                    for kc in range(KC):
                        mm(bank(ku), gu[slot][:, 1, kc, :], H2T[:, kc, tsl], kc == 0, kc == KC - 1,
                           [t_gu[slot], t_H2T[tbk]], [tb[ku]], kc == KC - 1)
                    si = sgc[0] % 2
                    sgc[0] += 1
                    act(sg[si][:], bank(kg), AF.Silu, [tb[kg]], [t_sg[si]])
                    tt("dve", actT[:, jj, tsl], sg[si][:], bank(ku), ALU.mult, [t_sg[si], tb[ku]], [t_actT[jj][tbk]])

                def final_tile(gt):
                    s_, t_s = nsm_F()
                    o_ = ostg[gt % 2]; t_o = t_ostg[gt % 2]
                    act(o_[:], X[:, gt, :], AF.Square, [tX[gt]], [t_o, t_s], accum=s_[:, 0:1])
                    ts("dve", s_[:, 1:2], s_[:, 0:1], 1.0 / D, EPS, ALU.mult, ALU.add, [t_s], [t_s])
                    tt("pool", s_[:, 3:4], s_[:, 1:2], mhalf[:, 0:1], ALU.pow, [t_s, t_mhalf], [t_s])
                    stt(o_[:], X[:, gt, :], s_[:, 3:4], wb[:], ALU.mult, ALU.mult, [tX[gt], t_s, t_wb], [t_o])
                    c.dma("sp", out_d[gt * 128:(gt + 1) * 128, :], o_[:], reads=[t_o])

                wd_load(0)

                def stream_norm2():
                    norm2_A(0)
                    yield H("act", tX[1])
                    for gt in range(NT):
                        if gt + 1 < NT:
                            norm2_A(gt + 1)
                            yield H("pe", t_xn2s[gt % 2])
                        norm2_B(gt)
                        done.add(("H2T", gt))
                        yield H("act", tX[min(gt + 2, NT - 1)])

                def stream_ffn0():
                    j0, J = PASSES[0]
                    for tbk in range(4):
                        yield ("wait", lambda tbk=tbk: ("H2T", 4 * tbk + 3) in done)
                        for jj in range(J):
                            ffn_a(j0 + jj, jj, tbk)
                            if tbk == 3 and j0 + jj + NGU < NJ:
                                gu_load(j0 + jj + NGU)
                            yield H("pe", t_H2T[tbk])

                for p, (j0, J) in enumerate(PASSES):
                    if p + 1 < len(PASSES):
                        wd_load(p + 1)
                    last = p == len(PASSES) - 1

                    def ffn_b_tile(gt, p=p, J=J):
                        for hf in range(2):
                            k = nb()
                            for jj in range(J):
                                mm(bank(k), actT[:, jj, gt * 128:(gt + 1) * 128], Wd[p % 2][:, jj, hf * 512:(hf + 1) * 512],
                                   jj == 0, jj == J - 1, [t_actT[jj][gt // 4], t_Wd[p % 2]], [tb[k]], jj == J - 1)
                            tt("dve", X[:, gt, hf * 512:(hf + 1) * 512], X[:, gt, hf * 512:(hf + 1) * 512], bank(k),
                               ALU.add, [tX[gt], tb[k]], [tX[gt]])

                    if p == 0:
                        run_streams([stream_norm2(), stream_ffn0()])
                        c.dma("sp", wb[:], nw_fin_d, writes=[t_wb])
                        for gt in range(NT):
                            ffn_b_tile(gt)
                    elif not last:
                        for jj in range(J):
                            for tbk in range(4):
                                ffn_a(j0 + jj, jj, tbk)
                            if j0 + jj + NGU < NJ:
                                gu_load(j0 + jj + NGU)
                        for gt in range(NT):
                            ffn_b_tile(gt)
                    else:
                        for tbk in range(4):
                            for jj in range(J):
                                ffn_a(j0 + jj, jj, tbk)
                            for gt in range(4 * tbk, 4 * tbk + 4):
                                ffn_b_tile(gt)
                                final_tile(gt)
                c.wait_all("sp", t_ostg)
        else:
            for gt in range(NT):
                c.dma("sp", out_d[gt * 128:(gt + 1) * 128, :], X[:, gt, :], reads=[tX[gt]])
            c.wait_all("sp", tX)
        build_program.stats = (c.n_ins, c.n_wait, c.n_dsem, dict(c.cnt))
        build_program.tend = max(c.tfree.values())
        build_program.log = c.log
    return nc


def _consts():
    identf = np.eye(128, dtype=np.float32)
    u = np.arange(128)
    tri = (u[:, None] <= u[None, :]).astype(np.float32)
    nm = np.where(u[:, None] <= u[None, :], 0.0, NEG).astype(np.float32)
    negmask = np.tile(nm, (1, 4))
    q = u[:, None]
    kk = u[None, :]
    prev = np.where(kk > q, 0.0, NEG)
    cur = np.where(kk <= q, 0.0, NEG)
    mask = np.concatenate([prev, cur], axis=1)
    mask0 = np.concatenate([np.full((128, 128), NEG), cur], axis=1)
    maskb = np.concatenate([mask, mask0], axis=1).astype(np.float32)
    return identf, tri, negmask, maskb


def _prep_shared(inp):
    f = np.float32
    w_in = np.asarray(inp["w_in"], f)[0]
    w_in_l = np.ascontiguousarray(w_in.reshape(KC, 128, D_IN).transpose(1, 0, 2))
    kcols = w_in[:, 2056:2184]
    k2 = np.concatenate([kcols[:, 0:64], kcols[:, 0:64], kcols[:, 64:128], kcols[:, 64:128]], axis=1)
    w_k2_l = np.ascontiguousarray(k2.reshape(KC, 128, 256).transpose(1, 0, 2))
    w_out = np.asarray(inp["w_out"], f)[0]
    w_out_l = np.ascontiguousarray(w_out.reshape(KC, 128, D).transpose(1, 0, 2))
    wg = np.asarray(inp["w_gate"], f)[0].reshape(KC, 128, NJ, 128)
    wu = np.asarray(inp["w_up"], f)[0].reshape(KC, 128, NJ, 128)
    w_gu = np.ascontiguousarray(np.stack([wg, wu], axis=0).transpose(3, 2, 0, 1, 4))
    w_dn = np.asarray(inp["w_down"], f)[0]
    w_dn_l = np.ascontiguousarray(w_dn.reshape(NJ, 128, D).transpose(1, 0, 2))

    def bc(v, n):
        return np.ascontiguousarray(np.broadcast_to(np.asarray(v, f).reshape(1, n), (128, n)))

    convw = np.ascontiguousarray(np.asarray(inp["conv_w"], f)[0].T.reshape(8, 128, 4).transpose(1, 0, 2))
    convb = np.ascontiguousarray(np.asarray(inp["conv_b"], f)[0].reshape(8, 128).T)
    small = np.zeros((128, 48), f)
    small[:, 0:8] = convb
    small[:, 8:16] = bc(inp["dt_bias"][0], 8)
    small[:, 16:24] = bc(inp["a_log"][0], 8)
    small[:, 24:32] = bc(inp["d_skip"][0], 8)
    small[:, 32:40] = bc(inp["attn_sinks"][0], 8)
    identf, tri, negmask, maskb = _consts()
    return {
        "w_in": w_in_l, "w_k2": w_k2_l, "w_out": w_out_l, "w_gu": w_gu, "w_dn": w_dn_l,
        "nw_mix": bc(inp["norm_mix_w"][0], D), "nw_ffn": bc(inp["norm_ffn_w"][0], D),
        "nw_fin": bc(inp["norm_final_w"], D), "convw": convw, "smallp": small,
        "ssmw": bc(inp["ssm_norm_w"][0], 512), "identf": identf, "tri": tri,
        "negmask": negmask, "maskb": maskb,
    }


_NC_CACHE = {}


def kernel(**inputs):
    x = np.asarray(inputs["x"], np.float32)
    shared = _prep_shared(inputs)
    if "nc" not in _NC_CACHE:
        _NC_CACHE["nc"] = build_program()
    nc = _NC_CACHE["nc"]
    in_maps = []
    for b in range(8):
        m = dict(shared)
        m["x"] = np.ascontiguousarray(x[b])
        in_maps.append(m)
    res = run_bass_kernel_spmd(nc, in_maps, core_ids=list(range(8)))
    out = np.stack([np.asarray(r["out"], np.float32) for r in res.results], axis=0)
    return out
```
